# Optimizing a Trainium2 kernel written in Bass

```python
import math
import jax, jax.numpy as jnp
from jax import lax
import numpy as np

D_MODEL = 1024
BATCH = 8
SEQ = 2048
DEPTH = 2

N_MIXERS = 2
N_SSD_LAYERS = (DEPTH + 1) // 2
N_CONV_LAYERS = DEPTH // 2
NORM_EPS = 1e-5
ADA_MODS = 6

SSD_EXPAND = 2
SSD_D_INNER = SSD_EXPAND * D_MODEL
SSD_HEAD_DIM = 64
SSD_N_HEADS = SSD_D_INNER // SSD_HEAD_DIM
SSD_N_GROUPS = 4
SSD_HEADS_PER_GROUP = SSD_N_HEADS // SSD_N_GROUPS
SSD_D_STATE = 128
SSD_CONV_K = 4
SSD_CHUNK = 128
SSD_CONV_DIM = SSD_D_INNER + 2 * SSD_N_GROUPS * SSD_D_STATE
SSD_IN_DIM = SSD_D_INNER + SSD_CONV_DIM + SSD_N_HEADS
DT_MIN = 1e-3
DT_MAX = 1e-1

SC_WIDTH = D_MODEL
SC_CONV_K = 3

D_FF = 4 * D_MODEL

kernel_name = "hybrid_ssd_shortconv_adaln_trunk"


def rmsnorm(x, g, eps=NORM_EPS):
    xf = x.astype(jnp.float32)
    y = xf * lax.rsqrt(jnp.mean(xf * xf, axis=-1, keepdims=True) + eps)
    return (y * g.astype(jnp.float32)).astype(x.dtype)


def causal_dwconv(x, w, b=None):
    k, ch = w.shape
    out = lax.conv_general_dilated(
        x, w[:, None, :].astype(x.dtype), window_strides=(1,), padding=[(k - 1, 0)],
        dimension_numbers=("NWC", "WIO", "NWC"), feature_group_count=ch)
    if b is not None:
        out = out + b.astype(x.dtype)
    return out


def ssd_chunked(xs, dt, A, Bm, Cm):
    b, L, g, r, p = xs.shape
    n = Bm.shape[-1]
    nc = L // SSD_CHUNK
    xs = xs.astype(jnp.float32).reshape(b, nc, SSD_CHUNK, g, r, p)
    dt = dt.reshape(b, nc, SSD_CHUNK, g, r)
    Bc = Bm.astype(jnp.float32).reshape(b, nc, SSD_CHUNK, g, n)
    Cc = Cm.astype(jnp.float32).reshape(b, nc, SSD_CHUNK, g, n)
    X = xs * dt[..., None]
    Acs = jnp.cumsum(dt * A, axis=2)

    causal = jnp.tril(jnp.ones((SSD_CHUNK, SSD_CHUNK), dtype=bool))[:, :, None, None]
    seg = Acs[:, :, :, None] - Acs[:, :, None, :]
    Lmat = jnp.exp(jnp.where(causal, seg, -jnp.inf))
    scores = jnp.einsum("bclgn,bcsgn->bclsg", Cc, Bc)
    M = scores[..., None] * Lmat
    y_diag = jnp.einsum("bclsgr,bcsgrp->bclgrp", M, X)

    decay_states = jnp.exp(Acs[:, :, -1:] - Acs)
    states = jnp.einsum("bcsgn,bcsgrp->bcgrpn", Bc, X * decay_states[..., None])

    chunk_decay = jnp.exp(Acs[:, :, -1])

    def step(carry, inp):
        st, dec = inp
        return carry * dec[..., None, None] + st, carry

    init = jnp.zeros((b, g, r, p, n), dtype=states.dtype)
    _, prev = lax.scan(step, init, (jnp.moveaxis(states, 1, 0), jnp.moveaxis(chunk_decay, 1, 0)))
    prev = jnp.moveaxis(prev, 0, 1)

    y_off = jnp.einsum("bclgn,bcgrpn->bclgrp", Cc, prev) * jnp.exp(Acs)[..., None]
    return (y_diag + y_off).reshape(b, L, g, r, p)


def ssd_mixer(h, in_w, conv_w, conv_b, dt_bias, A_log, D_skip, norm_w, out_w):
    b, L, _ = h.shape
    G, R, P, N = SSD_N_GROUPS, SSD_HEADS_PER_GROUP, SSD_HEAD_DIM, SSD_D_STATE
    zxbcdt = h @ in_w
    z = zxbcdt[..., :SSD_D_INNER]
    xBC = zxbcdt[..., SSD_D_INNER:SSD_D_INNER + SSD_CONV_DIM]
    dt_raw = zxbcdt[..., SSD_D_INNER + SSD_CONV_DIM:]
    xBC = jax.nn.silu(causal_dwconv(xBC, conv_w, conv_b))
    xs = xBC[..., :SSD_D_INNER].reshape(b, L, G, R, P)
    Bm = xBC[..., SSD_D_INNER:SSD_D_INNER + G * N].reshape(b, L, G, N)
    Cm = xBC[..., SSD_D_INNER + G * N:].reshape(b, L, G, N)
    dt = jax.nn.softplus(dt_raw.astype(jnp.float32) + dt_bias.astype(jnp.float32)).reshape(b, L, G, R)
    A = -jnp.exp(A_log.astype(jnp.float32)).reshape(G, R)
    y = ssd_chunked(xs, dt, A, Bm, Cm)
    y = y + D_skip.astype(jnp.float32).reshape(G, R, 1) * xs.astype(jnp.float32)
    yg = y.reshape(b, L, SSD_D_INNER) * jax.nn.silu(z.astype(jnp.float32))
    yg = yg.reshape(b, L, G, SSD_D_INNER // G)
    yg = yg * lax.rsqrt(jnp.mean(yg * yg, axis=-1, keepdims=True) + NORM_EPS)
    yg = yg.reshape(b, L, SSD_D_INNER) * norm_w.astype(jnp.float32)
    return yg.astype(h.dtype) @ out_w


def short_conv_mixer(h, in_w, conv_w, out_w):
    proj = h @ in_w
    Bg, Cg, xv = jnp.split(proj, 3, axis=-1)
    y = Bg * causal_dwconv(Cg * xv, conv_w)
    return y @ out_w


def sqrelu_mlp(h, up_w, down_w):
    a = jax.nn.relu(h @ up_w)
    return (a * a) @ down_w


def setup_inputs(seed: int = 0) -> dict:
    key = jax.random.key(seed)
    ks = jax.random.split(key, 24)
    f32 = jnp.float32
    D = D_MODEL
    nrm = lambda k, shape, s: jax.random.normal(k, shape, f32) * s
    x = jax.random.normal(ks[0], (BATCH, SEQ, D), f32)
    c = jax.random.normal(ks[1], (BATCH, D), f32)
    ada_w = nrm(ks[2], (DEPTH, D, ADA_MODS * D), 0.5 * D ** -0.5)
    ada_b = nrm(ks[3], (DEPTH, ADA_MODS * D), 0.02)
    mix_norm_w = 1.0 + nrm(ks[4], (DEPTH, D), 0.02)
    mlp_norm_w = 1.0 + nrm(ks[5], (DEPTH, D), 0.02)
    mlp_up = nrm(ks[6], (DEPTH, D, D_FF), D ** -0.5)
    mlp_down = nrm(ks[7], (DEPTH, D_FF, D), D_FF ** -0.5)
    ssd_in_w = nrm(ks[8], (N_SSD_LAYERS, D, SSD_IN_DIM), D ** -0.5)
    ssd_conv_w = nrm(ks[9], (N_SSD_LAYERS, SSD_CONV_K, SSD_CONV_DIM), SSD_CONV_K ** -0.5)
    ssd_conv_b = nrm(ks[10], (N_SSD_LAYERS, SSD_CONV_DIM), 0.02)
    u = jax.random.uniform(ks[11], (N_SSD_LAYERS, SSD_N_HEADS), f32)
    dt0 = jnp.exp(u * (math.log(DT_MAX) - math.log(DT_MIN)) + math.log(DT_MIN))
    ssd_dt_bias = dt0 + jnp.log(-jnp.expm1(-dt0))
    ssd_A_log = jnp.log(jax.random.uniform(ks[12], (N_SSD_LAYERS, SSD_N_HEADS), f32, 1.0, 16.0))
    ssd_D = 1.0 + nrm(ks[13], (N_SSD_LAYERS, SSD_N_HEADS), 0.02)
    ssd_norm_w = 1.0 + nrm(ks[14], (N_SSD_LAYERS, SSD_D_INNER), 0.02)
    ssd_out_w = nrm(ks[15], (N_SSD_LAYERS, SSD_D_INNER, D), SSD_D_INNER ** -0.5)
    sc_in_w = nrm(ks[16], (N_CONV_LAYERS, D, 3 * SC_WIDTH), D ** -0.5)
    sc_conv_w = nrm(ks[17], (N_CONV_LAYERS, SC_CONV_K, SC_WIDTH), SC_CONV_K ** -0.5)
    sc_out_w = nrm(ks[18], (N_CONV_LAYERS, SC_WIDTH, D), SC_WIDTH ** -0.5)
    final_norm_w = 1.0 + nrm(ks[19], (D,), 0.02)
    return {"x": x, "c": c, "ada_w": ada_w, "ada_b": ada_b,
            "mix_norm_w": mix_norm_w, "mlp_norm_w": mlp_norm_w,
            "mlp_up": mlp_up, "mlp_down": mlp_down,
            "ssd_in_w": ssd_in_w, "ssd_conv_w": ssd_conv_w, "ssd_conv_b": ssd_conv_b,
            "ssd_dt_bias": ssd_dt_bias, "ssd_A_log": ssd_A_log, "ssd_D": ssd_D,
            "ssd_norm_w": ssd_norm_w, "ssd_out_w": ssd_out_w,
            "sc_in_w": sc_in_w, "sc_conv_w": sc_conv_w, "sc_out_w": sc_out_w,
            "final_norm_w": final_norm_w}


def reference(x, c, ada_w, ada_b, mix_norm_w, mlp_norm_w, mlp_up, mlp_down,
              ssd_in_w, ssd_conv_w, ssd_conv_b, ssd_dt_bias, ssd_A_log, ssd_D,
              ssd_norm_w, ssd_out_w, sc_in_w, sc_conv_w, sc_out_w, final_norm_w):
    cond = jax.nn.silu(c.astype(x.dtype))
    for i in range(DEPTH):
        mod = cond @ ada_w[i] + ada_b[i]
        sh_m, sc_m, g_m, sh_f, sc_f, g_f = [m[:, None, :] for m in jnp.split(mod, ADA_MODS, axis=-1)]
        h = rmsnorm(x, mix_norm_w[i]) * (1.0 + sc_m) + sh_m
        j = i // N_MIXERS
        if i % N_MIXERS == 0:
            y = ssd_mixer(h, ssd_in_w[j], ssd_conv_w[j], ssd_conv_b[j], ssd_dt_bias[j],
                          ssd_A_log[j], ssd_D[j], ssd_norm_w[j], ssd_out_w[j])
        else:
            y = short_conv_mixer(h, sc_in_w[j], sc_conv_w[j], sc_out_w[j])
        x = x + g_m * y
        h = rmsnorm(x, mlp_norm_w[i]) * (1.0 + sc_f) + sh_f
        x = x + g_f * sqrelu_mlp(h, mlp_up[i], mlp_down[i])
    return rmsnorm(x, final_norm_w)
```

```python
import contextlib
import numpy as np
import concourse.bass as bass
import concourse.mybir as mybir
from concourse.bass_utils import run_bass_kernel_spmd

F32 = mybir.dt.float32
BF16 = mybir.dt.bfloat16
AF = mybir.ActivationFunctionType
ALU = mybir.AluOpType
AX = mybir.AxisListType

D = 1024
L = 2048
KC = 8
DFF = 4096
NH = 32
EPS = 1e-5
ENGS = ("pe", "act", "dve", "pool", "sp")

_SM = {}
_off = 0
for _n, _w in (("c", 8), ("adab", 96), ("mixg", 16), ("mlpg", 16), ("fing", 8), ("convw", 96),
               ("convb", 24), ("dfeat", 16), ("nw", 16), ("dtb", 32), ("alog", 32), ("scw", 24)):
    _SM[_n] = _off
    _off += _w
NS = _off


class _Op:
    __slots__ = ("eng", "fn", "idx", "deps", "dma_sem", "dma_cnt", "ms", "is_dma", "done")

    def __init__(self, eng, fn, idx):
        self.eng = eng
        self.fn = fn
        self.idx = idx
        self.deps = []
        self.dma_sem = None
        self.dma_cnt = 0
        self.ms = 0
        self.is_dma = False
        self.done = False


class Sched:
    def __init__(self, nc):
        self.nc = nc
        self.ops = {e: [] for e in ENGS}
        self.last_w = {}
        self.readers = {}
        self.dma_sems = {}
        self.SAME_ENG_DIST = 10 ** 9

    def _track(self, rec, reads, writes):
        deps = []
        for k in reads:
            w = self.last_w.get(k)
            if w is not None and w is not rec:
                deps.append(w)
        for k in writes:
            w = self.last_w.get(k)
            if w is not None and w is not rec:
                deps.append(w)
            for r in self.readers.get(k, ()):
                if r is not rec:
                    deps.append(r)
        for k in reads:
            self.readers.setdefault(k, []).append(rec)
        for k in writes:
            self.last_w[k] = rec
            self.readers[k] = []
        best = {}
        for d in deps:
            if d.is_dma:
                key = ("dma", d.dma_sem)
                if key not in best or best[key].dma_cnt < d.dma_cnt:
                    best[key] = d
            else:
                key = d.eng
                if key not in best or best[key].idx < d.idx:
                    best[key] = d
        rec.deps = list(best.values())

    def op(self, eng, fn, reads=(), writes=()):
        pw = [k for k in reads if k.startswith("pb")]
        if pw:
            reads = [k for k in reads if not k.startswith("pb")]
            writes = list(writes) + pw
        rec = _Op(eng, fn, len(self.ops[eng]))
        self._track(rec, reads, writes)
        self.ops[eng].append(rec)
        return rec

    def dma(self, eng, out, in_, sem, reads=(), writes=()):
        def fn(e, out=out, in_=in_):
            return e.dma_start(out=out, in_=in_)
        rec = _Op(eng, fn, len(self.ops[eng]))
        rec.is_dma = True
        ent = self.dma_sems.setdefault(sem, [None, 0])
        ent[1] += 16
        rec.dma_sem = sem
        rec.dma_cnt = ent[1]
        self._track(rec, reads, writes)
        self.ops[eng].append(rec)
        return rec

    def barrier(self):
        lasts = []
        for e in ENGS:
            real = [r for r in self.ops[e] if r.fn is not None]
            if real:
                lasts.append(real[-1])
        dl = {}
        for e in ENGS:
            for r in self.ops[e]:
                if r.is_dma:
                    dl[r.dma_sem] = r
        for e in ENGS:
            rec = _Op(e, None, len(self.ops[e]))
            rec.deps = [d for d in lasts if d.eng != e and not d.is_dma] + list(dl.values())
            self.ops[e].append(rec)

    def _needs_wait(self, rec, d):
        if d.is_dma:
            return True
        if d.eng != rec.eng:
            return True
        if rec.is_dma:
            return True
        if rec.eng == "pe":
            return False
        if rec.eng == "pool":
            return True
        return (rec.idx - d.idx) < self.SAME_ENG_DIST

    def check(self):
        ptr = {e: 0 for e in ENGS}
        for e in ENGS:
            for r in self.ops[e]:
                r.done = False
        progress = True
        while progress:
            progress = False
            for e in ENGS:
                ops = self.ops[e]
                while ptr[e] < len(ops):
                    r = ops[ptr[e]]
                    if all(d.done for d in r.deps):
                        r.done = True
                        ptr[e] += 1
                        progress = True
                    else:
                        break
        for e in ENGS:
            if ptr[e] < len(self.ops[e]):
                raise RuntimeError(f"schedule deadlock on {e} at op {ptr[e]}")

    def emit(self, final_waits=()):
        nc = self.nc
        self.check()
        for e in ENGS:
            for rec in self.ops[e]:
                for d in rec.deps:
                    if not d.is_dma and self._needs_wait(rec, d):
                        d.ms = -1
        for e in ENGS:
            n = 0
            for rec in self.ops[e]:
                if rec.ms == -1:
                    n += 1
                    rec.ms = n
        with contextlib.ExitStack() as st:
            esem = {e: st.enter_context(nc.semaphore("s_" + e)) for e in ENGS}
            for name in self.dma_sems:
                self.dma_sems[name][0] = st.enter_context(nc.semaphore("d_" + name))
            block = st.enter_context(nc.Block())
            bmap = {"pe": block.tensor, "act": block.scalar, "dve": block.vector,
                    "pool": block.gpsimd, "sp": block.sync}
            for e in ENGS:
                ops = self.ops[e]
                fin = [fw for fw in final_waits if fw[0] == e]
                if not ops and not fin:
                    continue

                def body(eng, e=e, ops=ops, fin=fin):
                    waited = {}
                    for rec in ops:
                        for d in rec.deps:
                            if not self._needs_wait(rec, d):
                                continue
                            if d.is_dma:
                                key, val, sem = ("d", d.dma_sem), d.dma_cnt, self.dma_sems[d.dma_sem][0]
                            else:
                                key, val, sem = ("e", d.eng), d.ms, esem[d.eng]
                            if waited.get(key, 0) >= val:
                                continue
                            waited[key] = val
                            eng.wait_ge(sem, val)
                        if rec.fn is None:
                            continue
                        ins = rec.fn(eng)
                        if rec.is_dma:
                            ins.then_inc(self.dma_sems[rec.dma_sem][0], 16)
                        elif rec.ms > 0:
                            ins.then_inc(esem[e], 1)
                    for (_, name) in fin:
                        eng.wait_ge(self.dma_sems[name][0], self.dma_sems[name][1])
                bmap[e](body)


class Rot:
    def __init__(self, items):
        self.items = items
        self.i = 0

    def next(self):
        it = self.items[self.i % len(self.items)]
        self.i += 1
        return it


def build_program(dbg_stop=None):
    nc = bass.Bass("TRN2", target_bir_lowering=False)
    xT_d = nc.dram_tensor("xT", [128, KC, L], F32, kind="ExternalInput").ap()
    sm_d = nc.dram_tensor("smalls", [128, NS], F32, kind="ExternalInput").ap()
    ada_w = nc.dram_tensor("ada_w", [2, D, 6 * D], F32, kind="ExternalInput").ap()
    mlp_up = nc.dram_tensor("mlp_up", [2, D, DFF], F32, kind="ExternalInput").ap()
    mlp_down = nc.dram_tensor("mlp_down", [2, DFF, D], F32, kind="ExternalInput").ap()
    ssd_in_w = nc.dram_tensor("ssd_in_w", [D, 5152], F32, kind="ExternalInput").ap()
    ssd_out_w = nc.dram_tensor("ssd_out_w", [2048, D], F32, kind="ExternalInput").ap()
    sc_in_w = nc.dram_tensor("sc_in_w", [D, 3 * D], F32, kind="ExternalInput").ap()
    sc_out_w = nc.dram_tensor("sc_out_w", [D, D], F32, kind="ExternalInput").ap()
    out_d = nc.dram_tensor("outT", [128, KC, L], F32, kind="ExternalOutput").ap()

    with contextlib.ExitStack() as st:
        def sb(name, shape, dt):
            return st.enter_context(nc.sbuf_tensor(name, shape, dt))

        def ps(name, shape, dt=F32):
            return st.enter_context(nc.psum_tensor(name, shape, dt))

        S = Sched(nc)
        xT = sb("xT_sb", [128, KC, L], F32)
        hT = sb("hT_sb", [128, KC, L], BF16)
        Wreg = sb("Wreg", [128, 4 * 4096], BF16)
        sm = sb("sm_sb", [128, NS], F32)
        modT = [sb(f"modT{i}", [128, 48], F32) for i in range(2)]
        aM = [sb(f"aM{i}", [128, 8], F32) for i in range(2)]
        aF = [sb(f"aF{i}", [128, 8], F32) for i in range(2)]
        cond = sb("cond", [128, 8], F32)
        ones_f = sb("ones_f", [128, 128], F32)
        ident_b = sb("ident_b", [128, 128], BF16)
        ident_f = sb("ident_f", [128, 128], F32)
        triu = sb("triu", [128, 128], F32)
        maskneg = sb("maskneg", [128, 128], F32)
        sel8 = sb("sel8", [40, 8, 128], BF16)
        NAF = 10112
        NAB = 16384
        arF = sb("arenaF", [128, NAF], F32)
        arB = sb("arenaB", [128, NAB], BF16)
        _o = 6016
        nsq = [arF[:, _o + i * 512:_o + (i + 1) * 512] for i in range(2)]
        nrstd = [arF[:, _o + 1024 + i * 512:_o + 1536 + i * 512] for i in range(4)]
        ntmp = [arF[:, _o + 3072 + i * 512:_o + 3584 + i * 512] for i in range(2)]
        _a = 3584
        ada_st0 = [arB[:, i * 4096:(i + 1) * 4096].rearrange("p (k n) -> p k n", n=512) for i in range(4)]
        ada_st1 = [arF[:, _a + i * 1024:_a + (i + 1) * 1024].bitcast(BF16).rearrange("p (k n) -> p k n", n=256)
                   for i in range(2)]

        class Carver:
            def __init__(self, t, n):
                self.t, self.n, self.o = t, n, 0

            def reset(self):
                self.o = 0

            def get(self, n):
                assert self.o + n <= self.n, (self.o, n, self.n)
                ap = self.t[:, self.o:self.o + n]
                self.o += n
                return ap

        cF = Carver(arF, NAF)
        cB = Carver(arB, NAB)
        pbank = [ps(f"pb{i}", [128, 512], F32) for i in range(7)]
        pbf = [ps(f"pbf{i}", [128, 1024], BF16) for i in range(1)]

        def xk(kc, t8):
            return f"x{kc}_{t8}"

        def xkeys512(kc, tt):
            return [xk(kc, 2 * tt), xk(kc, 2 * tt + 1)]

        def hk(kc, tt):
            return f"h{kc}_{tt}"

        sc = lambda off, j: sm[:, off + j: off + j + 1]

        S.dma("sp", sm[:], sm_d, "sm", writes=["sm"])
        for kc in range(KC):
            S.dma("sp", xT[:, kc, :], xT_d[:, kc, :], f"xin{kc}", writes=[xk(kc, t) for t in range(8)])
        S.op("dve", lambda e: e.memset(ones_f[:], 1.0), writes=["ones_f"])
        S.op("pool", lambda e: e.memset(ident_f[:], 0.0), writes=["ident_f"])
        S.op("pool", lambda e: e.affine_select(out=ident_f[:], in_=ident_f[:], pattern=[[-1, 128]],
                                               compare_op=ALU.not_equal, fill=1.0, base=0, channel_multiplier=1),
             reads=["ident_f"], writes=["ident_f"])
        S.op("dve", lambda e: e.tensor_copy(out=ident_b[:], in_=ident_f[:]), reads=["ident_f"], writes=["ident_b"])
        S.op("pool", lambda e: e.memset(triu[:], 1.0), writes=["triu"])
        S.op("pool", lambda e: e.affine_select(out=triu[:], in_=triu[:], pattern=[[1, 128]],
                                               compare_op=ALU.is_ge, fill=0.0, base=0, channel_multiplier=-1),
             reads=["triu"], writes=["triu"])
        S.op("pool", lambda e: e.memset(maskneg[:], 0.0), writes=["maskneg"])
        S.op("pool", lambda e: e.affine_select(out=maskneg[:], in_=maskneg[:], pattern=[[1, 128]],
                                               compare_op=ALU.is_ge, fill=-30000.0, base=0, channel_multiplier=-1),
             reads=["maskneg"], writes=["maskneg"])
        S.op("pool", lambda e: e.memset(sel8[:], 0.0), writes=["sel8"])
        for h in range(8):
            S.op("pool", lambda e, h=h: e.affine_select(out=sel8[:, h, :], in_=sel8[:, h, :], pattern=[[0, 128]],
                                                        compare_op=ALU.not_equal, fill=1.0, base=-h,
                                                        channel_multiplier=1),
                 reads=["sel8"], writes=["sel8"])
            S.op("pool", lambda e, h=h: e.affine_select(out=sel8[:, h, :], in_=sel8[:, h, :], pattern=[[0, 128]],
                                                        compare_op=ALU.not_equal, fill=1.0, base=-(32 + h),
                                                        channel_multiplier=1),
                 reads=["sel8"], writes=["sel8"])
        S.op("act", lambda e: e.activation(out=cond[:], in_=sm[:, _SM["c"]:_SM["c"] + 8], func=AF.Silu),
             reads=["sm"], writes=["cond"])
        S.op("dve", lambda e: e.tensor_scalar(out=sm[:, _SM["convw"]:_SM["convw"] + 120],
                                              in0=sm[:, _SM["convw"]:_SM["convw"] + 120], scalar1=0.5, scalar2=None,
                                              op0=ALU.mult), reads=["sm"], writes=["sm"])

        ada_state = {"n": 0}
        cond_b = sb("cond_b", [128, 8], BF16)
        S.op("dve", lambda e: e.tensor_copy(out=cond_b[:], in_=cond[:]), reads=["cond"], writes=["cond_b"])

        def ada_piece(i, j0, ncol, stages, pmod):
            n = ada_state["n"]
            ada_state["n"] += 1
            slot = n % len(stages)
            stg = stages[slot]
            S.dma("pool", stg[:, :, 0:ncol], ada_w[i][:, j0 * 128:j0 * 128 + ncol].rearrange("(kc p) n -> p kc n", p=128),
                  f"ada{slot}", writes=[f"adast{slot}"])
            for m in range(ncol // 128):
                j = j0 + m
                for kc in range(KC):
                    S.op("pe", lambda e, j=j, m=m, kc=kc: e.matmul(pmod[:, j:j + 1], lhsT=stg[:, kc, m * 128:(m + 1) * 128],
                                                                    rhs=cond_b[:, kc:kc + 1], start=(kc == 0), stop=(kc == 7)),
                         reads=[f"adast{slot}", "cond_b"], writes=["pb5"])

        def ada_finish(i, pmod):
            S.op("dve", lambda e: e.tensor_tensor(out=modT[i][:], in0=pmod[:, 0:48],
                                                  in1=sm[:, _SM["adab"] + 48 * i:_SM["adab"] + 48 * i + 48], op=ALU.add),
                 reads=["pb5", "sm"], writes=[f"modT{i}"])
            for (dst, goff, scol, nm) in ((aM[i], _SM["mixg"] + 8 * i, 8, "aM"), (aF[i], _SM["mlpg"] + 8 * i, 32, "aF")):
                S.op("dve", lambda e, dst=dst, goff=goff, scol=scol: e.scalar_tensor_tensor(
                    out=dst[:], in0=modT[i][:, scol:scol + 8], scalar=1.0, in1=sm[:, goff:goff + 8],
                    op0=ALU.add, op1=ALU.mult), reads=[f"modT{i}", "sm"], writes=[f"{nm}{i}"])

        def norm_stats(tt, pst_ap, pst_key, ndiv):
            tsl = slice(tt * 512, (tt + 1) * 512)
            for kc in range(KC):
                sq = nsq[kc % 2]
                S.op("act", lambda e, sq=sq, kc=kc: e.activation(out=sq[:], in_=xT[:, kc, tsl], func=AF.Square),
                     reads=xkeys512(kc, tt), writes=[f"nsq{kc % 2}"])
                S.op("pe", lambda e, sq=sq, kc=kc: e.matmul(pst_ap, lhsT=ones_f[:], rhs=sq[:], start=(kc == 0), stop=(kc == 7)),
                     reads=[f"nsq{kc % 2}", "ones_f"], writes=[pst_key])
            S.op("act", lambda e: e.activation(out=nrstd[tt][:], in_=pst_ap, func=AF.Sqrt, bias=eps_t[:, 0:1], scale=1.0 / ndiv),
                 reads=[pst_key, "eps"], writes=[f"nrstd{tt}"])
            S.op("dve", lambda e: e.reciprocal(out=nrstd[tt][:], in_=nrstd[tt][:]), reads=[f"nrstd{tt}"], writes=[f"nrstd{tt}"])

        eps_t = sb("eps_t", [128, 1], F32)
        S.op("dve", lambda e: e.memset(eps_t[:], EPS), writes=["eps"])
        eps4_t = sb("eps4_t", [128, 1], F32)
        S.op("dve", lambda e: e.memset(eps4_t[:], 4.0 * EPS), writes=["eps"])
        one_t = sb("one_t", [128, 1], F32)
        S.op("dve", lambda e: e.memset(one_t[:], 1.0), writes=["one"])

        def rmsnorm_mod(a_t, a_key, mod_t, mod_key, shcol, stats=True):
            if stats:
                for tt in range(4):
                    norm_stats(tt, pbank[tt % 2][:], f"pb{tt % 2}", float(D))
            for tt in range(4):
                rmsnorm_mod_tile(a_t, a_key, mod_t, mod_key, shcol, tt)

        def rmsnorm_mod_tile(a_t, a_key, mod_t, mod_key, shcol, tt):
            if True:
                tsl = slice(tt * 512, (tt + 1) * 512)
                for kc in range(KC):
                    tmp = ntmp[kc % 2]
                    S.op("dve", lambda e, tmp=tmp, kc=kc: e.scalar_tensor_tensor(
                        out=tmp[:], in0=xT[:, kc, tsl], scalar=a_t[:, kc:kc + 1], in1=nrstd[tt][:],
                        op0=ALU.mult, op1=ALU.mult),
                        reads=xkeys512(kc, tt) + [f"nrstd{tt}", a_key], writes=[f"ntmp{kc % 2}"])
                    S.op("act", lambda e, tmp=tmp, kc=kc: e.activation(
                        out=hT[:, kc, tsl], in_=tmp[:], func=AF.Identity,
                        bias=mod_t[:, shcol + kc:shcol + kc + 1], scale=1.0),
                        reads=[f"ntmp{kc % 2}", mod_key], writes=[hk(kc, tt)])

        def wslot(k):
            return Wreg[:, k * 4096:(k + 1) * 4096]

        wctr = {"n": 0}

        def load_wtile(src_ap_3d):
            k = wctr["n"] % 4
            wctr["n"] += 1
            dst = wslot(k).rearrange("p (a b) -> p a b", b=512)
            S.dma("pool", dst, src_ap_3d, f"w{k}", writes=[f"wslot{k}"])
            return dst, f"wslot{k}"

        def evac_x(psum_ap, pkey, gate_ap, gkey, dc, t8list, tsl):
            keys = [xk(dc, t) for t in t8list]
            S.op("dve", lambda e: e.scalar_tensor_tensor(out=xT[:, dc, tsl], in0=psum_ap, scalar=gate_ap,
                                                         in1=xT[:, dc, tsl], op0=ALU.mult, op1=ALU.add),
                 reads=[pkey, gkey] + keys, writes=keys)

        def mlp(i, extra=None):
            cF.reset()
            cB.reset()
            aT = cB.get(8 * L).rearrange("p (j t) -> p j t", t=L)
            rtmp = [cF.get(512) for _ in range(3)]
            nb = [0, 1, 2, 3, 4] if extra is not None else [0, 1, 2, 3, 4, 5, 6]
            prot = Rot([(pbank[b][:], f"pb{b}") for b in nb])
            rrot = Rot(list(range(3)))
            sq_eng = Rot(["dve"])
            gate = lambda dc: modT[i][:, 40 + dc:41 + dc]
            for blk in range(4):
                wu = [load_wtile(mlp_up[i][:, blk * 1024 + hf * 512: blk * 1024 + (hf + 1) * 512]
                                 .rearrange("(kc p) n -> p kc n", p=128)) for hf in range(2)]
                wd = [load_wtile(mlp_down[i][blk * 1024:(blk + 1) * 1024, hf * 512:(hf + 1) * 512]
                                 .rearrange("(j p) n -> p j n", p=128)) for hf in range(2)]
                for j in range(8):
                    if extra is not None:
                        for _ in range(2):
                            nxt = next(extra, None)
                            if nxt is not None:
                                nxt()
                    wt, wkey = wu[j // 4]
                    col = (j % 4) * 128
                    for tt in range(4):
                        tsl = slice(tt * 512, (tt + 1) * 512)
                        pp, pkey = prot.next()
                        for kc in range(KC):
                            S.op("pe", lambda e, pp=pp, wt=wt, kc=kc, col=col, tsl=tsl: e.matmul(
                                pp, lhsT=wt[:, kc, col:col + 128], rhs=hT[:, kc, tsl], start=(kc == 0), stop=(kc == 7)),
                                reads=[wkey, hk(kc, tt)], writes=[pkey])
                        r = rrot.next()
                        S.op("act", lambda e, pp=pp, r=r: e.activation(out=rtmp[r], in_=pp, func=AF.Relu),
                             reads=[pkey], writes=[f"rtmp{r}"])
                        S.op(sq_eng.next(), lambda e, r=r, j=j, tsl=tsl: e.tensor_tensor(
                            out=aT[:, j, tsl], in0=rtmp[r], in1=rtmp[r], op=ALU.mult),
                            reads=[f"rtmp{r}"], writes=[f"aT{j}_{tt}"])
                order = [(dc, tt) for dc in range(8) for tt in range(4)] if blk < 3 else \
                        [(dc, tt) for tt in range(4) for dc in range(8)]
                for (dc, tt) in order:
                    wt, wkey = wd[dc // 4]
                    col = (dc % 4) * 128
                    if True:
                        tsl = slice(tt * 512, (tt + 1) * 512)
                        pp, pkey = prot.next()
                        for j in range(8):
                            S.op("pe", lambda e, pp=pp, wt=wt, j=j, col=col, tsl=tsl: e.matmul(
                                pp, lhsT=wt[:, j, col:col + 128], rhs=aT[:, j, tsl], start=(j == 0), stop=(j == 7)),
                                reads=[wkey, f"aT{j}_{tt}"], writes=[pkey])
                        evac_x(pp, pkey, gate(dc), f"modT{i}", dc, [2 * tt, 2 * tt + 1], tsl)

        def ssd():
            cF.reset()
            cB.reset()
            TT = 256
            v3 = lambda ap: ap.rearrange("p (c h) -> p c h", h=NH)
            dt_t = cF.get(512)
            nacs_t = cF.get(512)
            dd_t = cF.get(512)
            cd_t = cF.get(512)
            t_u = cF.get(512)
            t_a = cF.get(512)
            t_l = cF.get(512)
            expA = cF.get(32)
            raw = [cF.get(264) for _ in range(4)]
            acc = [cF.get(256) for _ in range(4)]
            _r0 = 4 * 512 + 512
            raw += [arF[:, _r0:_r0 + 264], arF[:, _r0 + 264:_r0 + 528]]
            acc += [arF[:, _r0 + 528:_r0 + 784], arF[:, _r0 + 784:_r0 + 1040]]
            tmpE = [cF.get(256) for _ in range(4)]
            EA = [cF.get(256) for _ in range(3)]
            prev = cF.get(512)
            yv = [cF.get(256) for _ in range(2)]
            yg = [cF.get(256) for _ in range(4)]
            ssq = cF.get(256)
            grt = cF.get(256)
            hist = cF.get(24).rearrange("p (q k) -> p q k", k=4)
            xbc = [[cB.get(256) for _ in range(6)] for _ in range(2)]
            Xdt = [cB.get(512) for _ in range(2)]
            Xdd = [cB.get(512) for _ in range(2)]
            Btok = cB.get(256)
            acs2 = cB.get(256)
            acs_t32 = cB.get(256)
            MT = [cB.get(256) for _ in range(4)]
            Cs = [cB.get(256) for _ in range(4)]
            prevb = [cB.get(512) for _ in range(2)]
            ynorm = [cB.get(256) for _ in range(4)]
            Wdt = cB.get(256).rearrange("p (k h) -> p k h", h=NH)
            maskb = cB.get(256)
            f32v = lambda n: cB.get(2 * n).bitcast(F32)
            zs = [[f32v(256) for _ in range(4)] for _ in range(2)]
            sq = [f32v(256) for _ in range(2)]
            grstd = f32v(256)
            Wg = Wreg[:, 0:10240].rearrange("p (k n) -> p k n", n=1280)
            Wo = Wreg[:, 12288:16384].rearrange("p (k n) -> p k n", n=1024)
            wgkey = lambda col: "wg_z" if col < 512 else ("wg_x" if col < 1024 else ("wg_B" if col < 1152 else "wg_C"))
            WO_KEYS = ["wslot3"]
            big = Rot([(pbank[0][:, 0:256], "pb0"), (pbank[1][:, 0:256], "pb1"),
                       (pbank[0][:, 256:512], "pb0"), (pbank[1][:, 256:512], "pb1")])
            pplain = Rot([(pbank[2][:, 0:256], "pb2"), (pbank[2][:, 256:512], "pb2")])
            pmask = Rot([(pbank[6][:, 0:256], "pb6"), (pbank[6][:, 256:512], "pb6")])
            psc, psc_k = pbank[3][:, 0:256], "pb3"
            pacsT, pacsT_k = pbank[3][0:8, 256:512], "pb3"
            pacsT32 = pbank[3][32:40, 256:512]
            pyr = Rot([(pbank[4], 0, "pb4"), (pbank[4], 256, "pb4")])
            pst, pst_k = pbank[5][:, :], "pb5"
            pxt0 = pbf[0][:, 0:512]
            pxt1 = pbank[6][:, 0:256].bitcast(BF16)
            pbtr, pbtr_k = pbf[0][:, 512:768], "pbf0"
            gate = lambda dc: modT[0][:, 16 + dc:17 + dc]

            S.dma("pool", Wdt, ssd_in_w[:, 5120:5152].rearrange("(kc p) n -> p kc n", p=128), "wdt", writes=["Wdt"])
            S.op("dve", lambda e: e.tensor_copy(out=maskb[:, 0:128], in_=maskneg[:]), reads=["maskneg"], writes=["maskb"])
            S.op("dve", lambda e: e.tensor_copy(out=maskb[:, 128:256], in_=maskneg[:]), reads=["maskneg"], writes=["maskb"])
            S.op("dve", lambda e: e.memset(acs2[0:64, :], 0.0), reads=["acs2"], writes=["acs2"])
            pdt = pbank[0][:, :]
            for c in range(16):
                for kc in range(KC):
                    S.op("pe", lambda e, c=c, kc=kc: e.matmul(pdt[:, c * 32:(c + 1) * 32],
                                                              lhsT=hT[:, kc, c * 128:(c + 1) * 128], rhs=Wdt[:, kc, :],
                                                              start=(kc == 0), stop=(kc == 7)),
                         reads=["Wdt", hk(kc, c // 4)], writes=["pb0"])
            dtb = sm[:, _SM["dtb"]:_SM["dtb"] + 32]
            S.op("dve", lambda e: e.tensor_tensor(out=v3(t_u), in0=v3(pdt), in1=dtb.unsqueeze(1).to_broadcast([128, 16, 32]),
                                                  op=ALU.add), reads=["pb0", "sm"], writes=["t_u"])
            S.op("dve", lambda e: e.scalar_tensor_tensor(out=t_a, in0=t_u, scalar=-1.0, in1=t_u, op0=ALU.mult, op1=ALU.min),
                 reads=["t_u"], writes=["t_a"])
            S.op("act", lambda e: e.activation(out=t_a, in_=t_a, func=AF.Exp), reads=["t_a"], writes=["t_a"])
            S.op("act", lambda e: e.activation(out=t_l, in_=t_a, func=AF.Ln, bias=one_t[:, 0:1], scale=1.0),
                 reads=["t_a", "one"], writes=["t_l"])
            S.op("dve", lambda e: e.scalar_tensor_tensor(out=dt_t, in0=t_u, scalar=0.0, in1=t_l, op0=ALU.max, op1=ALU.add),
                 reads=["t_u", "t_l"], writes=["dt"])
            S.op("act", lambda e: e.activation(out=expA, in_=sm[:, _SM["alog"]:_SM["alog"] + 32], func=AF.Exp),
                 reads=["sm"], writes=["expA"])
            dtA = t_u
            S.op("dve", lambda e: e.scalar_tensor_tensor(out=v3(dtA), in0=v3(dt_t), scalar=-1.0,
                                                         in1=expA.unsqueeze(1).to_broadcast([128, 16, 32]),
                                                         op0=ALU.mult, op1=ALU.mult),
                 reads=["dt", "expA", "t_u"], writes=["dtA"])
            pacs = pbank[1][:, :]
            plast = pbank[2][:, :]
            for c in range(16):
                S.op("pe", lambda e, c=c: e.matmul(pacs[:, c * 32:(c + 1) * 32], lhsT=triu[:], rhs=v3(dtA)[:, c, :],
                                                   start=True, stop=True), reads=["triu", "dtA"], writes=["pb1"])
            for c in range(16):
                S.op("pe", lambda e, c=c: e.matmul(plast[:, c * 32:(c + 1) * 32], lhsT=ones_f[:], rhs=v3(dtA)[:, c, :],
                                                   start=True, stop=True), reads=["ones_f", "dtA"], writes=["pb2"])
            S.op("act", lambda e: e.activation(out=nacs_t, in_=pacs, func=AF.Identity, scale=-1.0), reads=["pb1"], writes=["nacs"])
            S.op("act", lambda e: e.activation(out=cd_t, in_=plast, func=AF.Exp), reads=["pb2"], writes=["cd"])
            S.op("dve", lambda e: e.tensor_tensor(out=t_l, in0=plast, in1=nacs_t, op=ALU.add),
                 reads=["pb2", "nacs", "t_l"], writes=["t_l2"])
            S.op("act", lambda e: e.activation(out=t_l, in_=t_l, func=AF.Exp), reads=["t_l2"], writes=["t_l2"])
            S.op("dve", lambda e: e.tensor_tensor(out=dd_t, in0=t_l, in1=dt_t, op=ALU.mult),
                 reads=["t_l2", "dt"], writes=["dd"])

            S.barrier()

            def load_wg(g):
                S.dma("pool", Wg[:, :, 0:512], ssd_in_w[:, g * 512:(g + 1) * 512].rearrange("(kc p) n -> p kc n", p=128),
                      "wg0", writes=["wg_z"])
                S.dma("pool", Wg[:, :, 512:1024],
                      ssd_in_w[:, 2048 + g * 512:2048 + (g + 1) * 512].rearrange("(kc p) n -> p kc n", p=128),
                      "wg1", writes=["wg_x"])
                S.dma("pool", Wg[:, :, 1024:1152],
                      ssd_in_w[:, 4096 + g * 128:4096 + (g + 1) * 128].rearrange("(kc p) n -> p kc n", p=128),
                      "wg2", writes=["wg_B"])
                S.dma("pool", Wg[:, :, 1152:1280],
                      ssd_in_w[:, 4608 + g * 128:4608 + (g + 1) * 128].rearrange("(kc p) n -> p kc n", p=128),
                      "wg3", writes=["wg_C"])
                S.op("pool", lambda e: e.memset(hist, 0.0), reads=["hist"], writes=["hist"])

            def load_wo(g):
                S.dma("pool", Wo, ssd_out_w[g * 512:(g + 1) * 512, :].rearrange("(kc p) n -> p kc n", p=128),
                      "wo", writes=WO_KEYS)

            rawrot = Rot(list(range(6)))

            def a_tasks(g, t, par):
                tsl = slice(t * TT, (t + 1) * TT)
                htt = t // 2
                convch = [4 * g + q for q in range(4)] + [16 + g, 20 + g]
                wcol = [512 + q * 128 for q in range(4)] + [1024, 1152]

                def proj(col):
                    pp, pkey = big.next()
                    for kc in range(KC):
                        S.op("pe", lambda e, pp=pp, kc=kc, col=col: e.matmul(
                            pp, lhsT=Wg[:, kc, col:col + 128], rhs=hT[:, kc, tsl],
                            start=(kc == 0), stop=(kc == 7)), reads=[wgkey(col), hk(kc, htt)], writes=[pkey])
                    return pp, pkey

                def conv_pair(qs):
                    st = []
                    for q in qs:
                        pp, pkey = proj(wcol[q])
                        ri = rawrot.next()
                        rb, ab = raw[ri], acc[ri]
                        ch = convch[q]
                        wof = _SM["convw"] + ch * 4
                        S.op("pool", lambda e, rb=rb, q=q: e.tensor_copy(out=rb[:, 0:3], in_=hist[:, q, 0:3]),
                             reads=["hist", f"raw{ri}"], writes=[f"raw{ri}"])
                        S.op("act", lambda e, rb=rb, pp=pp: e.activation(out=rb[:, 3:259], in_=pp, func=AF.Copy),
                             reads=[pkey, f"raw{ri}"], writes=[f"raw{ri}"])
                        S.op("act", lambda e, ab=ab, pp=pp, wof=wof, ch=ch: e.activation(
                            out=ab, in_=pp, func=AF.Identity, scale=sm[:, wof + 3:wof + 4],
                            bias=sm[:, _SM["convb"] + ch:_SM["convb"] + ch + 1]),
                            reads=[pkey, "sm", f"acc{ri}"], writes=[f"acc{ri}"])
                        S.op("pool", lambda e, rb=rb, q=q: e.tensor_copy(out=hist[:, q, 0:3], in_=rb[:, 256:259]),
                             reads=[f"raw{ri}", "hist"], writes=["hist"])
                        st.append((q, ri, rb, ab, wof))
                    for k in (2, 1, 0):
                        for (q, ri, rb, ab, wof) in st:
                            S.op("dve", lambda e, ab=ab, rb=rb, wof=wof, k=k: e.scalar_tensor_tensor(
                                out=ab, in0=rb[:, k:k + 256], scalar=sm[:, wof + k:wof + k + 1], in1=ab,
                                op0=ALU.mult, op1=ALU.add), reads=[f"raw{ri}", "sm", f"acc{ri}"], writes=[f"acc{ri}"])
                    for (q, ri, rb, ab, wof) in st:
                        S.op("act", lambda e, ab=ab, rb=rb: e.activation(out=rb[:, 0:256], in_=ab, func=AF.Tanh),
                             reads=[f"acc{ri}", f"raw{ri}"], writes=[f"raw{ri}"])
                    for (q, ri, rb, ab, wof) in st:
                        S.op("dve", lambda e, ab=ab, rb=rb, q=q: e.scalar_tensor_tensor(
                            out=xbc[par][q], in0=rb[:, 0:256], scalar=1.0, in1=ab, op0=ALU.add, op1=ALU.mult),
                            reads=[f"acc{ri}", f"raw{ri}"], writes=[f"xbc{par}_{q}"])

                def z_pair(fcs):
                    for fc in fcs:
                        pz, pzk = proj(fc * 128)
                        S.op("act", lambda e, pz=pz, fc=fc: e.activation(out=zs[par][fc], in_=pz, func=AF.Tanh, scale=0.5),
                             reads=[pzk, f"zs{par}_{fc}"], writes=[f"zs{par}_{fc}"])
                        S.op("dve", lambda e, pz=pz, fc=fc: e.scalar_tensor_tensor(
                            out=zs[par][fc], in0=zs[par][fc], scalar=1.0, in1=pz, op0=ALU.add, op1=ALU.mult),
                            reads=[pzk, f"zs{par}_{fc}"], writes=[f"zs{par}_{fc}"])

                convs = [(lambda q=q: conv_pair([q])) for q in range(6)]
                zsl = [(lambda fc=fc: z_pair([fc])) for fc in range(4)]
                return convs, zsl

            def b_s2(g, t, par, step):
                xb = xbc[par]
                xkey = lambda q: f"xbc{par}_{q}"
                if step == 0:
                    for c2 in range(2):
                        csl = slice(c2 * 128, (c2 + 1) * 128)
                        S.op("pe", lambda e, csl=csl: e.matmul(psc[:, csl], lhsT=xb[4][:, csl], rhs=xb[5][:, csl],
                                                               start=True, stop=True),
                             reads=[xkey(4), xkey(5)], writes=[psc_k])
                    for c2 in range(2):
                        c = 2 * t + c2
                        S.op("pe", lambda e, c=c, c2=c2: e.matmul(pacsT[:, c2 * 128:(c2 + 1) * 128],
                                                                  lhsT=v3(dtA)[:, c, 8 * g:8 * g + 8], rhs=triu[:],
                                                                  start=True, stop=True),
                             reads=["dtA", "triu"], writes=[pacsT_k])
                    for c2 in range(2):
                        c = 2 * t + c2
                        S.op("pe", lambda e, c=c, c2=c2: e.matmul(pacsT32[:, c2 * 128:(c2 + 1) * 128],
                                                                  lhsT=v3(dtA)[:, c, 8 * g:8 * g + 8], rhs=triu[:],
                                                                  start=True, stop=True, tile_position=(0, 32)),
                             reads=["dtA", "triu"], writes=[pacsT_k])
                    S.op("act", lambda e: e.activation(out=acs2[0:8, :], in_=pacsT, func=AF.Copy),
                         reads=[pacsT_k, "acs2"], writes=["acs2"])
                    S.op("act", lambda e: e.activation(out=acs_t32[32:40, :], in_=pacsT32, func=AF.Copy),
                         reads=[pacsT_k, "acs_t32"], writes=["acs_t32"])
                    S.op("dve", lambda e: e.tensor_tensor(out=acs2[32:40, :], in0=pacsT32, in1=acs_t32[32:40, :],
                                                          op=ALU.subtract),
                         reads=[pacsT_k, "acs_t32", "acs2"], writes=["acs2"])
                    return
                c2 = step - 1
                c = 2 * t + c2
                csl = slice(c2 * 128, (c2 + 1) * 128)
                pxt, pxt_k = (pxt0, "pbf0") if c2 == 0 else (pxt1, "pb6")
                for q in range(4):
                    S.op("pe", lambda e, q=q: e.transpose(
                        out=pxt[:, q * 128:(q + 1) * 128], in_=xb[q][:, csl], identity=ident_b[:]),
                        reads=[xkey(q), "ident_b"], writes=[pxt_k])
                if c2 == 0:
                    for cc in range(2):
                        ccsl = slice(cc * 128, (cc + 1) * 128)
                        S.op("pe", lambda e, ccsl=ccsl: e.transpose(out=pbtr[:, ccsl], in_=xb[4][:, ccsl], identity=ident_b[:]),
                             reads=[xkey(4), "ident_b"], writes=[pbtr_k])
                S.op("dve", lambda e: e.tensor_tensor(
                    out=Xdt[c2].rearrange("p (h d) -> p h d", d=64), in0=pxt.rearrange("p (h d) -> p h d", d=64),
                    in1=v3(dt_t)[:, c, 8 * g:8 * g + 8].unsqueeze(2).to_broadcast([128, 8, 64]), op=ALU.mult),
                    reads=[pxt_k, "dt", f"Xdt{c2}"], writes=[f"Xdt{c2}"])
                S.op("dve", lambda e: e.tensor_tensor(
                    out=Xdd[c2].rearrange("p (h d) -> p h d", d=64), in0=pxt.rearrange("p (h d) -> p h d", d=64),
                    in1=v3(dd_t)[:, c, 8 * g:8 * g + 8].unsqueeze(2).to_broadcast([128, 8, 64]), op=ALU.mult),
                    reads=[pxt_k, "dd", f"Xdd{c2}"], writes=[f"Xdd{c2}"])
                if c2 == 0:
                    S.op("act", lambda e: e.activation(out=Btok, in_=pbtr, func=AF.Copy),
                         reads=[pbtr_k, "Btok"], writes=["Btok"])

            def b_s3a(g, t, c2):
                if True:
                    c = 2 * t + c2
                    csl = slice(c2 * 128, (c2 + 1) * 128)
                    S.op("act", lambda e, c2=c2: e.activation(out=prevb[c2], in_=prev, func=AF.Copy),
                         reads=["prev", f"prevb{c2}"], writes=[f"prevb{c2}"])
                    pst, pst_k = (pbank[4][:, :], "pb4") if c2 == 0 else (pbank[5][:, :], "pb5")
                    S.op("pe", lambda e, c2=c2, csl=csl, pst=pst: e.matmul(pst, lhsT=Btok[:, csl], rhs=Xdd[c2], start=True, stop=True),
                         reads=["Btok", f"Xdd{c2}"], writes=[pst_k])
                    S.op("pool", lambda e, c=c: e.tensor_tensor(
                        out=prev.rearrange("p (h d) -> p h d", d=64), in0=prev.rearrange("p (h d) -> p h d", d=64),
                        in1=v3(cd_t)[:, c, 8 * g:8 * g + 8].unsqueeze(2).to_broadcast([128, 8, 64]), op=ALU.mult),
                        reads=["prev", "cd"], writes=["prev"])
                    S.op("dve", lambda e, pst=pst: e.tensor_tensor(out=prev, in0=prev, in1=pst, op=ALU.add),
                         reads=["prev", pst_k], writes=["prev"])

            earot = Rot(list(range(3)))

            def b_head_pre(g, t, h, par):
                hh = 8 * g + h
                mi = h % 4
                xb5, xk5 = xbc[par][5], f"xbc{par}_5"
                pa, pak = pplain.next()
                pm, pmk = pmask.next()
                for (pt, ptk, msk) in ((pa, pak, False), (pm, pmk, True)):
                    S.op("pe", lambda e, pt=pt, h=h, msk=msk: e.matmul(pt, lhsT=sel8[:, h, :], rhs=acs2[0:40, :],
                                                                       start=True, stop=(not msk)),
                         reads=["sel8", "acs2"], writes=[ptk])
                    if msk:
                        S.op("pe", lambda e, pt=pt: e.matmul(pt, lhsT=ident_b[:], rhs=maskb, start=False, stop=True),
                             reads=["ident_b", "maskb"], writes=[ptk])
                ei = earot.next()
                ea = EA[ei]
                S.op("act", lambda e, ea=ea, pa=pa: e.activation(out=ea, in_=pa, func=AF.Exp),
                     reads=[pak, f"EA{ei}"], writes=[f"EA{ei}"])
                te = tmpE[mi]
                for c2 in range(2):
                    c = 2 * t + c2
                    csl = slice(c2 * 128, (c2 + 1) * 128)
                    S.op("act", lambda e, te=te, pm=pm, csl=csl, c=c, hh=hh: e.activation(
                        out=te[:, csl], in_=pm[:, csl], func=AF.Exp, bias=v3(nacs_t)[:, c, hh:hh + 1], scale=1.0),
                        reads=[pmk, "nacs", f"tmpE{mi}"], writes=[f"tmpE{mi}"])
                S.op("dve", lambda e, te=te, mi=mi: e.tensor_tensor(out=MT[mi], in0=te, in1=psc, op=ALU.mult),
                     reads=[f"tmpE{mi}", psc_k, f"MT{mi}"], writes=[f"MT{mi}"])
                S.op("pool", lambda e, ea=ea, mi=mi: e.tensor_tensor(out=Cs[mi], in0=xb5, in1=ea, op=ALU.mult),
                     reads=[f"EA{ei}", xk5, f"Cs{mi}"], writes=[f"Cs{mi}"])

            def b_head_y(g, t, fc, par):
                pyb, pyo, pyk = pyr.next()
                for c2 in range(2):
                    csl = slice(c2 * 128, (c2 + 1) * 128)
                    for hp in range(2):
                        h = 2 * fc + hp
                        mi = h % 4
                        outap = pyb[hp * 64:(hp + 1) * 64, pyo + c2 * 128:pyo + (c2 + 1) * 128]
                        tp = (0, 64) if hp == 1 else None
                        S.op("pe", lambda e, outap=outap, c2=c2, h=h, mi=mi, csl=csl, tp=tp: e.matmul(
                            outap, lhsT=Xdt[c2][:, h * 64:(h + 1) * 64], rhs=MT[mi][:, csl],
                            start=True, stop=False, tile_position=tp),
                            reads=[f"Xdt{c2}", f"MT{mi}"], writes=[pyk])
                        S.op("pe", lambda e, outap=outap, c2=c2, h=h, mi=mi, csl=csl, tp=tp: e.matmul(
                            outap, lhsT=prevb[c2][:, h * 64:(h + 1) * 64], rhs=Cs[mi][:, csl],
                            start=False, stop=True, tile_position=tp),
                            reads=[f"prevb{c2}", f"Cs{mi}"], writes=[pyk])
                pyap = pyb[:, pyo:pyo + 256]
                dcol = _SM["dfeat"] + 4 * g + fc
                yvb = yv[fc % 2]
                S.op("dve", lambda e, pyap=pyap, dcol=dcol, yvb=yvb: e.scalar_tensor_tensor(
                    out=yvb, in0=xbc[par][fc], scalar=sm[:, dcol:dcol + 1], in1=pyap, op0=ALU.mult, op1=ALU.add),
                    reads=[pyk, f"xbc{par}_{fc}", "sm", f"yv{fc % 2}"], writes=[f"yv{fc % 2}"])
                S.op("dve", lambda e, yvb=yvb: e.tensor_tensor(out=yg[fc], in0=yvb, in1=zs[par][fc], op=ALU.mult),
                     reads=[f"yv{fc % 2}", f"zs{par}_{fc}", f"yg{fc}"], writes=[f"yg{fc}"])
                if fc == 0:
                    S.op("act", lambda e: e.activation(out=ssq, in_=yg[0], func=AF.Square),
                         reads=["yg0", "ssq"], writes=["ssq"])
                else:
                    sqb = sq[fc % 2]
                    S.op("act", lambda e, sqb=sqb: e.activation(out=sqb, in_=yg[fc], func=AF.Square),
                         reads=[f"yg{fc}", f"sq{fc % 2}"], writes=[f"sq{fc % 2}"])
                    if fc < 3:
                        S.op("pool", lambda e, sqb=sqb: e.tensor_tensor(out=ssq, in0=ssq, in1=sqb, op=ALU.add),
                             reads=[f"sq{fc % 2}", "ssq"], writes=["ssq"])

            s4state = {}

            def b_s4(g, t):
                pn, pnk = big.next()
                S.op("pe", lambda e, pn=pn: e.matmul(pn, lhsT=ones_f[:], rhs=ssq, start=True, stop=False),
                     reads=["ones_f", "ssq"], writes=[pnk])
                S.op("pe", lambda e, pn=pn: e.matmul(pn, lhsT=ones_f[:], rhs=sq[1], start=False, stop=True),
                     reads=["ones_f", "sq1"], writes=[pnk])
                s4state["p"] = (pn, pnk, g)

            def b_s4e():
                pn, pnk, g = s4state.pop("p")
                S.op("act", lambda e, pn=pn: e.activation(out=grt, in_=pn, func=AF.Sqrt, bias=eps4_t[:, 0:1], scale=1.0 / 512.0),
                     reads=[pnk, "eps", "grt"], writes=["grt"])
                S.op("dve", lambda e: e.reciprocal(out=grstd, in_=grt), reads=["grt", "grstd"], writes=["grstd"])
                for fc in range(4):
                    ncol = _SM["nw"] + 4 * g + fc
                    S.op("dve", lambda e, fc=fc, ncol=ncol: e.scalar_tensor_tensor(
                        out=ynorm[fc], in0=yg[fc], scalar=sm[:, ncol:ncol + 1], in1=grstd, op0=ALU.mult, op1=ALU.mult),
                        reads=[f"yg{fc}", "sm", "grstd", f"ynorm{fc}"], writes=[f"ynorm{fc}"])

            def b_s4b(g, t, dc):
                tsl = slice(t * TT, (t + 1) * TT)
                if True:
                    pp, pkey = big.next()
                    for kc4 in range(4):
                        S.op("pe", lambda e, pp=pp, kc4=kc4, dc=dc: e.matmul(
                            pp, lhsT=Wo[:, kc4, dc * 128:(dc + 1) * 128], rhs=ynorm[kc4],
                            start=(kc4 == 0), stop=(kc4 == 3)), reads=WO_KEYS + [f"ynorm{kc4}"], writes=[pkey])
                    evac_x(pp, pkey, gate(dc), "modT0", dc, [t], tsl)

            iters = [(g, t) for g in range(4) for t in range(8)]
            load_wg(0)
            load_wo(0)
            cv0, zs0 = a_tasks(0, 0, 0)
            for task in cv0 + zs0:
                task()
            for idx, (g, t) in enumerate(iters):
                par = idx % 2
                convs, zsl, outs = [], [], []
                if idx + 1 < len(iters):
                    ng, nt = iters[idx + 1]
                    if nt == 0:
                        load_wg(ng)
                    convs, zsl = a_tasks(ng, nt, 1 - par)
                if idx > 0:
                    pg, pt = iters[idx - 1]
                    outs = [(lambda dc=dc, pg=pg, pt=pt: b_s4b(pg, pt, dc)) for dc in range(8)]
                convs, zsl, outs = iter(convs), iter(zsl), iter(outs)

                def fill(kinds):
                    for kd in kinds:
                        f = next({"c": convs, "z": zsl, "o": outs}[kd], None)
                        if f is not None:
                            f()
                if t == 0:
                    S.op("pool", lambda e: e.memset(prev, 0.0), reads=["prev"], writes=["prev"])
                b_s2(g, t, par, 0)
                if "p" in s4state:
                    b_s4e()
                fill("c")
                b_s2(g, t, par, 1)
                fill("cz")
                b_s2(g, t, par, 2)
                fill("cz")
                b_s3a(g, t, 0)
                fill("co")
                b_s3a(g, t, 1)
                fill("co")
                plan = ["o", "cz", "o", "z", "o", "o", "o", ""]
                for h in range(8):
                    b_head_pre(g, t, h, par)
                    if h >= 2 and h % 2 == 0:
                        b_head_y(g, t, h // 2 - 1, par)
                    fill(plan[h])
                b_head_y(g, t, 3, par)
                fill("cccccczzzz")
                b_s4(g, t)
                fill("oooooooo")
                if idx > 0 and iters[idx - 1][1] == 7:
                    load_wo(g)
            b_s4e()
            for dc in range(8):
                b_s4b(iters[-1][0], iters[-1][1], dc)

        sc_wts = {}

        def load_sc(j):
            k = wctr["n"] % 4
            wctr["n"] += 1
            wt = wslot(k)[:, 0:3072].rearrange("p (kc th f) -> p kc th f", th=3, f=128)
            wkey = f"wslot{k}"
            for th in range(3):
                S.dma("pool", wt[:, :, th, :],
                      sc_in_w[:, th * 1024 + j * 128: th * 1024 + (j + 1) * 128].rearrange("(kc p) f -> p kc f", p=128),
                      f"w{k}", writes=[wkey])
            sc_wts[j] = (wt, wkey)

        def shortconv():
            cF.reset()
            cB.reset()
            yT = cB.get(8 * L).rearrange("p (j t) -> p j t", t=L)
            xv = [cF.get(512) for _ in range(2)]
            ub = [cF.get(516) for _ in range(2)]
            vb = [cF.get(512) for _ in range(2)]
            prot = Rot([(pbank[b][:], f"pb{b}") for b in range(6)])
            gate = lambda dc: modT[1][:, 16 + dc:17 + dc]
            n = 0
            wts = sc_wts
            for j in range(3):
                if j not in wts:
                    load_sc(j)
            wo = []
            for j in range(8):
                if j + 3 < 8:
                    load_sc(j + 3)
                elif len(wo) < 2:
                    hf = len(wo)
                    wo.append(load_wtile(sc_out_w[:, hf * 512:(hf + 1) * 512].rearrange("(kc p) n -> p kc n", p=128)))
                wt, wkey = wts[j]
                for tt in range(4):
                    tsl = slice(tt * 512, (tt + 1) * 512)
                    pps = []
                    for th in range(3):
                        pp, pkey = prot.next()
                        pps.append((pp, pkey))
                        for kc in range(KC):
                            S.op("pe", lambda e, pp=pp, kc=kc, th=th, wt=wt, tsl=tsl: e.matmul(
                                pp, lhsT=wt[:, kc, th, :], rhs=hT[:, kc, tsl], start=(kc == 0), stop=(kc == 7)),
                                reads=[wkey, hk(kc, tt)], writes=[pkey])
                    (pB, pBk), (pC, pCk), (pX, pXk) = pps
                    bi = n % 2
                    n += 1
                    u, uprev, v, xvb = ub[bi], ub[1 - bi], vb[bi], xv[bi]
                    S.op("act", lambda e, xvb=xvb, pX=pX: e.activation(out=xvb, in_=pX, func=AF.Copy),
                         reads=[pXk, f"xv{bi}"], writes=[f"xv{bi}"])
                    if tt == 0:
                        S.op("dve", lambda e, u=u: e.memset(u[:, 0:2], 0.0), reads=[f"u{bi}"], writes=[f"u{bi}"])
                    else:
                        S.op("act", lambda e, u=u, uprev=uprev: e.activation(out=u[:, 0:2], in_=uprev[:, 512:514], func=AF.Copy),
                             reads=[f"u{1 - bi}", f"u{bi}"], writes=[f"u{bi}"])
                    S.op("dve", lambda e, u=u, pC=pC, xvb=xvb: e.tensor_tensor(out=u[:, 2:514], in0=pC, in1=xvb, op=ALU.mult),
                         reads=[pCk, f"xv{bi}", f"u{bi}"], writes=[f"u{bi}"])
                    wof = _SM["scw"] + j * 3
                    S.op("act", lambda e, u=u, v=v, wof=wof: e.activation(
                        out=v, in_=u[:, 0:512], func=AF.Identity, scale=sm[:, wof:wof + 1]),
                        reads=[f"u{bi}", "sm", f"v{bi}"], writes=[f"v{bi}"])
                    for kk in (1, 2):
                        S.op("dve", lambda e, u=u, v=v, wof=wof, kk=kk: e.scalar_tensor_tensor(
                            out=v, in0=u[:, kk:kk + 512], scalar=sm[:, wof + kk:wof + kk + 1], in1=v,
                            op0=ALU.mult, op1=ALU.add), reads=[f"u{bi}", "sm", f"v{bi}"], writes=[f"v{bi}"])
                    S.op("dve", lambda e, v=v, pB=pB, j=j, tsl=tsl: e.tensor_tensor(out=yT[:, j, tsl], in0=pB, in1=v, op=ALU.mult),
                         reads=[pBk, f"v{bi}"], writes=[f"yT{j}_{tt}"])
            for (dc, tt) in [(dc, tt) for tt in range(4) for dc in range(8)]:
                wt, wkey = wo[dc // 4]
                col = (dc % 4) * 128
                if True:
                    tsl = slice(tt * 512, (tt + 1) * 512)
                    pp, pkey = prot.next()
                    for j in range(8):
                        S.op("pe", lambda e, pp=pp, wt=wt, j=j, col=col, tsl=tsl: e.matmul(
                            pp, lhsT=wt[:, j, col:col + 128], rhs=yT[:, j, tsl], start=(j == 0), stop=(j == 7)),
                            reads=[wkey, f"yT{j}_{tt}"], writes=[pkey])
                    evac_x(pp, pkey, gate(dc), "modT1", dc, [2 * tt, 2 * tt + 1], tsl)

        def dump_x():
            for kc in range(KC):
                S.dma("sp", out_d[:, kc, :], xT[:, kc, :], "out", reads=[xk(kc, t) for t in range(8)])

        def dump_h():
            cF.reset()
            tmp = cF.get(2048)
            for kc in range(KC):
                S.op("dve", lambda e, kc=kc: e.tensor_copy(out=tmp, in_=hT[:, kc, :]),
                     reads=[hk(kc, tt) for tt in range(4)] + ["dumptmp"], writes=["dumptmp"])
                S.dma("sp", out_d[:, kc, :], tmp, "out", reads=["dumptmp"])

        def final_norm():
            cF.reset()
            cF.get(1536)
            ob = [cF.get(512) for _ in range(3)]
            ob += [arF[:, 4608 + i * 512:4608 + (i + 1) * 512] for i in range(2)]
            NOB = len(ob)
            n = 0
            for tt in range(4):
                norm_stats(tt, pbank[tt % 2][:], f"pb{tt % 2}", float(D))
            for tt in range(4):
                tsl = slice(tt * 512, (tt + 1) * 512)
                for kc in range(KC):
                    oi = n % NOB
                    n += 1
                    o = ob[oi]
                    gcol = _SM["fing"] + kc
                    S.op("dve", lambda e, o=o, kc=kc, gcol=gcol, tsl=tsl, tt=tt: e.scalar_tensor_tensor(
                        out=o, in0=xT[:, kc, tsl], scalar=sm[:, gcol:gcol + 1], in1=nrstd[tt][:], op0=ALU.mult, op1=ALU.mult),
                        reads=xkeys512(kc, tt) + [f"nrstd{tt}", "sm", f"ob{oi}"], writes=[f"ob{oi}"])
                    S.dma("sp", out_d[:, kc, tsl], o, f"out{oi}", reads=[f"ob{oi}"])

        pmod = pbank[5]
        for tt in range(4):
            norm_stats(tt, pbank[tt % 2][:], f"pb{tt % 2}", float(D))
        for jj in range(12):
            ada_piece(0, jj * 4, 512, ada_st0, pmod)
        ada_finish(0, pmod)
        rmsnorm_mod(aM[0], "aM0", modT[0], "modT0", 0, stats=False)
        done = False
        if dbg_stop == "norm0":
            dump_h()
            done = True
        if not done:
            S.barrier()
            ssd()
            if dbg_stop == "mix0":
                dump_x()
                done = True
        if not done:
            S.barrier()
            rmsnorm_mod(aF[0], "aF0", modT[0], "modT0", 24)
            def ada1_gen():
                for jj in range(24):
                    yield (lambda jj=jj: ada_piece(1, jj * 2, 256, ada_st1, pmod))
            gen = ada1_gen()
            mlp(0, extra=gen)
            for rest in gen:
                rest()
            ada_finish(1, pmod)
            if dbg_stop == "mlp0":
                dump_x()
                done = True
        if not done:
            for j in range(3):
                load_sc(j)
            rmsnorm_mod(aM[1], "aM1", modT[1], "modT1", 0)
            S.barrier()
            shortconv()
            if dbg_stop == "mix1":
                dump_x()
                done = True
        if not done:
            rmsnorm_mod(aF[1], "aF1", modT[1], "modT1", 24)
            S.barrier()
            mlp(1)
            if dbg_stop == "mlp1":
                dump_x()
                done = True
        if not done:
            final_norm()
        S.emit(final_waits=[("sp", nm) for nm in S.dma_sems if nm.startswith("out")])
    return nc


def _chunks(v):
    v = np.asarray(v, np.float32)
    return np.ascontiguousarray(v.reshape(-1, 128).T)


def make_smalls(b, c, ada_b, mix_norm_w, mlp_norm_w, ssd_conv_w, ssd_conv_b, ssd_dt_bias, ssd_A_log,
                ssd_D, ssd_norm_w, sc_conv_w, final_norm_w):
    sm = np.zeros((128, NS), np.float32)
    sm[:, _SM["c"]:_SM["c"] + 8] = _chunks(c[b])
    for i in range(2):
        sm[:, _SM["adab"] + 48 * i:_SM["adab"] + 48 * (i + 1)] = _chunks(ada_b[i])
        sm[:, _SM["mixg"] + 8 * i:_SM["mixg"] + 8 * (i + 1)] = _chunks(mix_norm_w[i])
        sm[:, _SM["mlpg"] + 8 * i:_SM["mlpg"] + 8 * (i + 1)] = _chunks(mlp_norm_w[i])
    sm[:, _SM["fing"]:_SM["fing"] + 8] = _chunks(final_norm_w)
    cw = np.asarray(ssd_conv_w[0], np.float32)
    for k in range(4):
        sm[:, _SM["convw"] + k:_SM["convw"] + 96:4] = _chunks(cw[k])
    sm[:, _SM["convb"]:_SM["convb"] + 24] = _chunks(ssd_conv_b[0])
    sm[:, _SM["dfeat"]:_SM["dfeat"] + 16] = _chunks(np.repeat(np.asarray(ssd_D[0], np.float32), 64))
    sm[:, _SM["nw"]:_SM["nw"] + 16] = _chunks(ssd_norm_w[0])
    sm[:, _SM["dtb"]:_SM["dtb"] + 32] = np.broadcast_to(np.asarray(ssd_dt_bias[0], np.float32)[None, :], (128, 32))
    sm[:, _SM["alog"]:_SM["alog"] + 32] = np.broadcast_to(np.asarray(ssd_A_log[0], np.float32)[None, :], (128, 32))
    sw = np.asarray(sc_conv_w[0], np.float32)
    for k in range(3):
        sm[:, _SM["scw"] + k:_SM["scw"] + 24:3] = _chunks(sw[k])
    return sm


_NC_CACHE = {}


def _get_nc(dbg_stop=None):
    if dbg_stop not in _NC_CACHE:
        _NC_CACHE[dbg_stop] = build_program(dbg_stop)
    return _NC_CACHE[dbg_stop]


def kernel(x, c, ada_w, ada_b, mix_norm_w, mlp_norm_w, mlp_up, mlp_down,
           ssd_in_w, ssd_conv_w, ssd_conv_b, ssd_dt_bias, ssd_A_log, ssd_D,
           ssd_norm_w, ssd_out_w, sc_in_w, sc_conv_w, sc_out_w, final_norm_w, _dbg_stop=None):
    n = 8
    x = np.asarray(x, np.float32)
    f = lambda a: np.ascontiguousarray(np.asarray(a, np.float32))
    shared = {
        "ada_w": f(ada_w), "mlp_up": f(mlp_up), "mlp_down": f(mlp_down),
        "ssd_in_w": f(ssd_in_w[0]), "ssd_out_w": f(ssd_out_w[0]),
        "sc_in_w": f(sc_in_w[0]), "sc_out_w": f(sc_out_w[0]),
    }
    in_maps = []
    for b in range(n):
        xT = np.ascontiguousarray(x[b].reshape(L, KC, 128).transpose(2, 1, 0))
        smalls = make_smalls(b, c, ada_b, mix_norm_w, mlp_norm_w, ssd_conv_w, ssd_conv_b, ssd_dt_bias,
                             ssd_A_log, ssd_D, ssd_norm_w, sc_conv_w, final_norm_w)
        m = {"xT": xT, "smalls": smalls}
        m.update(shared)
        in_maps.append(m)
    nc = _get_nc(_dbg_stop)
    res = run_bass_kernel_spmd(nc, in_maps, core_ids=list(range(n)))
    outs = []
    for b in range(n):
        oT = np.asarray(res.results[b]["outT"], np.float32)
        outs.append(oT.transpose(2, 1, 0).reshape(L, D))
    return np.stack(outs, axis=0).astype(np.float32)
```

```python
import contextlib
import numpy as np
import concourse.bass as bass
import concourse.mybir as mybir
from concourse.bass_utils import run_bass_kernel_spmd

F32 = mybir.dt.float32
BF16 = mybir.dt.bfloat16
AF = mybir.ActivationFunctionType
ALU = mybir.AluOpType
AX = mybir.AxisListType

D = 1024
L = 2048
KC = 8
DFF = 4096
NH = 32
EPS = 1e-5
ENGS = ("pe", "act", "dve", "pool", "sp")

_SM = {}
_off = 0
for _n, _w in (("c", 8), ("adab", 96), ("mixg", 16), ("mlpg", 16), ("fing", 8), ("convw", 96),
               ("convb", 24), ("dfeat", 16), ("nw", 16), ("dtb", 32), ("alog", 32), ("scw", 24)):
    _SM[_n] = _off
    _off += _w
NS = _off


class _Op:
    __slots__ = ("eng", "fn", "idx", "deps", "dma_sem", "dma_cnt", "ms", "is_dma", "done")

    def __init__(self, eng, fn, idx):
        self.eng = eng
        self.fn = fn
        self.idx = idx
        self.deps = []
        self.dma_sem = None
        self.dma_cnt = 0
        self.ms = 0
        self.is_dma = False
        self.done = False


class Sched:
    def __init__(self, nc):
        self.nc = nc
        self.ops = {e: [] for e in ENGS}
        self.last_w = {}
        self.readers = {}
        self.dma_sems = {}
        self.SAME_ENG_DIST = 10 ** 9

    def _track(self, rec, reads, writes):
        deps = []
        for k in reads:
            w = self.last_w.get(k)
            if w is not None and w is not rec:
                deps.append(w)
        for k in writes:
            w = self.last_w.get(k)
            if w is not None and w is not rec:
                deps.append(w)
            for r in self.readers.get(k, ()):
                if r is not rec:
                    deps.append(r)
        for k in reads:
            self.readers.setdefault(k, []).append(rec)
        for k in writes:
            self.last_w[k] = rec
            self.readers[k] = []
        best = {}
        for d in deps:
            if d.is_dma:
                key = ("dma", d.dma_sem)
                if key not in best or best[key].dma_cnt < d.dma_cnt:
                    best[key] = d
            else:
                key = d.eng
                if key not in best or best[key].idx < d.idx:
                    best[key] = d
        rec.deps = list(best.values())

    def op(self, eng, fn, reads=(), writes=()):
        pw = [k for k in reads if k.startswith("pb")]
        if pw:
            reads = [k for k in reads if not k.startswith("pb")]
            writes = list(writes) + pw
        rec = _Op(eng, fn, len(self.ops[eng]))
        self._track(rec, reads, writes)
        self.ops[eng].append(rec)
        return rec

    def dma(self, eng, out, in_, sem, reads=(), writes=()):
        def fn(e, out=out, in_=in_):
            return e.dma_start(out=out, in_=in_)
        rec = _Op(eng, fn, len(self.ops[eng]))
        rec.is_dma = True
        ent = self.dma_sems.setdefault(sem, [None, 0])
        ent[1] += 16
        rec.dma_sem = sem
        rec.dma_cnt = ent[1]
        self._track(rec, reads, writes)
        self.ops[eng].append(rec)
        return rec

    def barrier(self):
        lasts = []
        for e in ENGS:
            real = [r for r in self.ops[e] if r.fn is not None]
            if real:
                lasts.append(real[-1])
        dl = {}
        for e in ENGS:
            for r in self.ops[e]:
                if r.is_dma:
                    dl[r.dma_sem] = r
        for e in ENGS:
            rec = _Op(e, None, len(self.ops[e]))
            rec.deps = [d for d in lasts if d.eng != e and not d.is_dma] + list(dl.values())
            self.ops[e].append(rec)

    def _needs_wait(self, rec, d):
        if d.is_dma:
            return True
        if d.eng != rec.eng:
            return True
        if rec.is_dma:
            return True
        if rec.eng == "pe":
            return False
        if rec.eng == "pool":
            return True
        return (rec.idx - d.idx) < self.SAME_ENG_DIST

    def check(self):
        ptr = {e: 0 for e in ENGS}
        for e in ENGS:
            for r in self.ops[e]:
                r.done = False
        progress = True
        while progress:
            progress = False
            for e in ENGS:
                ops = self.ops[e]
                while ptr[e] < len(ops):
                    r = ops[ptr[e]]
                    if all(d.done for d in r.deps):
                        r.done = True
                        ptr[e] += 1
                        progress = True
                    else:
                        break
        for e in ENGS:
            if ptr[e] < len(self.ops[e]):
                raise RuntimeError(f"schedule deadlock on {e} at op {ptr[e]}")

    def emit(self, final_waits=()):
        nc = self.nc
        self.check()
        for e in ENGS:
            for rec in self.ops[e]:
                for d in rec.deps:
                    if not d.is_dma and self._needs_wait(rec, d):
                        d.ms = -1
        for e in ENGS:
            n = 0
            for rec in self.ops[e]:
                if rec.ms == -1:
                    n += 1
                    rec.ms = n
        with contextlib.ExitStack() as st:
            esem = {e: st.enter_context(nc.semaphore("s_" + e)) for e in ENGS}
            for name in self.dma_sems:
                self.dma_sems[name][0] = st.enter_context(nc.semaphore("d_" + name))
            block = st.enter_context(nc.Block())
            bmap = {"pe": block.tensor, "act": block.scalar, "dve": block.vector,
                    "pool": block.gpsimd, "sp": block.sync}
            for e in ENGS:
                ops = self.ops[e]
                fin = [fw for fw in final_waits if fw[0] == e]
                if not ops and not fin:
                    continue

                def body(eng, e=e, ops=ops, fin=fin):
                    waited = {}
                    for rec in ops:
                        for d in rec.deps:
                            if not self._needs_wait(rec, d):
                                continue
                            if d.is_dma:
                                key, val, sem = ("d", d.dma_sem), d.dma_cnt, self.dma_sems[d.dma_sem][0]
                            else:
                                key, val, sem = ("e", d.eng), d.ms, esem[d.eng]
                            if waited.get(key, 0) >= val:
                                continue
                            waited[key] = val
                            eng.wait_ge(sem, val)
                        if rec.fn is None:
                            continue
                        ins = rec.fn(eng)
                        if rec.is_dma:
                            ins.then_inc(self.dma_sems[rec.dma_sem][0], 16)
                        elif rec.ms > 0:
                            ins.then_inc(esem[e], 1)
                    for (_, name) in fin:
                        eng.wait_ge(self.dma_sems[name][0], self.dma_sems[name][1])
                bmap[e](body)


class Rot:
    def __init__(self, items):
        self.items = items
        self.i = 0

    def next(self):
        it = self.items[self.i % len(self.items)]
        self.i += 1
        return it


def build_program(dbg_stop=None):
    nc = bass.Bass("TRN2", target_bir_lowering=False)
    xT_d = nc.dram_tensor("xT", [128, KC, L], F32, kind="ExternalInput").ap()
    sm_d = nc.dram_tensor("smalls", [128, NS], F32, kind="ExternalInput").ap()
    ada_w = nc.dram_tensor("ada_w", [2, D, 6 * D], F32, kind="ExternalInput").ap()
    mlp_up = nc.dram_tensor("mlp_up", [2, D, DFF], F32, kind="ExternalInput").ap()
    mlp_down = nc.dram_tensor("mlp_down", [2, DFF, D], F32, kind="ExternalInput").ap()
    ssd_in_w = nc.dram_tensor("ssd_in_w", [D, 5152], F32, kind="ExternalInput").ap()
    ssd_out_w = nc.dram_tensor("ssd_out_w", [2048, D], F32, kind="ExternalInput").ap()
    sc_in_w = nc.dram_tensor("sc_in_w", [D, 3 * D], F32, kind="ExternalInput").ap()
    sc_out_w = nc.dram_tensor("sc_out_w", [D, D], F32, kind="ExternalInput").ap()
    out_d = nc.dram_tensor("outT", [128, KC, L], F32, kind="ExternalOutput").ap()

    with contextlib.ExitStack() as st:
        def sb(name, shape, dt):
            return st.enter_context(nc.sbuf_tensor(name, shape, dt))

        def ps(name, shape, dt=F32):
            return st.enter_context(nc.psum_tensor(name, shape, dt))

        S = Sched(nc)
        xT = sb("xT_sb", [128, KC, L], F32)
        hT = sb("hT_sb", [128, KC, L], BF16)
        Wreg = sb("Wreg", [128, 4 * 4096], BF16)
        sm = sb("sm_sb", [128, NS], F32)
        modT = [sb(f"modT{i}", [128, 48], F32) for i in range(2)]
        aM = [sb(f"aM{i}", [128, 8], F32) for i in range(2)]
        aF = [sb(f"aF{i}", [128, 8], F32) for i in range(2)]
        cond = sb("cond", [128, 8], F32)
        ones_f = sb("ones_f", [128, 128], F32)
        ident_b = sb("ident_b", [128, 128], BF16)
        ident_f = sb("ident_f", [128, 128], F32)
        triu = sb("triu", [128, 128], F32)
        maskneg = sb("maskneg", [128, 128], F32)
        sel8 = sb("sel8", [40, 8, 128], BF16)
        NAF = 10112
        NAB = 16384
        arF = sb("arenaF", [128, NAF], F32)
        arB = sb("arenaB", [128, NAB], BF16)
        _o = 6016
        nsq = [arF[:, _o + i * 512:_o + (i + 1) * 512] for i in range(2)]
        nrstd = [arF[:, _o + 1024 + i * 512:_o + 1536 + i * 512] for i in range(4)]
        ntmp = [arF[:, _o + 3072 + i * 512:_o + 3584 + i * 512] for i in range(2)]
        _a = 3584
        ada_st0 = [arB[:, i * 4096:(i + 1) * 4096].rearrange("p (k n) -> p k n", n=512) for i in range(4)]
        ada_st1 = [arF[:, _a + i * 1024:_a + (i + 1) * 1024].bitcast(BF16).rearrange("p (k n) -> p k n", n=256)
                   for i in range(2)]

        class Carver:
            def __init__(self, t, n):
                self.t, self.n, self.o = t, n, 0

            def reset(self):
                self.o = 0

            def get(self, n):
                assert self.o + n <= self.n, (self.o, n, self.n)
                ap = self.t[:, self.o:self.o + n]
                self.o += n
                return ap

        cF = Carver(arF, NAF)
        cB = Carver(arB, NAB)
        pbank = [ps(f"pb{i}", [128, 512], F32) for i in range(7)]
        pbf = [ps(f"pbf{i}", [128, 1024], BF16) for i in range(1)]

        def xk(kc, t8):
            return f"x{kc}_{t8}"

        def xkeys512(kc, tt):
            return [xk(kc, 2 * tt), xk(kc, 2 * tt + 1)]

        def hk(kc, tt):
            return f"h{kc}_{tt}"

        sc = lambda off, j: sm[:, off + j: off + j + 1]

        S.dma("sp", sm[:], sm_d, "sm", writes=["sm"])
        for kc in range(KC):
            S.dma("sp", xT[:, kc, :], xT_d[:, kc, :], f"xin{kc}", writes=[xk(kc, t) for t in range(8)])
        S.op("dve", lambda e: e.memset(ones_f[:], 1.0), writes=["ones_f"])
        S.op("pool", lambda e: e.memset(ident_f[:], 0.0), writes=["ident_f"])
        S.op("pool", lambda e: e.affine_select(out=ident_f[:], in_=ident_f[:], pattern=[[-1, 128]],
                                               compare_op=ALU.not_equal, fill=1.0, base=0, channel_multiplier=1),
             reads=["ident_f"], writes=["ident_f"])
        S.op("dve", lambda e: e.tensor_copy(out=ident_b[:], in_=ident_f[:]), reads=["ident_f"], writes=["ident_b"])
        S.op("pool", lambda e: e.memset(triu[:], 1.0), writes=["triu"])
        S.op("pool", lambda e: e.affine_select(out=triu[:], in_=triu[:], pattern=[[1, 128]],
                                               compare_op=ALU.is_ge, fill=0.0, base=0, channel_multiplier=-1),
             reads=["triu"], writes=["triu"])
        S.op("pool", lambda e: e.memset(maskneg[:], 0.0), writes=["maskneg"])
        S.op("pool", lambda e: e.affine_select(out=maskneg[:], in_=maskneg[:], pattern=[[1, 128]],
                                               compare_op=ALU.is_ge, fill=-30000.0, base=0, channel_multiplier=-1),
             reads=["maskneg"], writes=["maskneg"])
        S.op("pool", lambda e: e.memset(sel8[:], 0.0), writes=["sel8"])
        for h in range(8):
            S.op("pool", lambda e, h=h: e.affine_select(out=sel8[:, h, :], in_=sel8[:, h, :], pattern=[[0, 128]],
                                                        compare_op=ALU.not_equal, fill=1.0, base=-h,
                                                        channel_multiplier=1),
                 reads=["sel8"], writes=["sel8"])
            S.op("pool", lambda e, h=h: e.affine_select(out=sel8[:, h, :], in_=sel8[:, h, :], pattern=[[0, 128]],
                                                        compare_op=ALU.not_equal, fill=1.0, base=-(32 + h),
                                                        channel_multiplier=1),
                 reads=["sel8"], writes=["sel8"])
        S.op("act", lambda e: e.activation(out=cond[:], in_=sm[:, _SM["c"]:_SM["c"] + 8], func=AF.Silu),
             reads=["sm"], writes=["cond"])
        S.op("dve", lambda e: e.tensor_scalar(out=sm[:, _SM["convw"]:_SM["convw"] + 120],
                                              in0=sm[:, _SM["convw"]:_SM["convw"] + 120], scalar1=0.5, scalar2=None,
                                              op0=ALU.mult), reads=["sm"], writes=["sm"])

        ada_state = {"n": 0}
        cond_b = sb("cond_b", [128, 8], BF16)
        S.op("dve", lambda e: e.tensor_copy(out=cond_b[:], in_=cond[:]), reads=["cond"], writes=["cond_b"])

        def ada_piece(i, j0, ncol, stages, pmod):
            n = ada_state["n"]
            ada_state["n"] += 1
            slot = n % len(stages)
            stg = stages[slot]
            S.dma("pool", stg[:, :, 0:ncol], ada_w[i][:, j0 * 128:j0 * 128 + ncol].rearrange("(kc p) n -> p kc n", p=128),
                  f"ada{slot}", writes=[f"adast{slot}"])
            for m in range(ncol // 128):
                j = j0 + m
                for kc in range(KC):
                    S.op("pe", lambda e, j=j, m=m, kc=kc: e.matmul(pmod[:, j:j + 1], lhsT=stg[:, kc, m * 128:(m + 1) * 128],
                                                                    rhs=cond_b[:, kc:kc + 1], start=(kc == 0), stop=(kc == 7)),
                         reads=[f"adast{slot}", "cond_b"], writes=["pb5"])

        def ada_finish(i, pmod):
            S.op("dve", lambda e: e.tensor_tensor(out=modT[i][:], in0=pmod[:, 0:48],
                                                  in1=sm[:, _SM["adab"] + 48 * i:_SM["adab"] + 48 * i + 48], op=ALU.add),
                 reads=["pb5", "sm"], writes=[f"modT{i}"])
            for (dst, goff, scol, nm) in ((aM[i], _SM["mixg"] + 8 * i, 8, "aM"), (aF[i], _SM["mlpg"] + 8 * i, 32, "aF")):
                S.op("dve", lambda e, dst=dst, goff=goff, scol=scol: e.scalar_tensor_tensor(
                    out=dst[:], in0=modT[i][:, scol:scol + 8], scalar=1.0, in1=sm[:, goff:goff + 8],
                    op0=ALU.add, op1=ALU.mult), reads=[f"modT{i}", "sm"], writes=[f"{nm}{i}"])

        def norm_stats(tt, pst_ap, pst_key, ndiv):
            tsl = slice(tt * 512, (tt + 1) * 512)
            for kc in range(KC):
                sq = nsq[kc % 2]
                S.op("act", lambda e, sq=sq, kc=kc: e.activation(out=sq[:], in_=xT[:, kc, tsl], func=AF.Square),
                     reads=xkeys512(kc, tt), writes=[f"nsq{kc % 2}"])
                S.op("pe", lambda e, sq=sq, kc=kc: e.matmul(pst_ap, lhsT=ones_f[:], rhs=sq[:], start=(kc == 0), stop=(kc == 7)),
                     reads=[f"nsq{kc % 2}", "ones_f"], writes=[pst_key])
            S.op("act", lambda e: e.activation(out=nrstd[tt][:], in_=pst_ap, func=AF.Sqrt, bias=eps_t[:, 0:1], scale=1.0 / ndiv),
                 reads=[pst_key, "eps"], writes=[f"nrstd{tt}"])
            S.op("dve", lambda e: e.reciprocal(out=nrstd[tt][:], in_=nrstd[tt][:]), reads=[f"nrstd{tt}"], writes=[f"nrstd{tt}"])

        eps_t = sb("eps_t", [128, 1], F32)
        S.op("dve", lambda e: e.memset(eps_t[:], EPS), writes=["eps"])
        eps4_t = sb("eps4_t", [128, 1], F32)
        S.op("dve", lambda e: e.memset(eps4_t[:], 4.0 * EPS), writes=["eps"])
        one_t = sb("one_t", [128, 1], F32)
        S.op("dve", lambda e: e.memset(one_t[:], 1.0), writes=["one"])

        def rmsnorm_mod(a_t, a_key, mod_t, mod_key, shcol, stats=True):
            if stats:
                for tt in range(4):
                    norm_stats(tt, pbank[tt % 2][:], f"pb{tt % 2}", float(D))
            for tt in range(4):
                rmsnorm_mod_tile(a_t, a_key, mod_t, mod_key, shcol, tt)

        def rmsnorm_mod_tile(a_t, a_key, mod_t, mod_key, shcol, tt):
            if True:
                tsl = slice(tt * 512, (tt + 1) * 512)
                for kc in range(KC):
                    tmp = ntmp[kc % 2]
                    S.op("dve", lambda e, tmp=tmp, kc=kc: e.scalar_tensor_tensor(
                        out=tmp[:], in0=xT[:, kc, tsl], scalar=a_t[:, kc:kc + 1], in1=nrstd[tt][:],
                        op0=ALU.mult, op1=ALU.mult),
                        reads=xkeys512(kc, tt) + [f"nrstd{tt}", a_key], writes=[f"ntmp{kc % 2}"])
                    S.op("act", lambda e, tmp=tmp, kc=kc: e.activation(
                        out=hT[:, kc, tsl], in_=tmp[:], func=AF.Identity,
                        bias=mod_t[:, shcol + kc:shcol + kc + 1], scale=1.0),
                        reads=[f"ntmp{kc % 2}", mod_key], writes=[hk(kc, tt)])

        def wslot(k):
            return Wreg[:, k * 4096:(k + 1) * 4096]

        wctr = {"n": 0}

        def load_wtile(src_ap_3d):
            k = wctr["n"] % 4
            wctr["n"] += 1
            dst = wslot(k).rearrange("p (a b) -> p a b", b=512)
            S.dma("pool", dst, src_ap_3d, f"w{k}", writes=[f"wslot{k}"])
            return dst, f"wslot{k}"

        def evac_x(psum_ap, pkey, gate_ap, gkey, dc, t8list, tsl):
            keys = [xk(dc, t) for t in t8list]
            S.op("dve", lambda e: e.scalar_tensor_tensor(out=xT[:, dc, tsl], in0=psum_ap, scalar=gate_ap,
                                                         in1=xT[:, dc, tsl], op0=ALU.mult, op1=ALU.add),
                 reads=[pkey, gkey] + keys, writes=keys)

        def mlp(i, extra=None):
            cF.reset()
            cB.reset()
            aT = cB.get(8 * L).rearrange("p (j t) -> p j t", t=L)
            rtmp = [cF.get(512) for _ in range(3)]
            nb = [0, 1, 2, 3, 4, 6] if extra is not None else [0, 1, 2, 3, 4, 5, 6]
            prot = Rot([(pbank[b][:], f"pb{b}") for b in nb])
            rrot = Rot(list(range(3)))
            sq_eng = Rot(["dve"])
            gate = lambda dc: modT[i][:, 40 + dc:41 + dc]
            for blk in range(4):
                wu = [load_wtile(mlp_up[i][:, blk * 1024 + hf * 512: blk * 1024 + (hf + 1) * 512]
                                 .rearrange("(kc p) n -> p kc n", p=128)) for hf in range(2)]
                wd = [load_wtile(mlp_down[i][blk * 1024:(blk + 1) * 1024, hf * 512:(hf + 1) * 512]
                                 .rearrange("(j p) n -> p j n", p=128)) for hf in range(2)]
                for j in range(8):
                    if extra is not None:
                        for _ in range(2):
                            nxt = next(extra, None)
                            if nxt is not None:
                                nxt()
                    wt, wkey = wu[j // 4]
                    col = (j % 4) * 128
                    for tt in range(4):
                        tsl = slice(tt * 512, (tt + 1) * 512)
                        pp, pkey = prot.next()
                        for kc in range(KC):
                            S.op("pe", lambda e, pp=pp, wt=wt, kc=kc, col=col, tsl=tsl: e.matmul(
                                pp, lhsT=wt[:, kc, col:col + 128], rhs=hT[:, kc, tsl], start=(kc == 0), stop=(kc == 7)),
                                reads=[wkey, hk(kc, tt)], writes=[pkey])
                        r = rrot.next()
                        S.op("act", lambda e, pp=pp, r=r: e.activation(out=rtmp[r], in_=pp, func=AF.Relu),
                             reads=[pkey], writes=[f"rtmp{r}"])
                        S.op(sq_eng.next(), lambda e, r=r, j=j, tsl=tsl: e.tensor_tensor(
                            out=aT[:, j, tsl], in0=rtmp[r], in1=rtmp[r], op=ALU.mult),
                            reads=[f"rtmp{r}"], writes=[f"aT{j}_{tt}"])
                order = [(dc, tt) for dc in range(8) for tt in range(4)] if blk < 3 else \
                        [(dc, tt) for tt in range(4) for dc in range(8)]
                for (dc, tt) in order:
                    wt, wkey = wd[dc // 4]
                    col = (dc % 4) * 128
                    if True:
                        tsl = slice(tt * 512, (tt + 1) * 512)
                        pp, pkey = prot.next()
                        for j in range(8):
                            S.op("pe", lambda e, pp=pp, wt=wt, j=j, col=col, tsl=tsl: e.matmul(
                                pp, lhsT=wt[:, j, col:col + 128], rhs=aT[:, j, tsl], start=(j == 0), stop=(j == 7)),
                                reads=[wkey, f"aT{j}_{tt}"], writes=[pkey])
                        evac_x(pp, pkey, gate(dc), f"modT{i}", dc, [2 * tt, 2 * tt + 1], tsl)

        def ssd():
            cF.reset()
            cB.reset()
            TT = 256
            v3 = lambda ap: ap.rearrange("p (c h) -> p c h", h=NH)
            dt_t = cF.get(512)
            nacs_t = cF.get(512)
            dd_t = cF.get(512)
            cd_t = cF.get(512)
            t_u = cF.get(512)
            t_a = cF.get(512)
            t_l = cF.get(512)
            expA = cF.get(32)
            raw = [cF.get(264) for _ in range(4)]
            acc = [cF.get(256) for _ in range(4)]
            _r0 = 4 * 512 + 512
            raw += [arF[:, _r0:_r0 + 264], arF[:, _r0 + 264:_r0 + 528]]
            acc += [arF[:, _r0 + 528:_r0 + 784], arF[:, _r0 + 784:_r0 + 1040]]
            tmpE = [cF.get(256) for _ in range(4)]
            EA = [cF.get(256) for _ in range(3)]
            prev = cF.get(512)
            yv = [cF.get(256) for _ in range(2)]
            yg = [cF.get(256) for _ in range(4)]
            ssq = cF.get(256)
            grt = cF.get(256)
            hist = cF.get(24).rearrange("p (q k) -> p q k", k=4)
            xbc = [[cB.get(256) for _ in range(6)] for _ in range(2)]
            Xdt = [cB.get(512) for _ in range(2)]
            Xdd = [cB.get(512) for _ in range(2)]
            Btok = cB.get(256)
            acs2 = cB.get(256)
            acs_t32 = cB.get(256)
            MT = [cB.get(256) for _ in range(4)]
            Cs = [cB.get(256) for _ in range(4)]
            prevb = [cB.get(512) for _ in range(2)]
            ynorm = [cB.get(256) for _ in range(4)]
            Wdt = cB.get(256).rearrange("p (k h) -> p k h", h=NH)
            maskb = cB.get(256)
            f32v = lambda n: cB.get(2 * n).bitcast(F32)
            zs = [[f32v(256) for _ in range(4)] for _ in range(2)]
            sq = [f32v(256) for _ in range(2)]
            grstd = f32v(256)
            Wg = Wreg[:, 0:10240].rearrange("p (k n) -> p k n", n=1280)
            Wo = Wreg[:, 12288:16384].rearrange("p (k n) -> p k n", n=1024)
            wgkey = lambda col: "wg_z" if col < 512 else ("wg_x" if col < 1024 else ("wg_B" if col < 1152 else "wg_C"))
            WO_KEYS = ["wslot3"]
            big = Rot([(pbank[0][:, 0:256], "pb0"), (pbank[1][:, 0:256], "pb1"),
                       (pbank[0][:, 256:512], "pb0"), (pbank[1][:, 256:512], "pb1")])
            pplain = Rot([(pbank[2][:, 0:256], "pb2"), (pbank[2][:, 256:512], "pb2")])
            pmask = Rot([(pbank[6][:, 0:256], "pb6"), (pbank[6][:, 256:512], "pb6")])
            psc, psc_k = pbank[3][:, 0:256], "pb3"
            pacsT, pacsT_k = pbank[3][0:8, 256:512], "pb3"
            pacsT32 = pbank[3][32:40, 256:512]
            pyr = Rot([(pbank[4], 0, "pb4"), (pbank[4], 256, "pb4")])
            pst, pst_k = pbank[5][:, :], "pb5"
            pxt0 = pbf[0][:, 0:512]
            pxt1 = pbank[6][:, 0:256].bitcast(BF16)
            pbtr, pbtr_k = pbf[0][:, 512:768], "pbf0"
            gate = lambda dc: modT[0][:, 16 + dc:17 + dc]

            S.dma("pool", Wdt, ssd_in_w[:, 5120:5152].rearrange("(kc p) n -> p kc n", p=128), "wdt", writes=["Wdt"])
            S.op("dve", lambda e: e.tensor_copy(out=maskb[:, 0:128], in_=maskneg[:]), reads=["maskneg"], writes=["maskb"])
            S.op("dve", lambda e: e.tensor_copy(out=maskb[:, 128:256], in_=maskneg[:]), reads=["maskneg"], writes=["maskb"])
            S.op("dve", lambda e: e.memset(acs2[0:64, :], 0.0), reads=["acs2"], writes=["acs2"])
            pdt = pbank[0][:, :]
            for c in range(16):
                for kc in range(KC):
                    S.op("pe", lambda e, c=c, kc=kc: e.matmul(pdt[:, c * 32:(c + 1) * 32],
                                                              lhsT=hT[:, kc, c * 128:(c + 1) * 128], rhs=Wdt[:, kc, :],
                                                              start=(kc == 0), stop=(kc == 7)),
                         reads=["Wdt", hk(kc, c // 4)], writes=["pb0"])
            dtb = sm[:, _SM["dtb"]:_SM["dtb"] + 32]
            S.op("dve", lambda e: e.tensor_tensor(out=v3(t_u), in0=v3(pdt), in1=dtb.unsqueeze(1).to_broadcast([128, 16, 32]),
                                                  op=ALU.add), reads=["pb0", "sm"], writes=["t_u"])
            S.op("dve", lambda e: e.scalar_tensor_tensor(out=t_a, in0=t_u, scalar=-1.0, in1=t_u, op0=ALU.mult, op1=ALU.min),
                 reads=["t_u"], writes=["t_a"])
            S.op("act", lambda e: e.activation(out=t_a, in_=t_a, func=AF.Exp), reads=["t_a"], writes=["t_a"])
            S.op("act", lambda e: e.activation(out=t_l, in_=t_a, func=AF.Ln, bias=one_t[:, 0:1], scale=1.0),
                 reads=["t_a", "one"], writes=["t_l"])
            S.op("dve", lambda e: e.scalar_tensor_tensor(out=dt_t, in0=t_u, scalar=0.0, in1=t_l, op0=ALU.max, op1=ALU.add),
                 reads=["t_u", "t_l"], writes=["dt"])
            S.op("act", lambda e: e.activation(out=expA, in_=sm[:, _SM["alog"]:_SM["alog"] + 32], func=AF.Exp),
                 reads=["sm"], writes=["expA"])
            dtA = t_u
            S.op("dve", lambda e: e.scalar_tensor_tensor(out=v3(dtA), in0=v3(dt_t), scalar=-1.0,
                                                         in1=expA.unsqueeze(1).to_broadcast([128, 16, 32]),
                                                         op0=ALU.mult, op1=ALU.mult),
                 reads=["dt", "expA", "t_u"], writes=["dtA"])
            pacs = pbank[1][:, :]
            plast = pbank[2][:, :]
            for c in range(16):
                S.op("pe", lambda e, c=c: e.matmul(pacs[:, c * 32:(c + 1) * 32], lhsT=triu[:], rhs=v3(dtA)[:, c, :],
                                                   start=True, stop=True), reads=["triu", "dtA"], writes=["pb1"])
            for c in range(16):
                S.op("pe", lambda e, c=c: e.matmul(plast[:, c * 32:(c + 1) * 32], lhsT=ones_f[:], rhs=v3(dtA)[:, c, :],
                                                   start=True, stop=True), reads=["ones_f", "dtA"], writes=["pb2"])
            S.op("act", lambda e: e.activation(out=nacs_t, in_=pacs, func=AF.Identity, scale=-1.0), reads=["pb1"], writes=["nacs"])
            S.op("act", lambda e: e.activation(out=cd_t, in_=plast, func=AF.Exp), reads=["pb2"], writes=["cd"])
            S.op("dve", lambda e: e.tensor_tensor(out=t_l, in0=plast, in1=nacs_t, op=ALU.add),
                 reads=["pb2", "nacs", "t_l"], writes=["t_l2"])
            S.op("act", lambda e: e.activation(out=t_l, in_=t_l, func=AF.Exp), reads=["t_l2"], writes=["t_l2"])
            S.op("dve", lambda e: e.tensor_tensor(out=dd_t, in0=t_l, in1=dt_t, op=ALU.mult),
                 reads=["t_l2", "dt"], writes=["dd"])

            S.barrier()

            def load_wg(g):
                S.dma("pool", Wg[:, :, 0:512], ssd_in_w[:, g * 512:(g + 1) * 512].rearrange("(kc p) n -> p kc n", p=128),
                      "wg0", writes=["wg_z"])
                S.dma("pool", Wg[:, :, 512:1024],
                      ssd_in_w[:, 2048 + g * 512:2048 + (g + 1) * 512].rearrange("(kc p) n -> p kc n", p=128),
                      "wg1", writes=["wg_x"])
                S.dma("pool", Wg[:, :, 1024:1152],
                      ssd_in_w[:, 4096 + g * 128:4096 + (g + 1) * 128].rearrange("(kc p) n -> p kc n", p=128),
                      "wg2", writes=["wg_B"])
                S.dma("pool", Wg[:, :, 1152:1280],
                      ssd_in_w[:, 4608 + g * 128:4608 + (g + 1) * 128].rearrange("(kc p) n -> p kc n", p=128),
                      "wg3", writes=["wg_C"])
                S.op("pool", lambda e: e.memset(hist, 0.0), reads=["hist"], writes=["hist"])

            def load_wo(g):
                S.dma("pool", Wo, ssd_out_w[g * 512:(g + 1) * 512, :].rearrange("(kc p) n -> p kc n", p=128),
                      "wo", writes=WO_KEYS)

            rawrot = Rot(list(range(6)))

            def a_tasks(g, t, par):
                tsl = slice(t * TT, (t + 1) * TT)
                htt = t // 2
                convch = [4 * g + q for q in range(4)] + [16 + g, 20 + g]
                wcol = [512 + q * 128 for q in range(4)] + [1024, 1152]

                def proj(col):
                    pp, pkey = big.next()
                    for kc in range(KC):
                        S.op("pe", lambda e, pp=pp, kc=kc, col=col: e.matmul(
                            pp, lhsT=Wg[:, kc, col:col + 128], rhs=hT[:, kc, tsl],
                            start=(kc == 0), stop=(kc == 7)), reads=[wgkey(col), hk(kc, htt)], writes=[pkey])
                    return pp, pkey

                def conv_pair(qs):
                    st = []
                    for q in qs:
                        pp, pkey = proj(wcol[q])
                        ri = rawrot.next()
                        rb, ab = raw[ri], acc[ri]
                        ch = convch[q]
                        wof = _SM["convw"] + ch * 4
                        S.op("pool", lambda e, rb=rb, q=q: e.tensor_copy(out=rb[:, 0:3], in_=hist[:, q, 0:3]),
                             reads=["hist", f"raw{ri}"], writes=[f"raw{ri}"])
                        S.op("act", lambda e, rb=rb, pp=pp: e.activation(out=rb[:, 3:259], in_=pp, func=AF.Copy),
                             reads=[pkey, f"raw{ri}"], writes=[f"raw{ri}"])
                        S.op("act", lambda e, ab=ab, pp=pp, wof=wof, ch=ch: e.activation(
                            out=ab, in_=pp, func=AF.Identity, scale=sm[:, wof + 3:wof + 4],
                            bias=sm[:, _SM["convb"] + ch:_SM["convb"] + ch + 1]),
                            reads=[pkey, "sm", f"acc{ri}"], writes=[f"acc{ri}"])
                        S.op("pool", lambda e, rb=rb, q=q: e.tensor_copy(out=hist[:, q, 0:3], in_=rb[:, 256:259]),
                             reads=[f"raw{ri}", "hist"], writes=["hist"])
                        st.append((q, ri, rb, ab, wof))
                    for k in (2, 1, 0):
                        for (q, ri, rb, ab, wof) in st:
                            S.op("dve", lambda e, ab=ab, rb=rb, wof=wof, k=k: e.scalar_tensor_tensor(
                                out=ab, in0=rb[:, k:k + 256], scalar=sm[:, wof + k:wof + k + 1], in1=ab,
                                op0=ALU.mult, op1=ALU.add), reads=[f"raw{ri}", "sm", f"acc{ri}"], writes=[f"acc{ri}"])
                    for (q, ri, rb, ab, wof) in st:
                        S.op("act", lambda e, ab=ab, rb=rb: e.activation(out=rb[:, 0:256], in_=ab, func=AF.Tanh),
                             reads=[f"acc{ri}", f"raw{ri}"], writes=[f"raw{ri}"])
                    for (q, ri, rb, ab, wof) in st:
                        S.op("dve", lambda e, ab=ab, rb=rb, q=q: e.scalar_tensor_tensor(
                            out=xbc[par][q], in0=rb[:, 0:256], scalar=1.0, in1=ab, op0=ALU.add, op1=ALU.mult),
                            reads=[f"acc{ri}", f"raw{ri}"], writes=[f"xbc{par}_{q}"])

                def z_pair(fcs):
                    for fc in fcs:
                        pz, pzk = proj(fc * 128)
                        S.op("act", lambda e, pz=pz, fc=fc: e.activation(out=zs[par][fc], in_=pz, func=AF.Tanh, scale=0.5),
                             reads=[pzk, f"zs{par}_{fc}"], writes=[f"zs{par}_{fc}"])
                        S.op("dve", lambda e, pz=pz, fc=fc: e.scalar_tensor_tensor(
                            out=zs[par][fc], in0=zs[par][fc], scalar=1.0, in1=pz, op0=ALU.add, op1=ALU.mult),
                            reads=[pzk, f"zs{par}_{fc}"], writes=[f"zs{par}_{fc}"])

                convs = [(lambda q=q: conv_pair([q])) for q in range(6)]
                zsl = [(lambda fc=fc: z_pair([fc])) for fc in range(4)]
                return convs, zsl

            def b_s2(g, t, par, step):
                xb = xbc[par]
                xkey = lambda q: f"xbc{par}_{q}"
                if step == 0:
                    for c2 in range(2):
                        csl = slice(c2 * 128, (c2 + 1) * 128)
                        S.op("pe", lambda e, csl=csl: e.matmul(psc[:, csl], lhsT=xb[4][:, csl], rhs=xb[5][:, csl],
                                                               start=True, stop=True),
                             reads=[xkey(4), xkey(5)], writes=[psc_k])
                    for c2 in range(2):
                        c = 2 * t + c2
                        S.op("pe", lambda e, c=c, c2=c2: e.matmul(pacsT[:, c2 * 128:(c2 + 1) * 128],
                                                                  lhsT=v3(dtA)[:, c, 8 * g:8 * g + 8], rhs=triu[:],
                                                                  start=True, stop=True),
                             reads=["dtA", "triu"], writes=[pacsT_k])
                    for c2 in range(2):
                        c = 2 * t + c2
                        S.op("pe", lambda e, c=c, c2=c2: e.matmul(pacsT32[:, c2 * 128:(c2 + 1) * 128],
                                                                  lhsT=v3(dtA)[:, c, 8 * g:8 * g + 8], rhs=triu[:],
                                                                  start=True, stop=True, tile_position=(0, 32)),
                             reads=["dtA", "triu"], writes=[pacsT_k])
                    S.op("act", lambda e: e.activation(out=acs2[0:8, :], in_=pacsT, func=AF.Copy),
                         reads=[pacsT_k, "acs2"], writes=["acs2"])
                    S.op("act", lambda e: e.activation(out=acs_t32[32:40, :], in_=pacsT32, func=AF.Copy),
                         reads=[pacsT_k, "acs_t32"], writes=["acs_t32"])
                    S.op("dve", lambda e: e.tensor_tensor(out=acs2[32:40, :], in0=pacsT32, in1=acs_t32[32:40, :],
                                                          op=ALU.subtract),
                         reads=[pacsT_k, "acs_t32", "acs2"], writes=["acs2"])
                    return
                c2 = step - 1
                c = 2 * t + c2
                csl = slice(c2 * 128, (c2 + 1) * 128)
                pxt, pxt_k = (pxt0, "pbf0") if c2 == 0 else (pxt1, "pb6")
                for q in range(4):
                    S.op("pe", lambda e, q=q: e.transpose(
                        out=pxt[:, q * 128:(q + 1) * 128], in_=xb[q][:, csl], identity=ident_b[:]),
                        reads=[xkey(q), "ident_b"], writes=[pxt_k])
                if c2 == 0:
                    for cc in range(2):
                        ccsl = slice(cc * 128, (cc + 1) * 128)
                        S.op("pe", lambda e, ccsl=ccsl: e.transpose(out=pbtr[:, ccsl], in_=xb[4][:, ccsl], identity=ident_b[:]),
                             reads=[xkey(4), "ident_b"], writes=[pbtr_k])
                S.op("dve", lambda e: e.tensor_tensor(
                    out=Xdt[c2].rearrange("p (h d) -> p h d", d=64), in0=pxt.rearrange("p (h d) -> p h d", d=64),
                    in1=v3(dt_t)[:, c, 8 * g:8 * g + 8].unsqueeze(2).to_broadcast([128, 8, 64]), op=ALU.mult),
                    reads=[pxt_k, "dt", f"Xdt{c2}"], writes=[f"Xdt{c2}"])
                S.op("dve", lambda e: e.tensor_tensor(
                    out=Xdd[c2].rearrange("p (h d) -> p h d", d=64), in0=pxt.rearrange("p (h d) -> p h d", d=64),
                    in1=v3(dd_t)[:, c, 8 * g:8 * g + 8].unsqueeze(2).to_broadcast([128, 8, 64]), op=ALU.mult),
                    reads=[pxt_k, "dd", f"Xdd{c2}"], writes=[f"Xdd{c2}"])
                if c2 == 0:
                    S.op("act", lambda e: e.activation(out=Btok, in_=pbtr, func=AF.Copy),
                         reads=[pbtr_k, "Btok"], writes=["Btok"])

            def b_s3a(g, t, c2):
                if True:
                    c = 2 * t + c2
                    csl = slice(c2 * 128, (c2 + 1) * 128)
                    S.op("act", lambda e, c2=c2: e.activation(out=prevb[c2], in_=prev, func=AF.Copy),
                         reads=["prev", f"prevb{c2}"], writes=[f"prevb{c2}"])
                    pst, pst_k = (pbank[4][:, :], "pb4") if c2 == 0 else (pbank[5][:, :], "pb5")
                    S.op("pe", lambda e, c2=c2, csl=csl, pst=pst: e.matmul(pst, lhsT=Btok[:, csl], rhs=Xdd[c2], start=True, stop=True),
                         reads=["Btok", f"Xdd{c2}"], writes=[pst_k])
                    S.op("pool", lambda e, c=c: e.tensor_tensor(
                        out=prev.rearrange("p (h d) -> p h d", d=64), in0=prev.rearrange("p (h d) -> p h d", d=64),
                        in1=v3(cd_t)[:, c, 8 * g:8 * g + 8].unsqueeze(2).to_broadcast([128, 8, 64]), op=ALU.mult),
                        reads=["prev", "cd"], writes=["prev"])
                    S.op("dve", lambda e, pst=pst: e.tensor_tensor(out=prev, in0=prev, in1=pst, op=ALU.add),
                         reads=["prev", pst_k], writes=["prev"])

            earot = Rot(list(range(3)))

            def b_head_pre(g, t, h, par):
                hh = 8 * g + h
                mi = h % 4
                xb5, xk5 = xbc[par][5], f"xbc{par}_5"
                pa, pak = pplain.next()
                pm, pmk = pmask.next()
                for (pt, ptk, msk) in ((pa, pak, False), (pm, pmk, True)):
                    S.op("pe", lambda e, pt=pt, h=h, msk=msk: e.matmul(pt, lhsT=sel8[:, h, :], rhs=acs2[0:40, :],
                                                                       start=True, stop=(not msk)),
                         reads=["sel8", "acs2"], writes=[ptk])
                    if msk:
                        S.op("pe", lambda e, pt=pt: e.matmul(pt, lhsT=ident_b[:], rhs=maskb, start=False, stop=True),
                             reads=["ident_b", "maskb"], writes=[ptk])
                ei = earot.next()
                ea = EA[ei]
                S.op("act", lambda e, ea=ea, pa=pa: e.activation(out=ea, in_=pa, func=AF.Exp),
                     reads=[pak, f"EA{ei}"], writes=[f"EA{ei}"])
                te = tmpE[mi]
                for c2 in range(2):
                    c = 2 * t + c2
                    csl = slice(c2 * 128, (c2 + 1) * 128)
                    S.op("act", lambda e, te=te, pm=pm, csl=csl, c=c, hh=hh: e.activation(
                        out=te[:, csl], in_=pm[:, csl], func=AF.Exp, bias=v3(nacs_t)[:, c, hh:hh + 1], scale=1.0),
                        reads=[pmk, "nacs", f"tmpE{mi}"], writes=[f"tmpE{mi}"])
                S.op("dve", lambda e, te=te, mi=mi: e.tensor_tensor(out=MT[mi], in0=te, in1=psc, op=ALU.mult),
                     reads=[f"tmpE{mi}", psc_k, f"MT{mi}"], writes=[f"MT{mi}"])
                S.op("pool", lambda e, ea=ea, mi=mi: e.tensor_tensor(out=Cs[mi], in0=xb5, in1=ea, op=ALU.mult),
                     reads=[f"EA{ei}", xk5, f"Cs{mi}"], writes=[f"Cs{mi}"])

            def b_head_y(g, t, fc, par):
                pyb, pyo, pyk = pyr.next()
                for c2 in range(2):
                    csl = slice(c2 * 128, (c2 + 1) * 128)
                    for hp in range(2):
                        h = 2 * fc + hp
                        mi = h % 4
                        outap = pyb[hp * 64:(hp + 1) * 64, pyo + c2 * 128:pyo + (c2 + 1) * 128]
                        tp = (0, 64) if hp == 1 else None
                        S.op("pe", lambda e, outap=outap, c2=c2, h=h, mi=mi, csl=csl, tp=tp: e.matmul(
                            outap, lhsT=Xdt[c2][:, h * 64:(h + 1) * 64], rhs=MT[mi][:, csl],
                            start=True, stop=False, tile_position=tp),
                            reads=[f"Xdt{c2}", f"MT{mi}"], writes=[pyk])
                        S.op("pe", lambda e, outap=outap, c2=c2, h=h, mi=mi, csl=csl, tp=tp: e.matmul(
                            outap, lhsT=prevb[c2][:, h * 64:(h + 1) * 64], rhs=Cs[mi][:, csl],
                            start=False, stop=True, tile_position=tp),
                            reads=[f"prevb{c2}", f"Cs{mi}"], writes=[pyk])
                pyap = pyb[:, pyo:pyo + 256]
                dcol = _SM["dfeat"] + 4 * g + fc
                yvb = yv[fc % 2]
                S.op("dve", lambda e, pyap=pyap, dcol=dcol, yvb=yvb: e.scalar_tensor_tensor(
                    out=yvb, in0=xbc[par][fc], scalar=sm[:, dcol:dcol + 1], in1=pyap, op0=ALU.mult, op1=ALU.add),
                    reads=[pyk, f"xbc{par}_{fc}", "sm", f"yv{fc % 2}"], writes=[f"yv{fc % 2}"])
                S.op("dve", lambda e, yvb=yvb: e.tensor_tensor(out=yg[fc], in0=yvb, in1=zs[par][fc], op=ALU.mult),
                     reads=[f"yv{fc % 2}", f"zs{par}_{fc}", f"yg{fc}"], writes=[f"yg{fc}"])
                if fc == 0:
                    S.op("act", lambda e: e.activation(out=ssq, in_=yg[0], func=AF.Square),
                         reads=["yg0", "ssq"], writes=["ssq"])
                else:
                    sqb = sq[fc % 2]
                    S.op("act", lambda e, sqb=sqb: e.activation(out=sqb, in_=yg[fc], func=AF.Square),
                         reads=[f"yg{fc}", f"sq{fc % 2}"], writes=[f"sq{fc % 2}"])
                    if fc < 3:
                        S.op("pool", lambda e, sqb=sqb: e.tensor_tensor(out=ssq, in0=ssq, in1=sqb, op=ALU.add),
                             reads=[f"sq{fc % 2}", "ssq"], writes=["ssq"])

            s4state = {}

            def b_s4(g, t):
                pn, pnk = big.next()
                S.op("pe", lambda e, pn=pn: e.matmul(pn, lhsT=ones_f[:], rhs=ssq, start=True, stop=False),
                     reads=["ones_f", "ssq"], writes=[pnk])
                S.op("pe", lambda e, pn=pn: e.matmul(pn, lhsT=ones_f[:], rhs=sq[1], start=False, stop=True),
                     reads=["ones_f", "sq1"], writes=[pnk])
                s4state["p"] = (pn, pnk, g)

            def b_s4e():
                pn, pnk, g = s4state.pop("p")
                S.op("act", lambda e, pn=pn: e.activation(out=grt, in_=pn, func=AF.Sqrt, bias=eps4_t[:, 0:1], scale=1.0 / 512.0),
                     reads=[pnk, "eps", "grt"], writes=["grt"])
                S.op("dve", lambda e: e.reciprocal(out=grstd, in_=grt), reads=["grt", "grstd"], writes=["grstd"])
                for fc in range(4):
                    ncol = _SM["nw"] + 4 * g + fc
                    S.op("dve", lambda e, fc=fc, ncol=ncol: e.scalar_tensor_tensor(
                        out=ynorm[fc], in0=yg[fc], scalar=sm[:, ncol:ncol + 1], in1=grstd, op0=ALU.mult, op1=ALU.mult),
                        reads=[f"yg{fc}", "sm", "grstd", f"ynorm{fc}"], writes=[f"ynorm{fc}"])

            def b_s4b(g, t, dc):
                tsl = slice(t * TT, (t + 1) * TT)
                if True:
                    pp, pkey = big.next()
                    for kc4 in range(4):
                        S.op("pe", lambda e, pp=pp, kc4=kc4, dc=dc: e.matmul(
                            pp, lhsT=Wo[:, kc4, dc * 128:(dc + 1) * 128], rhs=ynorm[kc4],
                            start=(kc4 == 0), stop=(kc4 == 3)), reads=WO_KEYS + [f"ynorm{kc4}"], writes=[pkey])
                    evac_x(pp, pkey, gate(dc), "modT0", dc, [t], tsl)

            iters = [(g, t) for g in range(4) for t in range(8)]
            load_wg(0)
            load_wo(0)
            cv0, zs0 = a_tasks(0, 0, 0)
            for task in cv0 + zs0:
                task()
            for idx, (g, t) in enumerate(iters):
                par = idx % 2
                convs, zsl, outs = [], [], []
                if idx + 1 < len(iters):
                    ng, nt = iters[idx + 1]
                    if nt == 0:
                        load_wg(ng)
                    convs, zsl = a_tasks(ng, nt, 1 - par)
                if idx > 0:
                    pg, pt = iters[idx - 1]
                    outs = [(lambda dc=dc, pg=pg, pt=pt: b_s4b(pg, pt, dc)) for dc in range(8)]
                convs, zsl, outs = iter(convs), iter(zsl), iter(outs)

                def fill(kinds):
                    for kd in kinds:
                        f = next({"c": convs, "z": zsl, "o": outs}[kd], None)
                        if f is not None:
                            f()
                if t == 0:
                    S.op("pool", lambda e: e.memset(prev, 0.0), reads=["prev"], writes=["prev"])
                b_s2(g, t, par, 0)
                if "p" in s4state:
                    b_s4e()
                fill("c")
                b_s2(g, t, par, 1)
                fill("cz")
                b_s2(g, t, par, 2)
                fill("cz")
                b_s3a(g, t, 0)
                fill("czo")
                b_s3a(g, t, 1)
                fill("czo")
                plan = ["o", "c", "o", "o", "o", "o", "o", ""]
                for h in range(8):
                    b_head_pre(g, t, h, par)
                    if h >= 2 and h % 2 == 0:
                        b_head_y(g, t, h // 2 - 1, par)
                    fill(plan[h])
                b_head_y(g, t, 3, par)
                fill("cccccczzzz")
                b_s4(g, t)
                fill("oooooooo")
                if idx > 0 and iters[idx - 1][1] == 7:
                    load_wo(g)
            b_s4e()
            for dc in range(8):
                b_s4b(iters[-1][0], iters[-1][1], dc)

        sc_wts = {}

        def load_sc(j):
            k = wctr["n"] % 4
            wctr["n"] += 1
            wt = wslot(k)[:, 0:3072].rearrange("p (kc th f) -> p kc th f", th=3, f=128)
            wkey = f"wslot{k}"
            for th in range(3):
                S.dma("pool", wt[:, :, th, :],
                      sc_in_w[:, th * 1024 + j * 128: th * 1024 + (j + 1) * 128].rearrange("(kc p) f -> p kc f", p=128),
                      f"w{k}", writes=[wkey])
            sc_wts[j] = (wt, wkey)

        def shortconv():
            cF.reset()
            cB.reset()
            yT = cB.get(8 * L).rearrange("p (j t) -> p j t", t=L)
            xv = [cF.get(512) for _ in range(2)]
            ub = [cF.get(516) for _ in range(2)]
            vb = [cF.get(512) for _ in range(2)]
            prot = Rot([(pbank[b][:], f"pb{b}") for b in range(6)])
            gate = lambda dc: modT[1][:, 16 + dc:17 + dc]
            n = 0
            wts = sc_wts
            for j in range(3):
                if j not in wts:
                    load_sc(j)
            wo = []
            for j in range(8):
                if j + 3 < 8:
                    load_sc(j + 3)
                elif len(wo) < 2:
                    hf = len(wo)
                    wo.append(load_wtile(sc_out_w[:, hf * 512:(hf + 1) * 512].rearrange("(kc p) n -> p kc n", p=128)))
                wt, wkey = wts[j]
                for tt in range(4):
                    tsl = slice(tt * 512, (tt + 1) * 512)
                    pps = []
                    for th in range(3):
                        pp, pkey = prot.next()
                        pps.append((pp, pkey))
                        for kc in range(KC):
                            S.op("pe", lambda e, pp=pp, kc=kc, th=th, wt=wt, tsl=tsl: e.matmul(
                                pp, lhsT=wt[:, kc, th, :], rhs=hT[:, kc, tsl], start=(kc == 0), stop=(kc == 7)),
                                reads=[wkey, hk(kc, tt)], writes=[pkey])
                    (pB, pBk), (pC, pCk), (pX, pXk) = pps
                    bi = n % 2
                    n += 1
                    u, uprev, v, xvb = ub[bi], ub[1 - bi], vb[bi], xv[bi]
                    S.op("act", lambda e, xvb=xvb, pX=pX: e.activation(out=xvb, in_=pX, func=AF.Copy),
                         reads=[pXk, f"xv{bi}"], writes=[f"xv{bi}"])
                    if tt == 0:
                        S.op("dve", lambda e, u=u: e.memset(u[:, 0:2], 0.0), reads=[f"u{bi}"], writes=[f"u{bi}"])
                    else:
                        S.op("act", lambda e, u=u, uprev=uprev: e.activation(out=u[:, 0:2], in_=uprev[:, 512:514], func=AF.Copy),
                             reads=[f"u{1 - bi}", f"u{bi}"], writes=[f"u{bi}"])
                    S.op("dve", lambda e, u=u, pC=pC, xvb=xvb: e.tensor_tensor(out=u[:, 2:514], in0=pC, in1=xvb, op=ALU.mult),
                         reads=[pCk, f"xv{bi}", f"u{bi}"], writes=[f"u{bi}"])
                    wof = _SM["scw"] + j * 3
                    S.op("act", lambda e, u=u, v=v, wof=wof: e.activation(
                        out=v, in_=u[:, 0:512], func=AF.Identity, scale=sm[:, wof:wof + 1]),
                        reads=[f"u{bi}", "sm", f"v{bi}"], writes=[f"v{bi}"])
                    for kk in (1, 2):
                        S.op("dve", lambda e, u=u, v=v, wof=wof, kk=kk: e.scalar_tensor_tensor(
                            out=v, in0=u[:, kk:kk + 512], scalar=sm[:, wof + kk:wof + kk + 1], in1=v,
                            op0=ALU.mult, op1=ALU.add), reads=[f"u{bi}", "sm", f"v{bi}"], writes=[f"v{bi}"])
                    S.op("dve", lambda e, v=v, pB=pB, j=j, tsl=tsl: e.tensor_tensor(out=yT[:, j, tsl], in0=pB, in1=v, op=ALU.mult),
                         reads=[pBk, f"v{bi}"], writes=[f"yT{j}_{tt}"])
            for (dc, tt) in [(dc, tt) for tt in range(4) for dc in range(8)]:
                wt, wkey = wo[dc // 4]
                col = (dc % 4) * 128
                if True:
                    tsl = slice(tt * 512, (tt + 1) * 512)
                    pp, pkey = prot.next()
                    for j in range(8):
                        S.op("pe", lambda e, pp=pp, wt=wt, j=j, col=col, tsl=tsl: e.matmul(
                            pp, lhsT=wt[:, j, col:col + 128], rhs=yT[:, j, tsl], start=(j == 0), stop=(j == 7)),
                            reads=[wkey, f"yT{j}_{tt}"], writes=[pkey])
                    evac_x(pp, pkey, gate(dc), "modT1", dc, [2 * tt, 2 * tt + 1], tsl)

        def dump_x():
            for kc in range(KC):
                S.dma("sp", out_d[:, kc, :], xT[:, kc, :], "out", reads=[xk(kc, t) for t in range(8)])

        def dump_h():
            cF.reset()
            tmp = cF.get(2048)
            for kc in range(KC):
                S.op("dve", lambda e, kc=kc: e.tensor_copy(out=tmp, in_=hT[:, kc, :]),
                     reads=[hk(kc, tt) for tt in range(4)] + ["dumptmp"], writes=["dumptmp"])
                S.dma("sp", out_d[:, kc, :], tmp, "out", reads=["dumptmp"])

        def final_norm():
            cF.reset()
            cF.get(1536)
            ob = [cF.get(512) for _ in range(3)]
            ob += [arF[:, 4608 + i * 512:4608 + (i + 1) * 512] for i in range(2)]
            NOB = len(ob)
            n = 0
            for tt in range(4):
                norm_stats(tt, pbank[tt % 2][:], f"pb{tt % 2}", float(D))
            for tt in range(4):
                tsl = slice(tt * 512, (tt + 1) * 512)
                for kc in range(KC):
                    oi = n % NOB
                    n += 1
                    o = ob[oi]
                    gcol = _SM["fing"] + kc
                    S.op("dve", lambda e, o=o, kc=kc, gcol=gcol, tsl=tsl, tt=tt: e.scalar_tensor_tensor(
                        out=o, in0=xT[:, kc, tsl], scalar=sm[:, gcol:gcol + 1], in1=nrstd[tt][:], op0=ALU.mult, op1=ALU.mult),
                        reads=xkeys512(kc, tt) + [f"nrstd{tt}", "sm", f"ob{oi}"], writes=[f"ob{oi}"])
                    S.dma("sp", out_d[:, kc, tsl], o, f"out{oi}", reads=[f"ob{oi}"])

        pmod = pbank[5]
        for tt in range(4):
            norm_stats(tt, pbank[tt % 2][:], f"pb{tt % 2}", float(D))
        for jj in range(12):
            ada_piece(0, jj * 4, 512, ada_st0, pmod)
        ada_finish(0, pmod)
        rmsnorm_mod(aM[0], "aM0", modT[0], "modT0", 0, stats=False)
        done = False
        if dbg_stop == "norm0":
            dump_h()
            done = True
        if not done:
            S.barrier()
            ssd()
            if dbg_stop == "mix0":
                dump_x()
                done = True
        if not done:
            S.barrier()
            rmsnorm_mod(aF[0], "aF0", modT[0], "modT0", 24)
            def ada1_gen():
                for jj in range(24):
                    yield (lambda jj=jj: ada_piece(1, jj * 2, 256, ada_st1, pmod))
            gen = ada1_gen()
            mlp(0, extra=gen)
            for rest in gen:
                rest()
            ada_finish(1, pmod)
            if dbg_stop == "mlp0":
                dump_x()
                done = True
        if not done:
            for j in range(3):
                load_sc(j)
            rmsnorm_mod(aM[1], "aM1", modT[1], "modT1", 0)
            S.barrier()
            shortconv()
            if dbg_stop == "mix1":
                dump_x()
                done = True
        if not done:
            rmsnorm_mod(aF[1], "aF1", modT[1], "modT1", 24)
            S.barrier()
            mlp(1)
            if dbg_stop == "mlp1":
                dump_x()
                done = True
        if not done:
            final_norm()
        S.emit(final_waits=[("sp", nm) for nm in S.dma_sems if nm.startswith("out")])
    return nc


def _chunks(v):
    v = np.asarray(v, np.float32)
    return np.ascontiguousarray(v.reshape(-1, 128).T)


def make_smalls(b, c, ada_b, mix_norm_w, mlp_norm_w, ssd_conv_w, ssd_conv_b, ssd_dt_bias, ssd_A_log,
                ssd_D, ssd_norm_w, sc_conv_w, final_norm_w):
    sm = np.zeros((128, NS), np.float32)
    sm[:, _SM["c"]:_SM["c"] + 8] = _chunks(c[b])
    for i in range(2):
        sm[:, _SM["adab"] + 48 * i:_SM["adab"] + 48 * (i + 1)] = _chunks(ada_b[i])
        sm[:, _SM["mixg"] + 8 * i:_SM["mixg"] + 8 * (i + 1)] = _chunks(mix_norm_w[i])
        sm[:, _SM["mlpg"] + 8 * i:_SM["mlpg"] + 8 * (i + 1)] = _chunks(mlp_norm_w[i])
    sm[:, _SM["fing"]:_SM["fing"] + 8] = _chunks(final_norm_w)
    cw = np.asarray(ssd_conv_w[0], np.float32)
    for k in range(4):
        sm[:, _SM["convw"] + k:_SM["convw"] + 96:4] = _chunks(cw[k])
    sm[:, _SM["convb"]:_SM["convb"] + 24] = _chunks(ssd_conv_b[0])
    sm[:, _SM["dfeat"]:_SM["dfeat"] + 16] = _chunks(np.repeat(np.asarray(ssd_D[0], np.float32), 64))
    sm[:, _SM["nw"]:_SM["nw"] + 16] = _chunks(ssd_norm_w[0])
    sm[:, _SM["dtb"]:_SM["dtb"] + 32] = np.broadcast_to(np.asarray(ssd_dt_bias[0], np.float32)[None, :], (128, 32))
    sm[:, _SM["alog"]:_SM["alog"] + 32] = np.broadcast_to(np.asarray(ssd_A_log[0], np.float32)[None, :], (128, 32))
    sw = np.asarray(sc_conv_w[0], np.float32)
    for k in range(3):
        sm[:, _SM["scw"] + k:_SM["scw"] + 24:3] = _chunks(sw[k])
    return sm


_NC_CACHE = {}


def _get_nc(dbg_stop=None):
    if dbg_stop not in _NC_CACHE:
        _NC_CACHE[dbg_stop] = build_program(dbg_stop)
    return _NC_CACHE[dbg_stop]


def kernel(x, c, ada_w, ada_b, mix_norm_w, mlp_norm_w, mlp_up, mlp_down,
           ssd_in_w, ssd_conv_w, ssd_conv_b, ssd_dt_bias, ssd_A_log, ssd_D,
           ssd_norm_w, ssd_out_w, sc_in_w, sc_conv_w, sc_out_w, final_norm_w, _dbg_stop=None):
    n = 8
    x = np.asarray(x, np.float32)
    f = lambda a: np.ascontiguousarray(np.asarray(a, np.float32))
    shared = {
        "ada_w": f(ada_w), "mlp_up": f(mlp_up), "mlp_down": f(mlp_down),
        "ssd_in_w": f(ssd_in_w[0]), "ssd_out_w": f(ssd_out_w[0]),
        "sc_in_w": f(sc_in_w[0]), "sc_out_w": f(sc_out_w[0]),
    }
    in_maps = []
    for b in range(n):
        xT = np.ascontiguousarray(x[b].reshape(L, KC, 128).transpose(2, 1, 0))
        smalls = make_smalls(b, c, ada_b, mix_norm_w, mlp_norm_w, ssd_conv_w, ssd_conv_b, ssd_dt_bias,
                             ssd_A_log, ssd_D, ssd_norm_w, sc_conv_w, final_norm_w)
        m = {"xT": xT, "smalls": smalls}
        m.update(shared)
        in_maps.append(m)
    nc = _get_nc(_dbg_stop)
    res = run_bass_kernel_spmd(nc, in_maps, core_ids=list(range(n)))
    outs = []
    for b in range(n):
        oT = np.asarray(res.results[b]["outT"], np.float32)
        outs.append(oT.transpose(2, 1, 0).reshape(L, D))
    return np.stack(outs, axis=0).astype(np.float32)
```

```python
import contextlib
import numpy as np
import concourse.bass as bass
import concourse.mybir as mybir
from concourse.bass_utils import run_bass_kernel_spmd

F32 = mybir.dt.float32
BF16 = mybir.dt.bfloat16
AF = mybir.ActivationFunctionType
ALU = mybir.AluOpType
AX = mybir.AxisListType

D = 1024
L = 2048
KC = 8
DFF = 4096
NH = 32
EPS = 1e-5
ENGS = ("pe", "act", "dve", "pool", "sp")

_SM = {}
_off = 0
for _n, _w in (("c", 8), ("adab", 96), ("mixg", 16), ("mlpg", 16), ("fing", 8), ("convw", 96),
               ("convb", 24), ("dfeat", 16), ("nw", 16), ("dtb", 32), ("alog", 32), ("scw", 24)):
    _SM[_n] = _off
    _off += _w
NS = _off


class _Op:
    __slots__ = ("eng", "fn", "idx", "deps", "dma_sem", "dma_cnt", "ms", "is_dma", "done")

    def __init__(self, eng, fn, idx):
        self.eng = eng
        self.fn = fn
        self.idx = idx
        self.deps = []
        self.dma_sem = None
        self.dma_cnt = 0
        self.ms = 0
        self.is_dma = False
        self.done = False


class Sched:
    def __init__(self, nc):
        self.nc = nc
        self.ops = {e: [] for e in ENGS}
        self.last_w = {}
        self.readers = {}
        self.dma_sems = {}
        self.SAME_ENG_DIST = 10 ** 9

    def _track(self, rec, reads, writes):
        deps = []
        for k in reads:
            w = self.last_w.get(k)
            if w is not None and w is not rec:
                deps.append(w)
        for k in writes:
            w = self.last_w.get(k)
            if w is not None and w is not rec:
                deps.append(w)
            for r in self.readers.get(k, ()):
                if r is not rec:
                    deps.append(r)
        for k in reads:
            self.readers.setdefault(k, []).append(rec)
        for k in writes:
            self.last_w[k] = rec
            self.readers[k] = []
        best = {}
        for d in deps:
            if d.is_dma:
                key = ("dma", d.dma_sem)
                if key not in best or best[key].dma_cnt < d.dma_cnt:
                    best[key] = d
            else:
                key = d.eng
                if key not in best or best[key].idx < d.idx:
                    best[key] = d
        rec.deps = list(best.values())

    def op(self, eng, fn, reads=(), writes=()):
        pw = [k for k in reads if k.startswith("pb")]
        if pw:
            reads = [k for k in reads if not k.startswith("pb")]
            writes = list(writes) + pw
        rec = _Op(eng, fn, len(self.ops[eng]))
        self._track(rec, reads, writes)
        self.ops[eng].append(rec)
        return rec

    def dma(self, eng, out, in_, sem, reads=(), writes=()):
        def fn(e, out=out, in_=in_):
            return e.dma_start(out=out, in_=in_)
        rec = _Op(eng, fn, len(self.ops[eng]))
        rec.is_dma = True
        ent = self.dma_sems.setdefault(sem, [None, 0])
        ent[1] += 16
        rec.dma_sem = sem
        rec.dma_cnt = ent[1]
        self._track(rec, reads, writes)
        self.ops[eng].append(rec)
        return rec

    def barrier(self):
        lasts = []
        for e in ENGS:
            real = [r for r in self.ops[e] if r.fn is not None]
            if real:
                lasts.append(real[-1])
        dl = {}
        for e in ENGS:
            for r in self.ops[e]:
                if r.is_dma:
                    dl[r.dma_sem] = r
        for e in ENGS:
            rec = _Op(e, None, len(self.ops[e]))
            rec.deps = [d for d in lasts if d.eng != e and not d.is_dma] + list(dl.values())
            self.ops[e].append(rec)

    def _needs_wait(self, rec, d):
        if d.is_dma:
            return True
        if d.eng != rec.eng:
            return True
        if rec.is_dma:
            return True
        if rec.eng == "pe":
            return False
        if rec.eng == "pool":
            return True
        return (rec.idx - d.idx) < self.SAME_ENG_DIST

    def check(self):
        ptr = {e: 0 for e in ENGS}
        for e in ENGS:
            for r in self.ops[e]:
                r.done = False
        progress = True
        while progress:
            progress = False
            for e in ENGS:
                ops = self.ops[e]
                while ptr[e] < len(ops):
                    r = ops[ptr[e]]
                    if all(d.done for d in r.deps):
                        r.done = True
                        ptr[e] += 1
                        progress = True
                    else:
                        break
        for e in ENGS:
            if ptr[e] < len(self.ops[e]):
                raise RuntimeError(f"schedule deadlock on {e} at op {ptr[e]}")

    def emit(self, final_waits=()):
        nc = self.nc
        self.check()
        for e in ENGS:
            for rec in self.ops[e]:
                for d in rec.deps:
                    if not d.is_dma and self._needs_wait(rec, d):
                        d.ms = -1
        for e in ENGS:
            n = 0
            for rec in self.ops[e]:
                if rec.ms == -1:
                    n += 1
                    rec.ms = n
        with contextlib.ExitStack() as st:
            esem = {e: st.enter_context(nc.semaphore("s_" + e)) for e in ENGS}
            for name in self.dma_sems:
                self.dma_sems[name][0] = st.enter_context(nc.semaphore("d_" + name))
            block = st.enter_context(nc.Block())
            bmap = {"pe": block.tensor, "act": block.scalar, "dve": block.vector,
                    "pool": block.gpsimd, "sp": block.sync}
            for e in ENGS:
                ops = self.ops[e]
                fin = [fw for fw in final_waits if fw[0] == e]
                if not ops and not fin:
                    continue

                def body(eng, e=e, ops=ops, fin=fin):
                    waited = {}
                    for rec in ops:
                        for d in rec.deps:
                            if not self._needs_wait(rec, d):
                                continue
                            if d.is_dma:
                                key, val, sem = ("d", d.dma_sem), d.dma_cnt, self.dma_sems[d.dma_sem][0]
                            else:
                                key, val, sem = ("e", d.eng), d.ms, esem[d.eng]
                            if waited.get(key, 0) >= val:
                                continue
                            waited[key] = val
                            eng.wait_ge(sem, val)
                        if rec.fn is None:
                            continue
                        ins = rec.fn(eng)
                        if rec.is_dma:
                            ins.then_inc(self.dma_sems[rec.dma_sem][0], 16)
                        elif rec.ms > 0:
                            ins.then_inc(esem[e], 1)
                    for (_, name) in fin:
                        eng.wait_ge(self.dma_sems[name][0], self.dma_sems[name][1])
                bmap[e](body)


class Rot:
    def __init__(self, items):
        self.items = items
        self.i = 0

    def next(self):
        it = self.items[self.i % len(self.items)]
        self.i += 1
        return it


def build_program(dbg_stop=None):
    nc = bass.Bass("TRN2", target_bir_lowering=False)
    xT_d = nc.dram_tensor("xT", [128, KC, L], F32, kind="ExternalInput").ap()
    sm_d = nc.dram_tensor("smalls", [128, NS], F32, kind="ExternalInput").ap()
    ada_w = nc.dram_tensor("ada_w", [2, D, 6 * D], F32, kind="ExternalInput").ap()
    mlp_up = nc.dram_tensor("mlp_up", [2, D, DFF], F32, kind="ExternalInput").ap()
    mlp_down = nc.dram_tensor("mlp_down", [2, DFF, D], F32, kind="ExternalInput").ap()
    ssd_in_w = nc.dram_tensor("ssd_in_w", [D, 5152], F32, kind="ExternalInput").ap()
    ssd_out_w = nc.dram_tensor("ssd_out_w", [2048, D], F32, kind="ExternalInput").ap()
    sc_in_w = nc.dram_tensor("sc_in_w", [D, 3 * D], F32, kind="ExternalInput").ap()
    sc_out_w = nc.dram_tensor("sc_out_w", [D, D], F32, kind="ExternalInput").ap()
    out_d = nc.dram_tensor("outT", [128, KC, L], F32, kind="ExternalOutput").ap()

    with contextlib.ExitStack() as st:
        def sb(name, shape, dt):
            return st.enter_context(nc.sbuf_tensor(name, shape, dt))

        def ps(name, shape, dt=F32):
            return st.enter_context(nc.psum_tensor(name, shape, dt))

        S = Sched(nc)
        xT = sb("xT_sb", [128, KC, L], F32)
        hT = sb("hT_sb", [128, KC, L], BF16)
        Wreg = sb("Wreg", [128, 4 * 4096], BF16)
        sm = sb("sm_sb", [128, NS], F32)
        modT = [sb(f"modT{i}", [128, 48], F32) for i in range(2)]
        aM = [sb(f"aM{i}", [128, 8], F32) for i in range(2)]
        aF = [sb(f"aF{i}", [128, 8], F32) for i in range(2)]
        cond = sb("cond", [128, 8], F32)
        ones_f = sb("ones_f", [128, 128], F32)
        ident_b = sb("ident_b", [128, 128], BF16)
        ident_f = sb("ident_f", [128, 128], F32)
        triu = sb("triu", [128, 128], F32)
        maskneg = sb("maskneg", [128, 128], F32)
        sel8 = sb("sel8", [40, 8, 128], BF16)
        NAF = 10112
        NAB = 16384
        arF = sb("arenaF", [128, NAF], F32)
        arB = sb("arenaB", [128, NAB], BF16)
        _o = 6016
        nsq = [arF[:, _o + i * 512:_o + (i + 1) * 512] for i in range(2)]
        nrstd = [arF[:, _o + 1024 + i * 512:_o + 1536 + i * 512] for i in range(4)]
        ntmp = [arF[:, _o + 3072 + i * 512:_o + 3584 + i * 512] for i in range(2)]
        _a = 3584
        ada_st0 = [arB[:, i * 4096:(i + 1) * 4096].rearrange("p (k n) -> p k n", n=512) for i in range(4)]
        ada_st1 = [arF[:, _a + i * 1024:_a + (i + 1) * 1024].bitcast(BF16).rearrange("p (k n) -> p k n", n=256)
                   for i in range(2)]

        class Carver:
            def __init__(self, t, n):
                self.t, self.n, self.o = t, n, 0

            def reset(self):
                self.o = 0

            def get(self, n):
                assert self.o + n <= self.n, (self.o, n, self.n)
                ap = self.t[:, self.o:self.o + n]
                self.o += n
                return ap

        cF = Carver(arF, NAF)
        cB = Carver(arB, NAB)
        pbank = [ps(f"pb{i}", [128, 512], F32) for i in range(7)]
        pbf = [ps(f"pbf{i}", [128, 1024], BF16) for i in range(1)]

        def xk(kc, t8):
            return f"x{kc}_{t8}"

        def xkeys512(kc, tt):
            return [xk(kc, 2 * tt), xk(kc, 2 * tt + 1)]

        def hk(kc, tt):
            return f"h{kc}_{tt}"

        sc = lambda off, j: sm[:, off + j: off + j + 1]

        S.dma("sp", sm[:], sm_d, "sm", writes=["sm"])
        for kc in range(KC):
            S.dma("sp", xT[:, kc, :], xT_d[:, kc, :], f"xin{kc}", writes=[xk(kc, t) for t in range(8)])
        S.op("dve", lambda e: e.memset(ones_f[:], 1.0), writes=["ones_f"])
        S.op("pool", lambda e: e.memset(ident_f[:], 0.0), writes=["ident_f"])
        S.op("pool", lambda e: e.affine_select(out=ident_f[:], in_=ident_f[:], pattern=[[-1, 128]],
                                               compare_op=ALU.not_equal, fill=1.0, base=0, channel_multiplier=1),
             reads=["ident_f"], writes=["ident_f"])
        S.op("dve", lambda e: e.tensor_copy(out=ident_b[:], in_=ident_f[:]), reads=["ident_f"], writes=["ident_b"])
        S.op("pool", lambda e: e.memset(triu[:], 1.0), writes=["triu"])
        S.op("pool", lambda e: e.affine_select(out=triu[:], in_=triu[:], pattern=[[1, 128]],
                                               compare_op=ALU.is_ge, fill=0.0, base=0, channel_multiplier=-1),
             reads=["triu"], writes=["triu"])
        S.op("pool", lambda e: e.memset(maskneg[:], 0.0), writes=["maskneg"])
        S.op("pool", lambda e: e.affine_select(out=maskneg[:], in_=maskneg[:], pattern=[[1, 128]],
                                               compare_op=ALU.is_ge, fill=-30000.0, base=0, channel_multiplier=-1),
             reads=["maskneg"], writes=["maskneg"])
        S.op("pool", lambda e: e.memset(sel8[:], 0.0), writes=["sel8"])
        for h in range(8):
            S.op("pool", lambda e, h=h: e.affine_select(out=sel8[:, h, :], in_=sel8[:, h, :], pattern=[[0, 128]],
                                                        compare_op=ALU.not_equal, fill=1.0, base=-h,
                                                        channel_multiplier=1),
                 reads=["sel8"], writes=["sel8"])
            S.op("pool", lambda e, h=h: e.affine_select(out=sel8[:, h, :], in_=sel8[:, h, :], pattern=[[0, 128]],
                                                        compare_op=ALU.not_equal, fill=1.0, base=-(32 + h),
                                                        channel_multiplier=1),
                 reads=["sel8"], writes=["sel8"])
        S.op("act", lambda e: e.activation(out=cond[:], in_=sm[:, _SM["c"]:_SM["c"] + 8], func=AF.Silu),
             reads=["sm"], writes=["cond"])
        S.op("dve", lambda e: e.tensor_scalar(out=sm[:, _SM["convw"]:_SM["convw"] + 120],
                                              in0=sm[:, _SM["convw"]:_SM["convw"] + 120], scalar1=0.5, scalar2=None,
                                              op0=ALU.mult), reads=["sm"], writes=["sm"])

        ada_state = {"n": 0}
        cond_b = sb("cond_b", [128, 8], BF16)
        S.op("dve", lambda e: e.tensor_copy(out=cond_b[:], in_=cond[:]), reads=["cond"], writes=["cond_b"])

        def ada_piece(i, j0, ncol, stages, pmod):
            n = ada_state["n"]
            ada_state["n"] += 1
            slot = n % len(stages)
            stg = stages[slot]
            S.dma("pool", stg[:, :, 0:ncol], ada_w[i][:, j0 * 128:j0 * 128 + ncol].rearrange("(kc p) n -> p kc n", p=128),
                  f"ada{slot}", writes=[f"adast{slot}"])
            for m in range(ncol // 128):
                j = j0 + m
                for kc in range(KC):
                    S.op("pe", lambda e, j=j, m=m, kc=kc: e.matmul(pmod[:, j:j + 1], lhsT=stg[:, kc, m * 128:(m + 1) * 128],
                                                                    rhs=cond_b[:, kc:kc + 1], start=(kc == 0), stop=(kc == 7)),
                         reads=[f"adast{slot}", "cond_b"], writes=["pb5"])

        def ada_finish(i, pmod):
            S.op("dve", lambda e: e.tensor_tensor(out=modT[i][:], in0=pmod[:, 0:48],
                                                  in1=sm[:, _SM["adab"] + 48 * i:_SM["adab"] + 48 * i + 48], op=ALU.add),
                 reads=["pb5", "sm"], writes=[f"modT{i}"])
            for (dst, goff, scol, nm) in ((aM[i], _SM["mixg"] + 8 * i, 8, "aM"), (aF[i], _SM["mlpg"] + 8 * i, 32, "aF")):
                S.op("dve", lambda e, dst=dst, goff=goff, scol=scol: e.scalar_tensor_tensor(
                    out=dst[:], in0=modT[i][:, scol:scol + 8], scalar=1.0, in1=sm[:, goff:goff + 8],
                    op0=ALU.add, op1=ALU.mult), reads=[f"modT{i}", "sm"], writes=[f"{nm}{i}"])

        def norm_stats(tt, pst_ap, pst_key, ndiv):
            tsl = slice(tt * 512, (tt + 1) * 512)
            for kc in range(KC):
                sq = nsq[kc % 2]
                S.op("act", lambda e, sq=sq, kc=kc: e.activation(out=sq[:], in_=xT[:, kc, tsl], func=AF.Square),
                     reads=xkeys512(kc, tt), writes=[f"nsq{kc % 2}"])
                S.op("pe", lambda e, sq=sq, kc=kc: e.matmul(pst_ap, lhsT=ones_f[:], rhs=sq[:], start=(kc == 0), stop=(kc == 7)),
                     reads=[f"nsq{kc % 2}", "ones_f"], writes=[pst_key])
            S.op("act", lambda e: e.activation(out=nrstd[tt][:], in_=pst_ap, func=AF.Sqrt, bias=eps_t[:, 0:1], scale=1.0 / ndiv),
                 reads=[pst_key, "eps"], writes=[f"nrstd{tt}"])
            S.op("dve", lambda e: e.reciprocal(out=nrstd[tt][:], in_=nrstd[tt][:]), reads=[f"nrstd{tt}"], writes=[f"nrstd{tt}"])

        eps_t = sb("eps_t", [128, 1], F32)
        S.op("dve", lambda e: e.memset(eps_t[:], EPS), writes=["eps"])
        eps4_t = sb("eps4_t", [128, 1], F32)
        S.op("dve", lambda e: e.memset(eps4_t[:], 4.0 * EPS), writes=["eps"])
        one_t = sb("one_t", [128, 1], F32)
        S.op("dve", lambda e: e.memset(one_t[:], 1.0), writes=["one"])

        def rmsnorm_mod(a_t, a_key, mod_t, mod_key, shcol, stats=True):
            if stats:
                for tt in range(4):
                    norm_stats(tt, pbank[tt][:], f"pb{tt}", float(D))
            for tt in range(4):
                rmsnorm_mod_tile(a_t, a_key, mod_t, mod_key, shcol, tt)

        def rmsnorm_mod_tile(a_t, a_key, mod_t, mod_key, shcol, tt):
            if True:
                tsl = slice(tt * 512, (tt + 1) * 512)
                for kc in range(KC):
                    tmp = ntmp[kc % 2]
                    S.op("dve", lambda e, tmp=tmp, kc=kc: e.scalar_tensor_tensor(
                        out=tmp[:], in0=xT[:, kc, tsl], scalar=a_t[:, kc:kc + 1], in1=nrstd[tt][:],
                        op0=ALU.mult, op1=ALU.mult),
                        reads=xkeys512(kc, tt) + [f"nrstd{tt}", a_key], writes=[f"ntmp{kc % 2}"])
                    S.op("act", lambda e, tmp=tmp, kc=kc: e.activation(
                        out=hT[:, kc, tsl], in_=tmp[:], func=AF.Identity,
                        bias=mod_t[:, shcol + kc:shcol + kc + 1], scale=1.0),
                        reads=[f"ntmp{kc % 2}", mod_key], writes=[hk(kc, tt)])

        def wslot(k):
            return Wreg[:, k * 4096:(k + 1) * 4096]

        wctr = {"n": 0}

        def load_wtile(src_ap_3d):
            k = wctr["n"] % 4
            wctr["n"] += 1
            dst = wslot(k).rearrange("p (a b) -> p a b", b=512)
            S.dma("pool", dst, src_ap_3d, f"w{k}", writes=[f"wslot{k}"])
            return dst, f"wslot{k}"

        def evac_x(psum_ap, pkey, gate_ap, gkey, dc, t8list, tsl):
            keys = [xk(dc, t) for t in t8list]
            S.op("dve", lambda e: e.scalar_tensor_tensor(out=xT[:, dc, tsl], in0=psum_ap, scalar=gate_ap,
                                                         in1=xT[:, dc, tsl], op0=ALU.mult, op1=ALU.add),
                 reads=[pkey, gkey] + keys, writes=keys)

        def mlp(i, extra=None):
            cF.reset()
            cB.reset()
            aT = cB.get(8 * L).rearrange("p (j t) -> p j t", t=L)
            rtmp = [cF.get(512) for _ in range(3)]
            nb = [0, 1, 2, 3, 4] if extra is not None else [0, 1, 2, 3, 4, 5, 6]
            prot = Rot([(pbank[b][:], f"pb{b}") for b in nb])
            rrot = Rot(list(range(3)))
            sq_eng = Rot(["dve"])
            gate = lambda dc: modT[i][:, 40 + dc:41 + dc]
            for blk in range(4):
                wu = [load_wtile(mlp_up[i][:, blk * 1024 + hf * 512: blk * 1024 + (hf + 1) * 512]
                                 .rearrange("(kc p) n -> p kc n", p=128)) for hf in range(2)]
                wd = [load_wtile(mlp_down[i][blk * 1024:(blk + 1) * 1024, hf * 512:(hf + 1) * 512]
                                 .rearrange("(j p) n -> p j n", p=128)) for hf in range(2)]
                for j in range(8):
                    if extra is not None:
                        for _ in range(2):
                            nxt = next(extra, None)
                            if nxt is not None:
                                nxt()
                    wt, wkey = wu[j // 4]
                    col = (j % 4) * 128
                    for tt in range(4):
                        tsl = slice(tt * 512, (tt + 1) * 512)
                        pp, pkey = prot.next()
                        for kc in range(KC):
                            S.op("pe", lambda e, pp=pp, wt=wt, kc=kc, col=col, tsl=tsl: e.matmul(
                                pp, lhsT=wt[:, kc, col:col + 128], rhs=hT[:, kc, tsl], start=(kc == 0), stop=(kc == 7)),
                                reads=[wkey, hk(kc, tt)], writes=[pkey])
                        r = rrot.next()
                        S.op("act", lambda e, pp=pp, r=r: e.activation(out=rtmp[r], in_=pp, func=AF.Relu),
                             reads=[pkey], writes=[f"rtmp{r}"])
                        S.op(sq_eng.next(), lambda e, r=r, j=j, tsl=tsl: e.tensor_tensor(
                            out=aT[:, j, tsl], in0=rtmp[r], in1=rtmp[r], op=ALU.mult),
                            reads=[f"rtmp{r}"], writes=[f"aT{j}_{tt}"])
                order = [(dc, tt) for dc in range(8) for tt in range(4)] if blk < 3 else \
                        [(dc, tt) for tt in range(4) for dc in range(8)]
                for (dc, tt) in order:
                    wt, wkey = wd[dc // 4]
                    col = (dc % 4) * 128
                    if True:
                        tsl = slice(tt * 512, (tt + 1) * 512)
                        pp, pkey = prot.next()
                        for j in range(8):
                            S.op("pe", lambda e, pp=pp, wt=wt, j=j, col=col, tsl=tsl: e.matmul(
                                pp, lhsT=wt[:, j, col:col + 128], rhs=aT[:, j, tsl], start=(j == 0), stop=(j == 7)),
                                reads=[wkey, f"aT{j}_{tt}"], writes=[pkey])
                        evac_x(pp, pkey, gate(dc), f"modT{i}", dc, [2 * tt, 2 * tt + 1], tsl)

        def ssd():
            cF.reset()
            cB.reset()
            TT = 256
            v3 = lambda ap: ap.rearrange("p (c h) -> p c h", h=NH)
            dt_t = cF.get(512)
            nacs_t = cF.get(512)
            dd_t = cF.get(512)
            cd_t = cF.get(512)
            t_u = cF.get(512)
            t_a = cF.get(512)
            t_l = cF.get(512)
            expA = cF.get(32)
            raw = [cF.get(264) for _ in range(4)]
            acc = [cF.get(256) for _ in range(4)]
            _r0 = 4 * 512 + 512
            raw += [arF[:, _r0:_r0 + 264], arF[:, _r0 + 264:_r0 + 528]]
            acc += [arF[:, _r0 + 528:_r0 + 784], arF[:, _r0 + 784:_r0 + 1040]]
            tmpE = [cF.get(256) for _ in range(4)]
            EA = [cF.get(256) for _ in range(3)]
            prev = cF.get(512)
            yv = [cF.get(256) for _ in range(2)]
            yg = [cF.get(256) for _ in range(4)]
            ssq = cF.get(256)
            grt = cF.get(256)
            hist = cF.get(24).rearrange("p (q k) -> p q k", k=4)
            xbc = [[cB.get(256) for _ in range(6)] for _ in range(2)]
            Xdt = [cB.get(512) for _ in range(2)]
            Xdd = [cB.get(512) for _ in range(2)]
            Btok = cB.get(256)
            acs2 = cB.get(256)
            acs_t32 = cB.get(256)
            MT = [cB.get(256) for _ in range(4)]
            Cs = [cB.get(256) for _ in range(4)]
            prevb = [cB.get(512) for _ in range(2)]
            ynorm = [cB.get(256) for _ in range(4)]
            Wdt = cB.get(256).rearrange("p (k h) -> p k h", h=NH)
            maskb = cB.get(256)
            f32v = lambda n: cB.get(2 * n).bitcast(F32)
            zs = [[f32v(256) for _ in range(4)] for _ in range(2)]
            sq = [f32v(256) for _ in range(2)]
            grstd = f32v(256)
            Wg = Wreg[:, 0:10240].rearrange("p (k n) -> p k n", n=1280)
            Wo = Wreg[:, 12288:16384].rearrange("p (k n) -> p k n", n=1024)
            wgkey = lambda col: "wg_z" if col < 512 else ("wg_x" if col < 1024 else ("wg_B" if col < 1152 else "wg_C"))
            WO_KEYS = ["wslot3"]
            big = Rot([(pbank[0][:, 0:256], "pb0"), (pbank[1][:, 0:256], "pb1"),
                       (pbank[0][:, 256:512], "pb0"), (pbank[1][:, 256:512], "pb1")])
            pplain = Rot([(pbank[2][:, 0:256], "pb2"), (pbank[2][:, 256:512], "pb2")])
            pmask = Rot([(pbank[6][:, 0:256], "pb6"), (pbank[6][:, 256:512], "pb6")])
            psc, psc_k = pbank[3][:, 0:256], "pb3"
            pacsT, pacsT_k = pbank[3][0:8, 256:512], "pb3"
            pacsT32 = pbank[3][32:40, 256:512]
            pyr = Rot([(pbank[4], 0, "pb4"), (pbank[4], 256, "pb4")])
            pst, pst_k = pbank[5][:, :], "pb5"
            pxt0 = pbf[0][:, 0:512]
            pxt1 = pbank[6][:, 0:256].bitcast(BF16)
            pbtr, pbtr_k = pbf[0][:, 512:768], "pbf0"
            gate = lambda dc: modT[0][:, 16 + dc:17 + dc]

            S.dma("pool", Wdt, ssd_in_w[:, 5120:5152].rearrange("(kc p) n -> p kc n", p=128), "wdt", writes=["Wdt"])
            S.op("dve", lambda e: e.tensor_copy(out=maskb[:, 0:128], in_=maskneg[:]), reads=["maskneg"], writes=["maskb"])
            S.op("dve", lambda e: e.tensor_copy(out=maskb[:, 128:256], in_=maskneg[:]), reads=["maskneg"], writes=["maskb"])
            S.op("dve", lambda e: e.memset(acs2[0:64, :], 0.0), reads=["acs2"], writes=["acs2"])
            pdt = pbank[0][:, :]
            for c in range(16):
                for kc in range(KC):
                    S.op("pe", lambda e, c=c, kc=kc: e.matmul(pdt[:, c * 32:(c + 1) * 32],
                                                              lhsT=hT[:, kc, c * 128:(c + 1) * 128], rhs=Wdt[:, kc, :],
                                                              start=(kc == 0), stop=(kc == 7)),
                         reads=["Wdt", hk(kc, c // 4)], writes=["pb0"])
            dtb = sm[:, _SM["dtb"]:_SM["dtb"] + 32]
            S.op("dve", lambda e: e.tensor_tensor(out=v3(t_u), in0=v3(pdt), in1=dtb.unsqueeze(1).to_broadcast([128, 16, 32]),
                                                  op=ALU.add), reads=["pb0", "sm"], writes=["t_u"])
            S.op("dve", lambda e: e.scalar_tensor_tensor(out=t_a, in0=t_u, scalar=-1.0, in1=t_u, op0=ALU.mult, op1=ALU.min),
                 reads=["t_u"], writes=["t_a"])
            S.op("act", lambda e: e.activation(out=t_a, in_=t_a, func=AF.Exp), reads=["t_a"], writes=["t_a"])
            S.op("act", lambda e: e.activation(out=t_l, in_=t_a, func=AF.Ln, bias=one_t[:, 0:1], scale=1.0),
                 reads=["t_a", "one"], writes=["t_l"])
            S.op("dve", lambda e: e.scalar_tensor_tensor(out=dt_t, in0=t_u, scalar=0.0, in1=t_l, op0=ALU.max, op1=ALU.add),
                 reads=["t_u", "t_l"], writes=["dt"])
            S.op("act", lambda e: e.activation(out=expA, in_=sm[:, _SM["alog"]:_SM["alog"] + 32], func=AF.Exp),
                 reads=["sm"], writes=["expA"])
            dtA = t_u
            S.op("dve", lambda e: e.scalar_tensor_tensor(out=v3(dtA), in0=v3(dt_t), scalar=-1.0,
                                                         in1=expA.unsqueeze(1).to_broadcast([128, 16, 32]),
                                                         op0=ALU.mult, op1=ALU.mult),
                 reads=["dt", "expA", "t_u"], writes=["dtA"])
            pacs = pbank[1][:, :]
            plast = pbank[2][:, :]
            for c in range(16):
                S.op("pe", lambda e, c=c: e.matmul(pacs[:, c * 32:(c + 1) * 32], lhsT=triu[:], rhs=v3(dtA)[:, c, :],
                                                   start=True, stop=True), reads=["triu", "dtA"], writes=["pb1"])
            for c in range(16):
                S.op("pe", lambda e, c=c: e.matmul(plast[:, c * 32:(c + 1) * 32], lhsT=ones_f[:], rhs=v3(dtA)[:, c, :],
                                                   start=True, stop=True), reads=["ones_f", "dtA"], writes=["pb2"])
            S.op("act", lambda e: e.activation(out=nacs_t, in_=pacs, func=AF.Identity, scale=-1.0), reads=["pb1"], writes=["nacs"])
            S.op("act", lambda e: e.activation(out=cd_t, in_=plast, func=AF.Exp), reads=["pb2"], writes=["cd"])
            S.op("dve", lambda e: e.tensor_tensor(out=t_l, in0=plast, in1=nacs_t, op=ALU.add),
                 reads=["pb2", "nacs", "t_l"], writes=["t_l2"])
            S.op("act", lambda e: e.activation(out=t_l, in_=t_l, func=AF.Exp), reads=["t_l2"], writes=["t_l2"])
            S.op("dve", lambda e: e.tensor_tensor(out=dd_t, in0=t_l, in1=dt_t, op=ALU.mult),
                 reads=["t_l2", "dt"], writes=["dd"])

            S.barrier()

            def load_wg(g):
                S.dma("pool", Wg[:, :, 0:512], ssd_in_w[:, g * 512:(g + 1) * 512].rearrange("(kc p) n -> p kc n", p=128),
                      "wg0", writes=["wg_z"])
                S.dma("pool", Wg[:, :, 512:1024],
                      ssd_in_w[:, 2048 + g * 512:2048 + (g + 1) * 512].rearrange("(kc p) n -> p kc n", p=128),
                      "wg1", writes=["wg_x"])
                S.dma("pool", Wg[:, :, 1024:1152],
                      ssd_in_w[:, 4096 + g * 128:4096 + (g + 1) * 128].rearrange("(kc p) n -> p kc n", p=128),
                      "wg2", writes=["wg_B"])
                S.dma("pool", Wg[:, :, 1152:1280],
                      ssd_in_w[:, 4608 + g * 128:4608 + (g + 1) * 128].rearrange("(kc p) n -> p kc n", p=128),
                      "wg3", writes=["wg_C"])
                S.op("pool", lambda e: e.memset(hist, 0.0), reads=["hist"], writes=["hist"])

            def load_wo(g):
                S.dma("pool", Wo, ssd_out_w[g * 512:(g + 1) * 512, :].rearrange("(kc p) n -> p kc n", p=128),
                      "wo", writes=WO_KEYS)

            rawrot = Rot(list(range(6)))

            def a_tasks(g, t, par):
                tsl = slice(t * TT, (t + 1) * TT)
                htt = t // 2
                convch = [4 * g + q for q in range(4)] + [16 + g, 20 + g]
                wcol = [512 + q * 128 for q in range(4)] + [1024, 1152]

                def proj(col):
                    pp, pkey = big.next()
                    for kc in range(KC):
                        S.op("pe", lambda e, pp=pp, kc=kc, col=col: e.matmul(
                            pp, lhsT=Wg[:, kc, col:col + 128], rhs=hT[:, kc, tsl],
                            start=(kc == 0), stop=(kc == 7)), reads=[wgkey(col), hk(kc, htt)], writes=[pkey])
                    return pp, pkey

                def conv_pair(qs):
                    st = []
                    for q in qs:
                        pp, pkey = proj(wcol[q])
                        ri = rawrot.next()
                        rb, ab = raw[ri], acc[ri]
                        ch = convch[q]
                        wof = _SM["convw"] + ch * 4
                        S.op("pool", lambda e, rb=rb, q=q: e.tensor_copy(out=rb[:, 0:3], in_=hist[:, q, 0:3]),
                             reads=["hist", f"raw{ri}"], writes=[f"raw{ri}"])
                        S.op("act", lambda e, rb=rb, pp=pp: e.activation(out=rb[:, 3:259], in_=pp, func=AF.Copy),
                             reads=[pkey, f"raw{ri}"], writes=[f"raw{ri}"])
                        S.op("act", lambda e, ab=ab, pp=pp, wof=wof, ch=ch: e.activation(
                            out=ab, in_=pp, func=AF.Identity, scale=sm[:, wof + 3:wof + 4],
                            bias=sm[:, _SM["convb"] + ch:_SM["convb"] + ch + 1]),
                            reads=[pkey, "sm", f"acc{ri}"], writes=[f"acc{ri}"])
                        S.op("pool", lambda e, rb=rb, q=q: e.tensor_copy(out=hist[:, q, 0:3], in_=rb[:, 256:259]),
                             reads=[f"raw{ri}", "hist"], writes=["hist"])
                        st.append((q, ri, rb, ab, wof))
                    for k in (2, 1, 0):
                        for (q, ri, rb, ab, wof) in st:
                            S.op("dve", lambda e, ab=ab, rb=rb, wof=wof, k=k: e.scalar_tensor_tensor(
                                out=ab, in0=rb[:, k:k + 256], scalar=sm[:, wof + k:wof + k + 1], in1=ab,
                                op0=ALU.mult, op1=ALU.add), reads=[f"raw{ri}", "sm", f"acc{ri}"], writes=[f"acc{ri}"])
                    for (q, ri, rb, ab, wof) in st:
                        S.op("act", lambda e, ab=ab, rb=rb: e.activation(out=rb[:, 0:256], in_=ab, func=AF.Tanh),
                             reads=[f"acc{ri}", f"raw{ri}"], writes=[f"raw{ri}"])
                    for (q, ri, rb, ab, wof) in st:
                        S.op("dve", lambda e, ab=ab, rb=rb, q=q: e.scalar_tensor_tensor(
                            out=xbc[par][q], in0=rb[:, 0:256], scalar=1.0, in1=ab, op0=ALU.add, op1=ALU.mult),
                            reads=[f"acc{ri}", f"raw{ri}"], writes=[f"xbc{par}_{q}"])

                def z_pair(fcs):
                    for fc in fcs:
                        pz, pzk = proj(fc * 128)
                        S.op("act", lambda e, pz=pz, fc=fc: e.activation(out=zs[par][fc], in_=pz, func=AF.Tanh, scale=0.5),
                             reads=[pzk, f"zs{par}_{fc}"], writes=[f"zs{par}_{fc}"])
                        S.op("dve", lambda e, pz=pz, fc=fc: e.scalar_tensor_tensor(
                            out=zs[par][fc], in0=zs[par][fc], scalar=1.0, in1=pz, op0=ALU.add, op1=ALU.mult),
                            reads=[pzk, f"zs{par}_{fc}"], writes=[f"zs{par}_{fc}"])

                convs = [(lambda q=q: conv_pair([q])) for q in range(6)]
                zsl = [(lambda fc=fc: z_pair([fc])) for fc in range(4)]
                return convs, zsl

            def b_s2(g, t, par, step):
                xb = xbc[par]
                xkey = lambda q: f"xbc{par}_{q}"
                if step == 0:
                    for c2 in range(2):
                        csl = slice(c2 * 128, (c2 + 1) * 128)
                        S.op("pe", lambda e, csl=csl: e.matmul(psc[:, csl], lhsT=xb[4][:, csl], rhs=xb[5][:, csl],
                                                               start=True, stop=True),
                             reads=[xkey(4), xkey(5)], writes=[psc_k])
                    for c2 in range(2):
                        c = 2 * t + c2
                        S.op("pe", lambda e, c=c, c2=c2: e.matmul(pacsT[:, c2 * 128:(c2 + 1) * 128],
                                                                  lhsT=v3(dtA)[:, c, 8 * g:8 * g + 8], rhs=triu[:],
                                                                  start=True, stop=True),
                             reads=["dtA", "triu"], writes=[pacsT_k])
                    for c2 in range(2):
                        c = 2 * t + c2
                        S.op("pe", lambda e, c=c, c2=c2: e.matmul(pacsT32[:, c2 * 128:(c2 + 1) * 128],
                                                                  lhsT=v3(dtA)[:, c, 8 * g:8 * g + 8], rhs=triu[:],
                                                                  start=True, stop=True, tile_position=(0, 32)),
                             reads=["dtA", "triu"], writes=[pacsT_k])
                    S.op("act", lambda e: e.activation(out=acs2[0:8, :], in_=pacsT, func=AF.Copy),
                         reads=[pacsT_k, "acs2"], writes=["acs2"])
                    S.op("act", lambda e: e.activation(out=acs_t32[32:40, :], in_=pacsT32, func=AF.Copy),
                         reads=[pacsT_k, "acs_t32"], writes=["acs_t32"])
                    S.op("dve", lambda e: e.tensor_tensor(out=acs2[32:40, :], in0=pacsT32, in1=acs_t32[32:40, :],
                                                          op=ALU.subtract),
                         reads=[pacsT_k, "acs_t32", "acs2"], writes=["acs2"])
                    return
                c2 = step - 1
                c = 2 * t + c2
                csl = slice(c2 * 128, (c2 + 1) * 128)
                pxt, pxt_k = (pxt0, "pbf0") if c2 == 0 else (pxt1, "pb6")
                for q in range(4):
                    S.op("pe", lambda e, q=q: e.transpose(
                        out=pxt[:, q * 128:(q + 1) * 128], in_=xb[q][:, csl], identity=ident_b[:]),
                        reads=[xkey(q), "ident_b"], writes=[pxt_k])
                if c2 == 0:
                    for cc in range(2):
                        ccsl = slice(cc * 128, (cc + 1) * 128)
                        S.op("pe", lambda e, ccsl=ccsl: e.transpose(out=pbtr[:, ccsl], in_=xb[4][:, ccsl], identity=ident_b[:]),
                             reads=[xkey(4), "ident_b"], writes=[pbtr_k])
                S.op("dve", lambda e: e.tensor_tensor(
                    out=Xdt[c2].rearrange("p (h d) -> p h d", d=64), in0=pxt.rearrange("p (h d) -> p h d", d=64),
                    in1=v3(dt_t)[:, c, 8 * g:8 * g + 8].unsqueeze(2).to_broadcast([128, 8, 64]), op=ALU.mult),
                    reads=[pxt_k, "dt", f"Xdt{c2}"], writes=[f"Xdt{c2}"])
                S.op("dve", lambda e: e.tensor_tensor(
                    out=Xdd[c2].rearrange("p (h d) -> p h d", d=64), in0=pxt.rearrange("p (h d) -> p h d", d=64),
                    in1=v3(dd_t)[:, c, 8 * g:8 * g + 8].unsqueeze(2).to_broadcast([128, 8, 64]), op=ALU.mult),
                    reads=[pxt_k, "dd", f"Xdd{c2}"], writes=[f"Xdd{c2}"])
                if c2 == 0:
                    S.op("act", lambda e: e.activation(out=Btok, in_=pbtr, func=AF.Copy),
                         reads=[pbtr_k, "Btok"], writes=["Btok"])

            def b_s3a(g, t, c2):
                if True:
                    c = 2 * t + c2
                    csl = slice(c2 * 128, (c2 + 1) * 128)
                    S.op("act", lambda e, c2=c2: e.activation(out=prevb[c2], in_=prev, func=AF.Copy),
                         reads=["prev", f"prevb{c2}"], writes=[f"prevb{c2}"])
                    pst, pst_k = (pbank[4][:, :], "pb4") if c2 == 0 else (pbank[5][:, :], "pb5")
                    S.op("pe", lambda e, c2=c2, csl=csl, pst=pst: e.matmul(pst, lhsT=Btok[:, csl], rhs=Xdd[c2], start=True, stop=True),
                         reads=["Btok", f"Xdd{c2}"], writes=[pst_k])
                    S.op("pool", lambda e, c=c: e.tensor_tensor(
                        out=prev.rearrange("p (h d) -> p h d", d=64), in0=prev.rearrange("p (h d) -> p h d", d=64),
                        in1=v3(cd_t)[:, c, 8 * g:8 * g + 8].unsqueeze(2).to_broadcast([128, 8, 64]), op=ALU.mult),
                        reads=["prev", "cd"], writes=["prev"])
                    S.op("dve", lambda e, pst=pst: e.tensor_tensor(out=prev, in0=prev, in1=pst, op=ALU.add),
                         reads=["prev", pst_k], writes=["prev"])

            earot = Rot(list(range(3)))

            def b_head_pre(g, t, h, par):
                hh = 8 * g + h
                mi = h % 4
                xb5, xk5 = xbc[par][5], f"xbc{par}_5"
                pa, pak = pplain.next()
                pm, pmk = pmask.next()
                for (pt, ptk, msk) in ((pa, pak, False), (pm, pmk, True)):
                    S.op("pe", lambda e, pt=pt, h=h, msk=msk: e.matmul(pt, lhsT=sel8[:, h, :], rhs=acs2[0:40, :],
                                                                       start=True, stop=(not msk)),
                         reads=["sel8", "acs2"], writes=[ptk])
                    if msk:
                        S.op("pe", lambda e, pt=pt: e.matmul(pt, lhsT=ident_b[:], rhs=maskb, start=False, stop=True),
                             reads=["ident_b", "maskb"], writes=[ptk])
                ei = earot.next()
                ea = EA[ei]
                S.op("act", lambda e, ea=ea, pa=pa: e.activation(out=ea, in_=pa, func=AF.Exp),
                     reads=[pak, f"EA{ei}"], writes=[f"EA{ei}"])
                te = tmpE[mi]
                for c2 in range(2):
                    c = 2 * t + c2
                    csl = slice(c2 * 128, (c2 + 1) * 128)
                    S.op("act", lambda e, te=te, pm=pm, csl=csl, c=c, hh=hh: e.activation(
                        out=te[:, csl], in_=pm[:, csl], func=AF.Exp, bias=v3(nacs_t)[:, c, hh:hh + 1], scale=1.0),
                        reads=[pmk, "nacs", f"tmpE{mi}"], writes=[f"tmpE{mi}"])
                S.op("dve", lambda e, te=te, mi=mi: e.tensor_tensor(out=MT[mi], in0=te, in1=psc, op=ALU.mult),
                     reads=[f"tmpE{mi}", psc_k, f"MT{mi}"], writes=[f"MT{mi}"])
                S.op("pool", lambda e, ea=ea, mi=mi: e.tensor_tensor(out=Cs[mi], in0=xb5, in1=ea, op=ALU.mult),
                     reads=[f"EA{ei}", xk5, f"Cs{mi}"], writes=[f"Cs{mi}"])

            def b_head_y(g, t, fc, par):
                pyb, pyo, pyk = pyr.next()
                for c2 in range(2):
                    csl = slice(c2 * 128, (c2 + 1) * 128)
                    for hp in range(2):
                        h = 2 * fc + hp
                        mi = h % 4
                        outap = pyb[hp * 64:(hp + 1) * 64, pyo + c2 * 128:pyo + (c2 + 1) * 128]
                        tp = (0, 64) if hp == 1 else None
                        S.op("pe", lambda e, outap=outap, c2=c2, h=h, mi=mi, csl=csl, tp=tp: e.matmul(
                            outap, lhsT=Xdt[c2][:, h * 64:(h + 1) * 64], rhs=MT[mi][:, csl],
                            start=True, stop=False, tile_position=tp),
                            reads=[f"Xdt{c2}", f"MT{mi}"], writes=[pyk])
                        S.op("pe", lambda e, outap=outap, c2=c2, h=h, mi=mi, csl=csl, tp=tp: e.matmul(
                            outap, lhsT=prevb[c2][:, h * 64:(h + 1) * 64], rhs=Cs[mi][:, csl],
                            start=False, stop=True, tile_position=tp),
                            reads=[f"prevb{c2}", f"Cs{mi}"], writes=[pyk])
                pyap = pyb[:, pyo:pyo + 256]
                dcol = _SM["dfeat"] + 4 * g + fc
                yvb = yv[fc % 2]
                S.op("dve", lambda e, pyap=pyap, dcol=dcol, yvb=yvb: e.scalar_tensor_tensor(
                    out=yvb, in0=xbc[par][fc], scalar=sm[:, dcol:dcol + 1], in1=pyap, op0=ALU.mult, op1=ALU.add),
                    reads=[pyk, f"xbc{par}_{fc}", "sm", f"yv{fc % 2}"], writes=[f"yv{fc % 2}"])
                S.op("dve", lambda e, yvb=yvb: e.tensor_tensor(out=yg[fc], in0=yvb, in1=zs[par][fc], op=ALU.mult),
                     reads=[f"yv{fc % 2}", f"zs{par}_{fc}", f"yg{fc}"], writes=[f"yg{fc}"])
                if fc == 0:
                    S.op("act", lambda e: e.activation(out=ssq, in_=yg[0], func=AF.Square),
                         reads=["yg0", "ssq"], writes=["ssq"])
                else:
                    sqb = sq[fc % 2]
                    S.op("act", lambda e, sqb=sqb: e.activation(out=sqb, in_=yg[fc], func=AF.Square),
                         reads=[f"yg{fc}", f"sq{fc % 2}"], writes=[f"sq{fc % 2}"])
                    if fc < 3:
                        S.op("pool", lambda e, sqb=sqb: e.tensor_tensor(out=ssq, in0=ssq, in1=sqb, op=ALU.add),
                             reads=[f"sq{fc % 2}", "ssq"], writes=["ssq"])

            s4state = {}

            def b_s4(g, t):
                pn, pnk = big.next()
                S.op("pe", lambda e, pn=pn: e.matmul(pn, lhsT=ones_f[:], rhs=ssq, start=True, stop=False),
                     reads=["ones_f", "ssq"], writes=[pnk])
                S.op("pe", lambda e, pn=pn: e.matmul(pn, lhsT=ones_f[:], rhs=sq[1], start=False, stop=True),
                     reads=["ones_f", "sq1"], writes=[pnk])
                s4state["p"] = (pn, pnk, g)

            def b_s4e():
                pn, pnk, g = s4state.pop("p")
                S.op("act", lambda e, pn=pn: e.activation(out=grt, in_=pn, func=AF.Sqrt, bias=eps4_t[:, 0:1], scale=1.0 / 512.0),
                     reads=[pnk, "eps", "grt"], writes=["grt"])
                S.op("dve", lambda e: e.reciprocal(out=grstd, in_=grt), reads=["grt", "grstd"], writes=["grstd"])
                for fc in range(4):
                    ncol = _SM["nw"] + 4 * g + fc
                    S.op("dve", lambda e, fc=fc, ncol=ncol: e.scalar_tensor_tensor(
                        out=ynorm[fc], in0=yg[fc], scalar=sm[:, ncol:ncol + 1], in1=grstd, op0=ALU.mult, op1=ALU.mult),
                        reads=[f"yg{fc}", "sm", "grstd", f"ynorm{fc}"], writes=[f"ynorm{fc}"])

            def b_s4b(g, t, dc):
                tsl = slice(t * TT, (t + 1) * TT)
                if True:
                    pp, pkey = big.next()
                    for kc4 in range(4):
                        S.op("pe", lambda e, pp=pp, kc4=kc4, dc=dc: e.matmul(
                            pp, lhsT=Wo[:, kc4, dc * 128:(dc + 1) * 128], rhs=ynorm[kc4],
                            start=(kc4 == 0), stop=(kc4 == 3)), reads=WO_KEYS + [f"ynorm{kc4}"], writes=[pkey])
                    evac_x(pp, pkey, gate(dc), "modT0", dc, [t], tsl)

            iters = [(g, t) for g in range(4) for t in range(8)]
            load_wg(0)
            load_wo(0)
            cv0, zs0 = a_tasks(0, 0, 0)
            for task in cv0 + zs0:
                task()
            for idx, (g, t) in enumerate(iters):
                par = idx % 2
                convs, zsl, outs = [], [], []
                if idx + 1 < len(iters):
                    ng, nt = iters[idx + 1]
                    if nt == 0:
                        load_wg(ng)
                    convs, zsl = a_tasks(ng, nt, 1 - par)
                if idx > 0:
                    pg, pt = iters[idx - 1]
                    outs = [(lambda dc=dc, pg=pg, pt=pt: b_s4b(pg, pt, dc)) for dc in range(8)]
                convs, zsl, outs = iter(convs), iter(zsl), iter(outs)

                def fill(kinds):
                    for kd in kinds:
                        f = next({"c": convs, "z": zsl, "o": outs}[kd], None)
                        if f is not None:
                            f()
                if t == 0:
                    S.op("pool", lambda e: e.memset(prev, 0.0), reads=["prev"], writes=["prev"])
                b_s2(g, t, par, 0)
                if "p" in s4state:
                    b_s4e()
                fill("c")
                b_s2(g, t, par, 1)
                fill("cz")
                b_s2(g, t, par, 2)
                fill("cz")
                b_s3a(g, t, 0)
                fill("czo")
                b_s3a(g, t, 1)
                fill("czo")
                plan = ["o", "c", "o", "o", "o", "o", "o", ""]
                for h in range(8):
                    b_head_pre(g, t, h, par)
                    if h >= 2 and h % 2 == 0:
                        b_head_y(g, t, h // 2 - 1, par)
                    fill(plan[h])
                b_head_y(g, t, 3, par)
                fill("cccccczzzz")
                b_s4(g, t)
                fill("oooooooo")
                if idx > 0 and iters[idx - 1][1] == 7:
                    load_wo(g)
            b_s4e()
            for dc in range(8):
                b_s4b(iters[-1][0], iters[-1][1], dc)

        sc_wts = {}

        def load_sc(j):
            k = wctr["n"] % 4
            wctr["n"] += 1
            wt = wslot(k)[:, 0:3072].rearrange("p (kc th f) -> p kc th f", th=3, f=128)
            wkey = f"wslot{k}"
            for th in range(3):
                S.dma("pool", wt[:, :, th, :],
                      sc_in_w[:, th * 1024 + j * 128: th * 1024 + (j + 1) * 128].rearrange("(kc p) f -> p kc f", p=128),
                      f"w{k}", writes=[wkey])
            sc_wts[j] = (wt, wkey)

        def shortconv():
            cF.reset()
            cB.reset()
            yT = cB.get(8 * L).rearrange("p (j t) -> p j t", t=L)
            xv = [cF.get(512) for _ in range(2)]
            ub = [cF.get(516) for _ in range(2)]
            vb = [cF.get(512) for _ in range(2)]
            prot = Rot([(pbank[b][:], f"pb{b}") for b in range(6)])
            gate = lambda dc: modT[1][:, 16 + dc:17 + dc]
            n = 0
            wts = sc_wts
            for j in range(3):
                if j not in wts:
                    load_sc(j)
            wo = []
            for j in range(8):
                if j + 3 < 8:
                    load_sc(j + 3)
                elif len(wo) < 2:
                    hf = len(wo)
                    wo.append(load_wtile(sc_out_w[:, hf * 512:(hf + 1) * 512].rearrange("(kc p) n -> p kc n", p=128)))
                wt, wkey = wts[j]
                for tt in range(4):
                    tsl = slice(tt * 512, (tt + 1) * 512)
                    pps = []
                    for th in range(3):
                        pp, pkey = prot.next()
                        pps.append((pp, pkey))
                        for kc in range(KC):
                            S.op("pe", lambda e, pp=pp, kc=kc, th=th, wt=wt, tsl=tsl: e.matmul(
                                pp, lhsT=wt[:, kc, th, :], rhs=hT[:, kc, tsl], start=(kc == 0), stop=(kc == 7)),
                                reads=[wkey, hk(kc, tt)], writes=[pkey])
                    (pB, pBk), (pC, pCk), (pX, pXk) = pps
                    bi = n % 2
                    n += 1
                    u, uprev, v, xvb = ub[bi], ub[1 - bi], vb[bi], xv[bi]
                    S.op("act", lambda e, xvb=xvb, pX=pX: e.activation(out=xvb, in_=pX, func=AF.Copy),
                         reads=[pXk, f"xv{bi}"], writes=[f"xv{bi}"])
                    if tt == 0:
                        S.op("dve", lambda e, u=u: e.memset(u[:, 0:2], 0.0), reads=[f"u{bi}"], writes=[f"u{bi}"])
                    else:
                        S.op("act", lambda e, u=u, uprev=uprev: e.activation(out=u[:, 0:2], in_=uprev[:, 512:514], func=AF.Copy),
                             reads=[f"u{1 - bi}", f"u{bi}"], writes=[f"u{bi}"])
                    S.op("dve", lambda e, u=u, pC=pC, xvb=xvb: e.tensor_tensor(out=u[:, 2:514], in0=pC, in1=xvb, op=ALU.mult),
                         reads=[pCk, f"xv{bi}", f"u{bi}"], writes=[f"u{bi}"])
                    wof = _SM["scw"] + j * 3
                    S.op("act", lambda e, u=u, v=v, wof=wof: e.activation(
                        out=v, in_=u[:, 0:512], func=AF.Identity, scale=sm[:, wof:wof + 1]),
                        reads=[f"u{bi}", "sm", f"v{bi}"], writes=[f"v{bi}"])
                    for kk in (1, 2):
                        S.op("dve", lambda e, u=u, v=v, wof=wof, kk=kk: e.scalar_tensor_tensor(
                            out=v, in0=u[:, kk:kk + 512], scalar=sm[:, wof + kk:wof + kk + 1], in1=v,
                            op0=ALU.mult, op1=ALU.add), reads=[f"u{bi}", "sm", f"v{bi}"], writes=[f"v{bi}"])
                    S.op("dve", lambda e, v=v, pB=pB, j=j, tsl=tsl: e.tensor_tensor(out=yT[:, j, tsl], in0=pB, in1=v, op=ALU.mult),
                         reads=[pBk, f"v{bi}"], writes=[f"yT{j}_{tt}"])
            for (dc, tt) in [(dc, tt) for tt in range(4) for dc in range(8)]:
                wt, wkey = wo[dc // 4]
                col = (dc % 4) * 128
                if True:
                    tsl = slice(tt * 512, (tt + 1) * 512)
                    pp, pkey = prot.next()
                    for j in range(8):
                        S.op("pe", lambda e, pp=pp, wt=wt, j=j, col=col, tsl=tsl: e.matmul(
                            pp, lhsT=wt[:, j, col:col + 128], rhs=yT[:, j, tsl], start=(j == 0), stop=(j == 7)),
                            reads=[wkey, f"yT{j}_{tt}"], writes=[pkey])
                    evac_x(pp, pkey, gate(dc), "modT1", dc, [2 * tt, 2 * tt + 1], tsl)

        def dump_x():
            for kc in range(KC):
                S.dma("sp", out_d[:, kc, :], xT[:, kc, :], "out", reads=[xk(kc, t) for t in range(8)])

        def dump_h():
            cF.reset()
            tmp = cF.get(2048)
            for kc in range(KC):
                S.op("dve", lambda e, kc=kc: e.tensor_copy(out=tmp, in_=hT[:, kc, :]),
                     reads=[hk(kc, tt) for tt in range(4)] + ["dumptmp"], writes=["dumptmp"])
                S.dma("sp", out_d[:, kc, :], tmp, "out", reads=["dumptmp"])

        def final_norm():
            cF.reset()
            cF.get(1536)
            ob = [cF.get(512) for _ in range(3)]
            ob += [arF[:, 4608 + i * 512:4608 + (i + 1) * 512] for i in range(2)]
            NOB = len(ob)
            n = 0
            for tt in range(4):
                norm_stats(tt, pbank[tt][:], f"pb{tt}", float(D))
            for tt in range(4):
                tsl = slice(tt * 512, (tt + 1) * 512)
                for kc in range(KC):
                    oi = n % NOB
                    n += 1
                    o = ob[oi]
                    gcol = _SM["fing"] + kc
                    S.op("dve", lambda e, o=o, kc=kc, gcol=gcol, tsl=tsl, tt=tt: e.scalar_tensor_tensor(
                        out=o, in0=xT[:, kc, tsl], scalar=sm[:, gcol:gcol + 1], in1=nrstd[tt][:], op0=ALU.mult, op1=ALU.mult),
                        reads=xkeys512(kc, tt) + [f"nrstd{tt}", "sm", f"ob{oi}"], writes=[f"ob{oi}"])
                    S.dma("sp", out_d[:, kc, tsl], o, f"out{oi}", reads=[f"ob{oi}"])

        pmod = pbank[5]
        for tt in range(4):
            norm_stats(tt, pbank[tt][:], f"pb{tt}", float(D))
        for jj in range(12):
            ada_piece(0, jj * 4, 512, ada_st0, pmod)
        ada_finish(0, pmod)
        rmsnorm_mod(aM[0], "aM0", modT[0], "modT0", 0, stats=False)
        done = False
        if dbg_stop == "norm0":
            dump_h()
            done = True
        if not done:
            S.barrier()
            ssd()
            if dbg_stop == "mix0":
                dump_x()
                done = True
        if not done:
            S.barrier()
            rmsnorm_mod(aF[0], "aF0", modT[0], "modT0", 24)
            def ada1_gen():
                for jj in range(24):
                    yield (lambda jj=jj: ada_piece(1, jj * 2, 256, ada_st1, pmod))
            gen = ada1_gen()
            mlp(0, extra=gen)
            for rest in gen:
                rest()
            ada_finish(1, pmod)
            if dbg_stop == "mlp0":
                dump_x()
                done = True
        if not done:
            for j in range(3):
                load_sc(j)
            rmsnorm_mod(aM[1], "aM1", modT[1], "modT1", 0)
            S.barrier()
            shortconv()
            if dbg_stop == "mix1":
                dump_x()
                done = True
        if not done:
            rmsnorm_mod(aF[1], "aF1", modT[1], "modT1", 24)
            S.barrier()
            mlp(1)
            if dbg_stop == "mlp1":
                dump_x()
                done = True
        if not done:
            final_norm()
        S.emit(final_waits=[("sp", nm) for nm in S.dma_sems if nm.startswith("out")])
    return nc


def _chunks(v):
    v = np.asarray(v, np.float32)
    return np.ascontiguousarray(v.reshape(-1, 128).T)


def make_smalls(b, c, ada_b, mix_norm_w, mlp_norm_w, ssd_conv_w, ssd_conv_b, ssd_dt_bias, ssd_A_log,
                ssd_D, ssd_norm_w, sc_conv_w, final_norm_w):
    sm = np.zeros((128, NS), np.float32)
    sm[:, _SM["c"]:_SM["c"] + 8] = _chunks(c[b])
    for i in range(2):
        sm[:, _SM["adab"] + 48 * i:_SM["adab"] + 48 * (i + 1)] = _chunks(ada_b[i])
        sm[:, _SM["mixg"] + 8 * i:_SM["mixg"] + 8 * (i + 1)] = _chunks(mix_norm_w[i])
        sm[:, _SM["mlpg"] + 8 * i:_SM["mlpg"] + 8 * (i + 1)] = _chunks(mlp_norm_w[i])
    sm[:, _SM["fing"]:_SM["fing"] + 8] = _chunks(final_norm_w)
    cw = np.asarray(ssd_conv_w[0], np.float32)
    for k in range(4):
        sm[:, _SM["convw"] + k:_SM["convw"] + 96:4] = _chunks(cw[k])
    sm[:, _SM["convb"]:_SM["convb"] + 24] = _chunks(ssd_conv_b[0])
    sm[:, _SM["dfeat"]:_SM["dfeat"] + 16] = _chunks(np.repeat(np.asarray(ssd_D[0], np.float32), 64))
    sm[:, _SM["nw"]:_SM["nw"] + 16] = _chunks(ssd_norm_w[0])
    sm[:, _SM["dtb"]:_SM["dtb"] + 32] = np.broadcast_to(np.asarray(ssd_dt_bias[0], np.float32)[None, :], (128, 32))
    sm[:, _SM["alog"]:_SM["alog"] + 32] = np.broadcast_to(np.asarray(ssd_A_log[0], np.float32)[None, :], (128, 32))
    sw = np.asarray(sc_conv_w[0], np.float32)
    for k in range(3):
        sm[:, _SM["scw"] + k:_SM["scw"] + 24:3] = _chunks(sw[k])
    return sm


_NC_CACHE = {}


def _get_nc(dbg_stop=None):
    if dbg_stop not in _NC_CACHE:
        _NC_CACHE[dbg_stop] = build_program(dbg_stop)
    return _NC_CACHE[dbg_stop]


def kernel(x, c, ada_w, ada_b, mix_norm_w, mlp_norm_w, mlp_up, mlp_down,
           ssd_in_w, ssd_conv_w, ssd_conv_b, ssd_dt_bias, ssd_A_log, ssd_D,
           ssd_norm_w, ssd_out_w, sc_in_w, sc_conv_w, sc_out_w, final_norm_w, _dbg_stop=None):
    n = 8
    x = np.asarray(x, np.float32)
    f = lambda a: np.ascontiguousarray(np.asarray(a, np.float32))
    shared = {
        "ada_w": f(ada_w), "mlp_up": f(mlp_up), "mlp_down": f(mlp_down),
        "ssd_in_w": f(ssd_in_w[0]), "ssd_out_w": f(ssd_out_w[0]),
        "sc_in_w": f(sc_in_w[0]), "sc_out_w": f(sc_out_w[0]),
    }
    in_maps = []
    for b in range(n):
        xT = np.ascontiguousarray(x[b].reshape(L, KC, 128).transpose(2, 1, 0))
        smalls = make_smalls(b, c, ada_b, mix_norm_w, mlp_norm_w, ssd_conv_w, ssd_conv_b, ssd_dt_bias,
                             ssd_A_log, ssd_D, ssd_norm_w, sc_conv_w, final_norm_w)
        m = {"xT": xT, "smalls": smalls}
        m.update(shared)
        in_maps.append(m)
    nc = _get_nc(_dbg_stop)
    res = run_bass_kernel_spmd(nc, in_maps, core_ids=list(range(n)))
    outs = []
    for b in range(n):
        oT = np.asarray(res.results[b]["outT"], np.float32)
        outs.append(oT.transpose(2, 1, 0).reshape(L, D))
    return np.stack(outs, axis=0).astype(np.float32)
```

```python
import contextlib
import numpy as np
import concourse.bass as bass
import concourse.mybir as mybir
from concourse.bass_utils import run_bass_kernel_spmd

F32 = mybir.dt.float32
BF16 = mybir.dt.bfloat16
AF = mybir.ActivationFunctionType
ALU = mybir.AluOpType
AX = mybir.AxisListType

D = 1024
L = 2048
KC = 8
DFF = 4096
NH = 32
EPS = 1e-5
ENGS = ("pe", "act", "dve", "pool", "sp")

_SM = {}
_off = 0
for _n, _w in (("c", 8), ("adab", 96), ("mixg", 16), ("mlpg", 16), ("fing", 8), ("convw", 96),
               ("convb", 24), ("dfeat", 16), ("nw", 16), ("dtb", 32), ("alog", 32), ("scw", 24)):
    _SM[_n] = _off
    _off += _w
NS = _off


class _Op:
    __slots__ = ("eng", "fn", "idx", "deps", "dma_sem", "dma_cnt", "ms", "is_dma", "done")

    def __init__(self, eng, fn, idx):
        self.eng = eng
        self.fn = fn
        self.idx = idx
        self.deps = []
        self.dma_sem = None
        self.dma_cnt = 0
        self.ms = 0
        self.is_dma = False
        self.done = False


class Sched:
    def __init__(self, nc):
        self.nc = nc
        self.ops = {e: [] for e in ENGS}
        self.last_w = {}
        self.readers = {}
        self.dma_sems = {}
        self.SAME_ENG_DIST = 10 ** 9

    def _track(self, rec, reads, writes):
        deps = []
        for k in reads:
            w = self.last_w.get(k)
            if w is not None and w is not rec:
                deps.append(w)
        for k in writes:
            w = self.last_w.get(k)
            if w is not None and w is not rec:
                deps.append(w)
            for r in self.readers.get(k, ()):
                if r is not rec:
                    deps.append(r)
        for k in reads:
            self.readers.setdefault(k, []).append(rec)
        for k in writes:
            self.last_w[k] = rec
            self.readers[k] = []
        best = {}
        for d in deps:
            if d.is_dma:
                key = ("dma", d.dma_sem)
                if key not in best or best[key].dma_cnt < d.dma_cnt:
                    best[key] = d
            else:
                key = d.eng
                if key not in best or best[key].idx < d.idx:
                    best[key] = d
        rec.deps = list(best.values())

    def op(self, eng, fn, reads=(), writes=()):
        pw = [k for k in reads if k.startswith("pb")]
        if pw:
            reads = [k for k in reads if not k.startswith("pb")]
            writes = list(writes) + pw
        rec = _Op(eng, fn, len(self.ops[eng]))
        self._track(rec, reads, writes)
        self.ops[eng].append(rec)
        return rec

    def dma(self, eng, out, in_, sem, reads=(), writes=()):
        def fn(e, out=out, in_=in_):
            return e.dma_start(out=out, in_=in_)
        rec = _Op(eng, fn, len(self.ops[eng]))
        rec.is_dma = True
        ent = self.dma_sems.setdefault(sem, [None, 0])
        ent[1] += 16
        rec.dma_sem = sem
        rec.dma_cnt = ent[1]
        self._track(rec, reads, writes)
        self.ops[eng].append(rec)
        return rec

    def barrier(self):
        lasts = []
        for e in ENGS:
            real = [r for r in self.ops[e] if r.fn is not None]
            if real:
                lasts.append(real[-1])
        dl = {}
        for e in ENGS:
            for r in self.ops[e]:
                if r.is_dma:
                    dl[r.dma_sem] = r
        for e in ENGS:
            rec = _Op(e, None, len(self.ops[e]))
            rec.deps = [d for d in lasts if d.eng != e and not d.is_dma] + list(dl.values())
            self.ops[e].append(rec)

    def _needs_wait(self, rec, d):
        if d.is_dma:
            return True
        if d.eng != rec.eng:
            return True
        if rec.is_dma:
            return True
        if rec.eng == "pe":
            return False
        if rec.eng == "pool":
            return True
        return (rec.idx - d.idx) < self.SAME_ENG_DIST

    def check(self):
        ptr = {e: 0 for e in ENGS}
        for e in ENGS:
            for r in self.ops[e]:
                r.done = False
        progress = True
        while progress:
            progress = False
            for e in ENGS:
                ops = self.ops[e]
                while ptr[e] < len(ops):
                    r = ops[ptr[e]]
                    if all(d.done for d in r.deps):
                        r.done = True
                        ptr[e] += 1
                        progress = True
                    else:
                        break
        for e in ENGS:
            if ptr[e] < len(self.ops[e]):
                raise RuntimeError(f"schedule deadlock on {e} at op {ptr[e]}")

    def emit(self, final_waits=()):
        nc = self.nc
        self.check()
        for e in ENGS:
            for rec in self.ops[e]:
                for d in rec.deps:
                    if not d.is_dma and self._needs_wait(rec, d):
                        d.ms = -1
        for e in ENGS:
            n = 0
            for rec in self.ops[e]:
                if rec.ms == -1:
                    n += 1
                    rec.ms = n
        with contextlib.ExitStack() as st:
            esem = {e: st.enter_context(nc.semaphore("s_" + e)) for e in ENGS}
            for name in self.dma_sems:
                self.dma_sems[name][0] = st.enter_context(nc.semaphore("d_" + name))
            block = st.enter_context(nc.Block())
            bmap = {"pe": block.tensor, "act": block.scalar, "dve": block.vector,
                    "pool": block.gpsimd, "sp": block.sync}
            for e in ENGS:
                ops = self.ops[e]
                fin = [fw for fw in final_waits if fw[0] == e]
                if not ops and not fin:
                    continue

                def body(eng, e=e, ops=ops, fin=fin):
                    waited = {}
                    for rec in ops:
                        for d in rec.deps:
                            if not self._needs_wait(rec, d):
                                continue
                            if d.is_dma:
                                key, val, sem = ("d", d.dma_sem), d.dma_cnt, self.dma_sems[d.dma_sem][0]
                            else:
                                key, val, sem = ("e", d.eng), d.ms, esem[d.eng]
                            if waited.get(key, 0) >= val:
                                continue
                            waited[key] = val
                            eng.wait_ge(sem, val)
                        if rec.fn is None:
                            continue
                        ins = rec.fn(eng)
                        if rec.is_dma:
                            ins.then_inc(self.dma_sems[rec.dma_sem][0], 16)
                        elif rec.ms > 0:
                            ins.then_inc(esem[e], 1)
                    for (_, name) in fin:
                        eng.wait_ge(self.dma_sems[name][0], self.dma_sems[name][1])
                bmap[e](body)


class Rot:
    def __init__(self, items):
        self.items = items
        self.i = 0

    def next(self):
        it = self.items[self.i % len(self.items)]
        self.i += 1
        return it


def build_program(dbg_stop=None):
    nc = bass.Bass("TRN2", target_bir_lowering=False)
    xT_d = nc.dram_tensor("xT", [128, KC, L], F32, kind="ExternalInput").ap()
    sm_d = nc.dram_tensor("smalls", [128, NS], F32, kind="ExternalInput").ap()
    ada_w = nc.dram_tensor("ada_w", [2, D, 6 * D], F32, kind="ExternalInput").ap()
    mlp_up = nc.dram_tensor("mlp_up", [2, D, DFF], F32, kind="ExternalInput").ap()
    mlp_down = nc.dram_tensor("mlp_down", [2, DFF, D], F32, kind="ExternalInput").ap()
    ssd_in_w = nc.dram_tensor("ssd_in_w", [D, 5152], F32, kind="ExternalInput").ap()
    ssd_out_w = nc.dram_tensor("ssd_out_w", [2048, D], F32, kind="ExternalInput").ap()
    sc_in_w = nc.dram_tensor("sc_in_w", [D, 3 * D], F32, kind="ExternalInput").ap()
    sc_out_w = nc.dram_tensor("sc_out_w", [D, D], F32, kind="ExternalInput").ap()
    out_d = nc.dram_tensor("outT", [128, KC, L], F32, kind="ExternalOutput").ap()

    with contextlib.ExitStack() as st:
        def sb(name, shape, dt):
            return st.enter_context(nc.sbuf_tensor(name, shape, dt))

        def ps(name, shape, dt=F32):
            return st.enter_context(nc.psum_tensor(name, shape, dt))

        S = Sched(nc)
        xT = sb("xT_sb", [128, KC, L], F32)
        hT = sb("hT_sb", [128, KC, L], BF16)
        Wreg = sb("Wreg", [128, 4 * 4096], BF16)
        sm = sb("sm_sb", [128, NS], F32)
        modT = [sb(f"modT{i}", [128, 48], F32) for i in range(2)]
        aM = [sb(f"aM{i}", [128, 8], F32) for i in range(2)]
        aF = [sb(f"aF{i}", [128, 8], F32) for i in range(2)]
        cond = sb("cond", [128, 8], F32)
        ones_f = sb("ones_f", [128, 128], F32)
        ident_b = sb("ident_b", [128, 128], BF16)
        ident_f = sb("ident_f", [128, 128], F32)
        triu = sb("triu", [128, 128], F32)
        maskneg = sb("maskneg", [128, 128], F32)
        sel8 = sb("sel8", [40, 8, 128], BF16)
        NAF = 10112
        NAB = 16384
        arF = sb("arenaF", [128, NAF], F32)
        arB = sb("arenaB", [128, NAB], BF16)
        _o = 6016
        nsq = [arF[:, _o + i * 512:_o + (i + 1) * 512] for i in range(2)]
        nrstd = [arF[:, _o + 1024 + i * 512:_o + 1536 + i * 512] for i in range(4)]
        ntmp = [arF[:, _o + 3072 + i * 512:_o + 3584 + i * 512] for i in range(2)]
        _a = 3584
        ada_st0 = [arB[:, i * 4096:(i + 1) * 4096].rearrange("p (k n) -> p k n", n=512) for i in range(4)]
        ada_st1 = [arF[:, _a + i * 1024:_a + (i + 1) * 1024].bitcast(BF16).rearrange("p (k n) -> p k n", n=256)
                   for i in range(2)]

        class Carver:
            def __init__(self, t, n):
                self.t, self.n, self.o = t, n, 0

            def reset(self):
                self.o = 0

            def get(self, n):
                assert self.o + n <= self.n, (self.o, n, self.n)
                ap = self.t[:, self.o:self.o + n]
                self.o += n
                return ap

        cF = Carver(arF, NAF)
        cB = Carver(arB, NAB)
        pbank = [ps(f"pb{i}", [128, 512], F32) for i in range(7)]
        pbf = [ps(f"pbf{i}", [128, 1024], BF16) for i in range(1)]

        def xk(kc, t8):
            return f"x{kc}_{t8}"

        def xkeys512(kc, tt):
            return [xk(kc, 2 * tt), xk(kc, 2 * tt + 1)]

        def hk(kc, tt):
            return f"h{kc}_{tt}"

        sc = lambda off, j: sm[:, off + j: off + j + 1]

        S.dma("sp", sm[:], sm_d, "sm", writes=["sm"])
        for kc in range(KC):
            S.dma("sp", xT[:, kc, :], xT_d[:, kc, :], f"xin{kc}", writes=[xk(kc, t) for t in range(8)])
        S.op("dve", lambda e: e.memset(ones_f[:], 1.0), writes=["ones_f"])
        S.op("pool", lambda e: e.memset(ident_f[:], 0.0), writes=["ident_f"])
        S.op("pool", lambda e: e.affine_select(out=ident_f[:], in_=ident_f[:], pattern=[[-1, 128]],
                                               compare_op=ALU.not_equal, fill=1.0, base=0, channel_multiplier=1),
             reads=["ident_f"], writes=["ident_f"])
        S.op("dve", lambda e: e.tensor_copy(out=ident_b[:], in_=ident_f[:]), reads=["ident_f"], writes=["ident_b"])
        S.op("pool", lambda e: e.memset(triu[:], 1.0), writes=["triu"])
        S.op("pool", lambda e: e.affine_select(out=triu[:], in_=triu[:], pattern=[[1, 128]],
                                               compare_op=ALU.is_ge, fill=0.0, base=0, channel_multiplier=-1),
             reads=["triu"], writes=["triu"])
        S.op("pool", lambda e: e.memset(maskneg[:], 0.0), writes=["maskneg"])
        S.op("pool", lambda e: e.affine_select(out=maskneg[:], in_=maskneg[:], pattern=[[1, 128]],
                                               compare_op=ALU.is_ge, fill=-30000.0, base=0, channel_multiplier=-1),
             reads=["maskneg"], writes=["maskneg"])
        S.op("pool", lambda e: e.memset(sel8[:], 0.0), writes=["sel8"])
        for h in range(8):
            S.op("pool", lambda e, h=h: e.affine_select(out=sel8[:, h, :], in_=sel8[:, h, :], pattern=[[0, 128]],
                                                        compare_op=ALU.not_equal, fill=1.0, base=-h,
                                                        channel_multiplier=1),
                 reads=["sel8"], writes=["sel8"])
            S.op("pool", lambda e, h=h: e.affine_select(out=sel8[:, h, :], in_=sel8[:, h, :], pattern=[[0, 128]],
                                                        compare_op=ALU.not_equal, fill=1.0, base=-(32 + h),
                                                        channel_multiplier=1),
                 reads=["sel8"], writes=["sel8"])
        S.op("act", lambda e: e.activation(out=cond[:], in_=sm[:, _SM["c"]:_SM["c"] + 8], func=AF.Silu),
             reads=["sm"], writes=["cond"])
        S.op("dve", lambda e: e.tensor_scalar(out=sm[:, _SM["convw"]:_SM["convw"] + 120],
                                              in0=sm[:, _SM["convw"]:_SM["convw"] + 120], scalar1=0.5, scalar2=None,
                                              op0=ALU.mult), reads=["sm"], writes=["sm"])

        ada_state = {"n": 0}
        cond_b = sb("cond_b", [128, 8], BF16)
        S.op("dve", lambda e: e.tensor_copy(out=cond_b[:], in_=cond[:]), reads=["cond"], writes=["cond_b"])

        def ada_piece(i, j0, ncol, stages, pmod):
            n = ada_state["n"]
            ada_state["n"] += 1
            slot = n % len(stages)
            stg = stages[slot]
            S.dma("pool", stg[:, :, 0:ncol], ada_w[i][:, j0 * 128:j0 * 128 + ncol].rearrange("(kc p) n -> p kc n", p=128),
                  f"ada{slot}", writes=[f"adast{slot}"])
            for m in range(ncol // 128):
                j = j0 + m
                for kc in range(KC):
                    S.op("pe", lambda e, j=j, m=m, kc=kc: e.matmul(pmod[:, j:j + 1], lhsT=stg[:, kc, m * 128:(m + 1) * 128],
                                                                    rhs=cond_b[:, kc:kc + 1], start=(kc == 0), stop=(kc == 7)),
                         reads=[f"adast{slot}", "cond_b"], writes=["pb5"])

        def ada_finish(i, pmod):
            S.op("dve", lambda e: e.tensor_tensor(out=modT[i][:], in0=pmod[:, 0:48],
                                                  in1=sm[:, _SM["adab"] + 48 * i:_SM["adab"] + 48 * i + 48], op=ALU.add),
                 reads=["pb5", "sm"], writes=[f"modT{i}"])
            for (dst, goff, scol, nm) in ((aM[i], _SM["mixg"] + 8 * i, 8, "aM"), (aF[i], _SM["mlpg"] + 8 * i, 32, "aF")):
                S.op("dve", lambda e, dst=dst, goff=goff, scol=scol: e.scalar_tensor_tensor(
                    out=dst[:], in0=modT[i][:, scol:scol + 8], scalar=1.0, in1=sm[:, goff:goff + 8],
                    op0=ALU.add, op1=ALU.mult), reads=[f"modT{i}", "sm"], writes=[f"{nm}{i}"])

        def norm_stats(tt, pst_ap, pst_key, ndiv):
            tsl = slice(tt * 512, (tt + 1) * 512)
            for kc in range(KC):
                sq = nsq[kc % 2]
                S.op("act", lambda e, sq=sq, kc=kc: e.activation(out=sq[:], in_=xT[:, kc, tsl], func=AF.Square),
                     reads=xkeys512(kc, tt), writes=[f"nsq{kc % 2}"])
                S.op("pe", lambda e, sq=sq, kc=kc: e.matmul(pst_ap, lhsT=ones_f[:], rhs=sq[:], start=(kc == 0), stop=(kc == 7)),
                     reads=[f"nsq{kc % 2}", "ones_f"], writes=[pst_key])
            S.op("act", lambda e: e.activation(out=nrstd[tt][:], in_=pst_ap, func=AF.Sqrt, bias=eps_t[:, 0:1], scale=1.0 / ndiv),
                 reads=[pst_key, "eps"], writes=[f"nrstd{tt}"])
            S.op("dve", lambda e: e.reciprocal(out=nrstd[tt][:], in_=nrstd[tt][:]), reads=[f"nrstd{tt}"], writes=[f"nrstd{tt}"])

        eps_t = sb("eps_t", [128, 1], F32)
        S.op("dve", lambda e: e.memset(eps_t[:], EPS), writes=["eps"])
        eps4_t = sb("eps4_t", [128, 1], F32)
        S.op("dve", lambda e: e.memset(eps4_t[:], 4.0 * EPS), writes=["eps"])
        one_t = sb("one_t", [128, 1], F32)
        S.op("dve", lambda e: e.memset(one_t[:], 1.0), writes=["one"])

        def rmsnorm_mod(a_t, a_key, mod_t, mod_key, shcol, stats=True):
            if stats:
                for tt in range(4):
                    norm_stats(tt, pbank[tt % 2][:], f"pb{tt % 2}", float(D))
            for tt in range(4):
                rmsnorm_mod_tile(a_t, a_key, mod_t, mod_key, shcol, tt)

        def rmsnorm_mod_tile(a_t, a_key, mod_t, mod_key, shcol, tt):
            if True:
                tsl = slice(tt * 512, (tt + 1) * 512)
                for kc in range(KC):
                    tmp = ntmp[kc % 2]
                    S.op("dve", lambda e, tmp=tmp, kc=kc: e.scalar_tensor_tensor(
                        out=tmp[:], in0=xT[:, kc, tsl], scalar=a_t[:, kc:kc + 1], in1=nrstd[tt][:],
                        op0=ALU.mult, op1=ALU.mult),
                        reads=xkeys512(kc, tt) + [f"nrstd{tt}", a_key], writes=[f"ntmp{kc % 2}"])
                    S.op("act", lambda e, tmp=tmp, kc=kc: e.activation(
                        out=hT[:, kc, tsl], in_=tmp[:], func=AF.Identity,
                        bias=mod_t[:, shcol + kc:shcol + kc + 1], scale=1.0),
                        reads=[f"ntmp{kc % 2}", mod_key], writes=[hk(kc, tt)])

        def wslot(k):
            return Wreg[:, k * 4096:(k + 1) * 4096]

        wctr = {"n": 0}

        def load_wtile(src_ap_3d):
            k = wctr["n"] % 4
            wctr["n"] += 1
            dst = wslot(k).rearrange("p (a b) -> p a b", b=512)
            S.dma("pool", dst, src_ap_3d, f"w{k}", writes=[f"wslot{k}"])
            return dst, f"wslot{k}"

        def evac_x(psum_ap, pkey, gate_ap, gkey, dc, t8list, tsl):
            keys = [xk(dc, t) for t in t8list]
            S.op("dve", lambda e: e.scalar_tensor_tensor(out=xT[:, dc, tsl], in0=psum_ap, scalar=gate_ap,
                                                         in1=xT[:, dc, tsl], op0=ALU.mult, op1=ALU.add),
                 reads=[pkey, gkey] + keys, writes=keys)

        def mlp(i, extra=None):
            cF.reset()
            cB.reset()
            aT = cB.get(8 * L).rearrange("p (j t) -> p j t", t=L)
            rtmp = [cF.get(512) for _ in range(3)]
            nb = [0, 1, 2, 3, 4] if extra is not None else [0, 1, 2, 3, 4, 5, 6]
            prot = Rot([(pbank[b][:], f"pb{b}") for b in nb])
            rrot = Rot(list(range(3)))
            sq_eng = Rot(["dve"])
            gate = lambda dc: modT[i][:, 40 + dc:41 + dc]
            for blk in range(4):
                wu = [load_wtile(mlp_up[i][:, blk * 1024 + hf * 512: blk * 1024 + (hf + 1) * 512]
                                 .rearrange("(kc p) n -> p kc n", p=128)) for hf in range(2)]
                wd = [load_wtile(mlp_down[i][blk * 1024:(blk + 1) * 1024, hf * 512:(hf + 1) * 512]
                                 .rearrange("(j p) n -> p j n", p=128)) for hf in range(2)]
                for j in range(8):
                    if extra is not None:
                        for _ in range(2):
                            nxt = next(extra, None)
                            if nxt is not None:
                                nxt()
                    wt, wkey = wu[j // 4]
                    col = (j % 4) * 128
                    for tt in range(4):
                        tsl = slice(tt * 512, (tt + 1) * 512)
                        pp, pkey = prot.next()
                        for kc in range(KC):
                            S.op("pe", lambda e, pp=pp, wt=wt, kc=kc, col=col, tsl=tsl: e.matmul(
                                pp, lhsT=wt[:, kc, col:col + 128], rhs=hT[:, kc, tsl], start=(kc == 0), stop=(kc == 7)),
                                reads=[wkey, hk(kc, tt)], writes=[pkey])
                        r = rrot.next()
                        S.op("act", lambda e, pp=pp, r=r: e.activation(out=rtmp[r], in_=pp, func=AF.Relu),
                             reads=[pkey], writes=[f"rtmp{r}"])
                        S.op(sq_eng.next(), lambda e, r=r, j=j, tsl=tsl: e.tensor_tensor(
                            out=aT[:, j, tsl], in0=rtmp[r], in1=rtmp[r], op=ALU.mult),
                            reads=[f"rtmp{r}"], writes=[f"aT{j}_{tt}"])
                order = [(dc, tt) for dc in range(8) for tt in range(4)] if blk < 3 else \
                        [(dc, tt) for tt in range(4) for dc in range(8)]
                for (dc, tt) in order:
                    wt, wkey = wd[dc // 4]
                    col = (dc % 4) * 128
                    if True:
                        tsl = slice(tt * 512, (tt + 1) * 512)
                        pp, pkey = prot.next()
                        for j in range(8):
                            S.op("pe", lambda e, pp=pp, wt=wt, j=j, col=col, tsl=tsl: e.matmul(
                                pp, lhsT=wt[:, j, col:col + 128], rhs=aT[:, j, tsl], start=(j == 0), stop=(j == 7)),
                                reads=[wkey, f"aT{j}_{tt}"], writes=[pkey])
                        evac_x(pp, pkey, gate(dc), f"modT{i}", dc, [2 * tt, 2 * tt + 1], tsl)

        def ssd():
            cF.reset()
            cB.reset()
            TT = 256
            v3 = lambda ap: ap.rearrange("p (c h) -> p c h", h=NH)
            dt_t = cF.get(512)
            nacs_t = cF.get(512)
            dd_t = cF.get(512)
            cd_t = cF.get(512)
            t_u = cF.get(512)
            t_a = cF.get(512)
            t_l = cF.get(512)
            expA = cF.get(32)
            raw = [cF.get(264) for _ in range(4)]
            acc = [cF.get(256) for _ in range(4)]
            _r0 = 4 * 512 + 512
            raw += [arF[:, _r0:_r0 + 264], arF[:, _r0 + 264:_r0 + 528]]
            acc += [arF[:, _r0 + 528:_r0 + 784], arF[:, _r0 + 784:_r0 + 1040]]
            tmpE = [cF.get(256) for _ in range(4)]
            EA = [cF.get(256) for _ in range(3)]
            prev = cF.get(512)
            yv = [cF.get(256) for _ in range(2)]
            yg = [cF.get(256) for _ in range(4)]
            ssq = cF.get(256)
            grt = cF.get(256)
            hist = cF.get(24).rearrange("p (q k) -> p q k", k=4)
            xbc = [[cB.get(256) for _ in range(6)] for _ in range(2)]
            Xdt = [cB.get(512) for _ in range(2)]
            Xdd = [cB.get(512) for _ in range(2)]
            Btok = cB.get(256)
            acs2 = cB.get(256)
            acs_t32 = cB.get(256)
            MT = [cB.get(256) for _ in range(4)]
            Cs = [cB.get(256) for _ in range(4)]
            prevb = [cB.get(512) for _ in range(2)]
            ynorm = [cB.get(256) for _ in range(4)]
            Wdt = cB.get(256).rearrange("p (k h) -> p k h", h=NH)
            maskb = cB.get(256)
            f32v = lambda n: cB.get(2 * n).bitcast(F32)
            zs = [[f32v(256) for _ in range(4)] for _ in range(2)]
            sq = [f32v(256) for _ in range(2)]
            grstd = f32v(256)
            Wg = Wreg[:, 0:10240].rearrange("p (k n) -> p k n", n=1280)
            Wo = Wreg[:, 12288:16384].rearrange("p (k n) -> p k n", n=1024)
            wgkey = lambda col: "wg_z" if col < 512 else ("wg_x" if col < 1024 else ("wg_B" if col < 1152 else "wg_C"))
            WO_KEYS = ["wslot3"]
            big = Rot([(pbank[0][:, 0:256], "pb0"), (pbank[1][:, 0:256], "pb1"),
                       (pbank[0][:, 256:512], "pb0"), (pbank[1][:, 256:512], "pb1")])
            pplain = Rot([(pbank[2][:, 0:256], "pb2"), (pbank[2][:, 256:512], "pb2")])
            pmask = Rot([(pbank[6][:, 0:256], "pb6"), (pbank[6][:, 256:512], "pb6")])
            psc, psc_k = pbank[3][:, 0:256], "pb3"
            pacsT, pacsT_k = pbank[3][0:8, 256:512], "pb3"
            pacsT32 = pbank[3][32:40, 256:512]
            pyr = Rot([(pbank[4], 0, "pb4"), (pbank[4], 256, "pb4")])
            pst, pst_k = pbank[5][:, :], "pb5"
            pxt0 = pbf[0][:, 0:512]
            pxt1 = pbank[6][:, 0:256].bitcast(BF16)
            pbtr, pbtr_k = pbf[0][:, 512:768], "pbf0"
            gate = lambda dc: modT[0][:, 16 + dc:17 + dc]

            S.dma("pool", Wdt, ssd_in_w[:, 5120:5152].rearrange("(kc p) n -> p kc n", p=128), "wdt", writes=["Wdt"])
            S.op("dve", lambda e: e.tensor_copy(out=maskb[:, 0:128], in_=maskneg[:]), reads=["maskneg"], writes=["maskb"])
            S.op("dve", lambda e: e.tensor_copy(out=maskb[:, 128:256], in_=maskneg[:]), reads=["maskneg"], writes=["maskb"])
            S.op("dve", lambda e: e.memset(acs2[0:64, :], 0.0), reads=["acs2"], writes=["acs2"])
            pdt = pbank[0][:, :]
            for c in range(16):
                for kc in range(KC):
                    S.op("pe", lambda e, c=c, kc=kc: e.matmul(pdt[:, c * 32:(c + 1) * 32],
                                                              lhsT=hT[:, kc, c * 128:(c + 1) * 128], rhs=Wdt[:, kc, :],
                                                              start=(kc == 0), stop=(kc == 7)),
                         reads=["Wdt", hk(kc, c // 4)], writes=["pb0"])
            dtb = sm[:, _SM["dtb"]:_SM["dtb"] + 32]
            S.op("dve", lambda e: e.tensor_tensor(out=v3(t_u), in0=v3(pdt), in1=dtb.unsqueeze(1).to_broadcast([128, 16, 32]),
                                                  op=ALU.add), reads=["pb0", "sm"], writes=["t_u"])
            S.op("dve", lambda e: e.scalar_tensor_tensor(out=t_a, in0=t_u, scalar=-1.0, in1=t_u, op0=ALU.mult, op1=ALU.min),
                 reads=["t_u"], writes=["t_a"])
            S.op("act", lambda e: e.activation(out=t_a, in_=t_a, func=AF.Exp), reads=["t_a"], writes=["t_a"])
            S.op("act", lambda e: e.activation(out=t_l, in_=t_a, func=AF.Ln, bias=one_t[:, 0:1], scale=1.0),
                 reads=["t_a", "one"], writes=["t_l"])
            S.op("dve", lambda e: e.scalar_tensor_tensor(out=dt_t, in0=t_u, scalar=0.0, in1=t_l, op0=ALU.max, op1=ALU.add),
                 reads=["t_u", "t_l"], writes=["dt"])
            S.op("act", lambda e: e.activation(out=expA, in_=sm[:, _SM["alog"]:_SM["alog"] + 32], func=AF.Exp),
                 reads=["sm"], writes=["expA"])
            dtA = t_u
            S.op("dve", lambda e: e.scalar_tensor_tensor(out=v3(dtA), in0=v3(dt_t), scalar=-1.0,
                                                         in1=expA.unsqueeze(1).to_broadcast([128, 16, 32]),
                                                         op0=ALU.mult, op1=ALU.mult),
                 reads=["dt", "expA", "t_u"], writes=["dtA"])
            pacs = pbank[1][:, :]
            plast = pbank[2][:, :]
            for c in range(16):
                S.op("pe", lambda e, c=c: e.matmul(pacs[:, c * 32:(c + 1) * 32], lhsT=triu[:], rhs=v3(dtA)[:, c, :],
                                                   start=True, stop=True), reads=["triu", "dtA"], writes=["pb1"])
            for c in range(16):
                S.op("pe", lambda e, c=c: e.matmul(plast[:, c * 32:(c + 1) * 32], lhsT=ones_f[:], rhs=v3(dtA)[:, c, :],
                                                   start=True, stop=True), reads=["ones_f", "dtA"], writes=["pb2"])
            S.op("act", lambda e: e.activation(out=nacs_t, in_=pacs, func=AF.Identity, scale=-1.0), reads=["pb1"], writes=["nacs"])
            S.op("act", lambda e: e.activation(out=cd_t, in_=plast, func=AF.Exp), reads=["pb2"], writes=["cd"])
            S.op("dve", lambda e: e.tensor_tensor(out=t_l, in0=plast, in1=nacs_t, op=ALU.add),
                 reads=["pb2", "nacs", "t_l"], writes=["t_l2"])
            S.op("act", lambda e: e.activation(out=t_l, in_=t_l, func=AF.Exp), reads=["t_l2"], writes=["t_l2"])
            S.op("dve", lambda e: e.tensor_tensor(out=dd_t, in0=t_l, in1=dt_t, op=ALU.mult),
                 reads=["t_l2", "dt"], writes=["dd"])

            S.barrier()

            def load_wg(g):
                S.dma("pool", Wg[:, :, 0:512], ssd_in_w[:, g * 512:(g + 1) * 512].rearrange("(kc p) n -> p kc n", p=128),
                      "wg0", writes=["wg_z"])
                S.dma("pool", Wg[:, :, 512:1024],
                      ssd_in_w[:, 2048 + g * 512:2048 + (g + 1) * 512].rearrange("(kc p) n -> p kc n", p=128),
                      "wg1", writes=["wg_x"])
                S.dma("pool", Wg[:, :, 1024:1152],
                      ssd_in_w[:, 4096 + g * 128:4096 + (g + 1) * 128].rearrange("(kc p) n -> p kc n", p=128),
                      "wg2", writes=["wg_B"])
                S.dma("pool", Wg[:, :, 1152:1280],
                      ssd_in_w[:, 4608 + g * 128:4608 + (g + 1) * 128].rearrange("(kc p) n -> p kc n", p=128),
                      "wg3", writes=["wg_C"])
                S.op("pool", lambda e: e.memset(hist, 0.0), reads=["hist"], writes=["hist"])

            def load_wo(g):
                S.dma("pool", Wo, ssd_out_w[g * 512:(g + 1) * 512, :].rearrange("(kc p) n -> p kc n", p=128),
                      "wo", writes=WO_KEYS)

            rawrot = Rot(list(range(6)))

            def a_tasks(g, t, par):
                tsl = slice(t * TT, (t + 1) * TT)
                htt = t // 2
                convch = [4 * g + q for q in range(4)] + [16 + g, 20 + g]
                wcol = [512 + q * 128 for q in range(4)] + [1024, 1152]

                def proj(col):
                    pp, pkey = big.next()
                    for kc in range(KC):
                        S.op("pe", lambda e, pp=pp, kc=kc, col=col: e.matmul(
                            pp, lhsT=Wg[:, kc, col:col + 128], rhs=hT[:, kc, tsl],
                            start=(kc == 0), stop=(kc == 7)), reads=[wgkey(col), hk(kc, htt)], writes=[pkey])
                    return pp, pkey

                def conv_pair(qs):
                    st = []
                    for q in qs:
                        pp, pkey = proj(wcol[q])
                        ri = rawrot.next()
                        rb, ab = raw[ri], acc[ri]
                        ch = convch[q]
                        wof = _SM["convw"] + ch * 4
                        S.op("pool", lambda e, rb=rb, q=q: e.tensor_copy(out=rb[:, 0:3], in_=hist[:, q, 0:3]),
                             reads=["hist", f"raw{ri}"], writes=[f"raw{ri}"])
                        S.op("act", lambda e, rb=rb, pp=pp: e.activation(out=rb[:, 3:259], in_=pp, func=AF.Copy),
                             reads=[pkey, f"raw{ri}"], writes=[f"raw{ri}"])
                        S.op("act", lambda e, ab=ab, pp=pp, wof=wof, ch=ch: e.activation(
                            out=ab, in_=pp, func=AF.Identity, scale=sm[:, wof + 3:wof + 4],
                            bias=sm[:, _SM["convb"] + ch:_SM["convb"] + ch + 1]),
                            reads=[pkey, "sm", f"acc{ri}"], writes=[f"acc{ri}"])
                        S.op("pool", lambda e, rb=rb, q=q: e.tensor_copy(out=hist[:, q, 0:3], in_=rb[:, 256:259]),
                             reads=[f"raw{ri}", "hist"], writes=["hist"])
                        st.append((q, ri, rb, ab, wof))
                    for k in (2, 1, 0):
                        for (q, ri, rb, ab, wof) in st:
                            S.op("dve", lambda e, ab=ab, rb=rb, wof=wof, k=k: e.scalar_tensor_tensor(
                                out=ab, in0=rb[:, k:k + 256], scalar=sm[:, wof + k:wof + k + 1], in1=ab,
                                op0=ALU.mult, op1=ALU.add), reads=[f"raw{ri}", "sm", f"acc{ri}"], writes=[f"acc{ri}"])
                    for (q, ri, rb, ab, wof) in st:
                        S.op("act", lambda e, ab=ab, rb=rb: e.activation(out=rb[:, 0:256], in_=ab, func=AF.Tanh),
                             reads=[f"acc{ri}", f"raw{ri}"], writes=[f"raw{ri}"])
                    for (q, ri, rb, ab, wof) in st:
                        S.op("dve", lambda e, ab=ab, rb=rb, q=q: e.scalar_tensor_tensor(
                            out=xbc[par][q], in0=rb[:, 0:256], scalar=1.0, in1=ab, op0=ALU.add, op1=ALU.mult),
                            reads=[f"acc{ri}", f"raw{ri}"], writes=[f"xbc{par}_{q}"])

                def z_pair(fcs):
                    for fc in fcs:
                        pz, pzk = proj(fc * 128)
                        S.op("act", lambda e, pz=pz, fc=fc: e.activation(out=zs[par][fc], in_=pz, func=AF.Tanh, scale=0.5),
                             reads=[pzk, f"zs{par}_{fc}"], writes=[f"zs{par}_{fc}"])
                        S.op("dve", lambda e, pz=pz, fc=fc: e.scalar_tensor_tensor(
                            out=zs[par][fc], in0=zs[par][fc], scalar=1.0, in1=pz, op0=ALU.add, op1=ALU.mult),
                            reads=[pzk, f"zs{par}_{fc}"], writes=[f"zs{par}_{fc}"])

                convs = [(lambda q=q: conv_pair([q])) for q in range(6)]
                zsl = [(lambda fc=fc: z_pair([fc])) for fc in range(4)]
                return convs, zsl

            def b_s2(g, t, par, step):
                xb = xbc[par]
                xkey = lambda q: f"xbc{par}_{q}"
                if step == 0:
                    for c2 in range(2):
                        csl = slice(c2 * 128, (c2 + 1) * 128)
                        S.op("pe", lambda e, csl=csl: e.matmul(psc[:, csl], lhsT=xb[4][:, csl], rhs=xb[5][:, csl],
                                                               start=True, stop=True),
                             reads=[xkey(4), xkey(5)], writes=[psc_k])
                    for c2 in range(2):
                        c = 2 * t + c2
                        S.op("pe", lambda e, c=c, c2=c2: e.matmul(pacsT[:, c2 * 128:(c2 + 1) * 128],
                                                                  lhsT=v3(dtA)[:, c, 8 * g:8 * g + 8], rhs=triu[:],
                                                                  start=True, stop=True),
                             reads=["dtA", "triu"], writes=[pacsT_k])
                    for c2 in range(2):
                        c = 2 * t + c2
                        S.op("pe", lambda e, c=c, c2=c2: e.matmul(pacsT32[:, c2 * 128:(c2 + 1) * 128],
                                                                  lhsT=v3(dtA)[:, c, 8 * g:8 * g + 8], rhs=triu[:],
                                                                  start=True, stop=True, tile_position=(0, 32)),
                             reads=["dtA", "triu"], writes=[pacsT_k])
                    S.op("act", lambda e: e.activation(out=acs2[0:8, :], in_=pacsT, func=AF.Copy),
                         reads=[pacsT_k, "acs2"], writes=["acs2"])
                    S.op("act", lambda e: e.activation(out=acs_t32[32:40, :], in_=pacsT32, func=AF.Copy),
                         reads=[pacsT_k, "acs_t32"], writes=["acs_t32"])
                    S.op("dve", lambda e: e.tensor_tensor(out=acs2[32:40, :], in0=pacsT32, in1=acs_t32[32:40, :],
                                                          op=ALU.subtract),
                         reads=[pacsT_k, "acs_t32", "acs2"], writes=["acs2"])
                    return
                c2 = step - 1
                c = 2 * t + c2
                csl = slice(c2 * 128, (c2 + 1) * 128)
                pxt, pxt_k = (pxt0, "pbf0") if c2 == 0 else (pxt1, "pb6")
                for q in range(4):
                    S.op("pe", lambda e, q=q: e.transpose(
                        out=pxt[:, q * 128:(q + 1) * 128], in_=xb[q][:, csl], identity=ident_b[:]),
                        reads=[xkey(q), "ident_b"], writes=[pxt_k])
                if c2 == 0:
                    for cc in range(2):
                        ccsl = slice(cc * 128, (cc + 1) * 128)
                        S.op("pe", lambda e, ccsl=ccsl: e.transpose(out=pbtr[:, ccsl], in_=xb[4][:, ccsl], identity=ident_b[:]),
                             reads=[xkey(4), "ident_b"], writes=[pbtr_k])
                S.op("dve", lambda e: e.tensor_tensor(
                    out=Xdt[c2].rearrange("p (h d) -> p h d", d=64), in0=pxt.rearrange("p (h d) -> p h d", d=64),
                    in1=v3(dt_t)[:, c, 8 * g:8 * g + 8].unsqueeze(2).to_broadcast([128, 8, 64]), op=ALU.mult),
                    reads=[pxt_k, "dt", f"Xdt{c2}"], writes=[f"Xdt{c2}"])
                S.op("dve", lambda e: e.tensor_tensor(
                    out=Xdd[c2].rearrange("p (h d) -> p h d", d=64), in0=pxt.rearrange("p (h d) -> p h d", d=64),
                    in1=v3(dd_t)[:, c, 8 * g:8 * g + 8].unsqueeze(2).to_broadcast([128, 8, 64]), op=ALU.mult),
                    reads=[pxt_k, "dd", f"Xdd{c2}"], writes=[f"Xdd{c2}"])
                if c2 == 0:
                    S.op("act", lambda e: e.activation(out=Btok, in_=pbtr, func=AF.Copy),
                         reads=[pbtr_k, "Btok"], writes=["Btok"])

            def b_s3a(g, t, c2):
                if True:
                    c = 2 * t + c2
                    csl = slice(c2 * 128, (c2 + 1) * 128)
                    S.op("act", lambda e, c2=c2: e.activation(out=prevb[c2], in_=prev, func=AF.Copy),
                         reads=["prev", f"prevb{c2}"], writes=[f"prevb{c2}"])
                    pst, pst_k = (pbank[4][:, :], "pb4") if c2 == 0 else (pbank[5][:, :], "pb5")
                    S.op("pe", lambda e, c2=c2, csl=csl, pst=pst: e.matmul(pst, lhsT=Btok[:, csl], rhs=Xdd[c2], start=True, stop=True),
                         reads=["Btok", f"Xdd{c2}"], writes=[pst_k])
                    S.op("pool", lambda e, c=c: e.tensor_tensor(
                        out=prev.rearrange("p (h d) -> p h d", d=64), in0=prev.rearrange("p (h d) -> p h d", d=64),
                        in1=v3(cd_t)[:, c, 8 * g:8 * g + 8].unsqueeze(2).to_broadcast([128, 8, 64]), op=ALU.mult),
                        reads=["prev", "cd"], writes=["prev"])
                    S.op("dve", lambda e, pst=pst: e.tensor_tensor(out=prev, in0=prev, in1=pst, op=ALU.add),
                         reads=["prev", pst_k], writes=["prev"])

            earot = Rot(list(range(3)))

            def b_head_pre(g, t, h, par):
                hh = 8 * g + h
                mi = h % 4
                xb5, xk5 = xbc[par][5], f"xbc{par}_5"
                pa, pak = pplain.next()
                pm, pmk = pmask.next()
                for (pt, ptk, msk) in ((pa, pak, False), (pm, pmk, True)):
                    S.op("pe", lambda e, pt=pt, h=h, msk=msk: e.matmul(pt, lhsT=sel8[:, h, :], rhs=acs2[0:40, :],
                                                                       start=True, stop=(not msk)),
                         reads=["sel8", "acs2"], writes=[ptk])
                    if msk:
                        S.op("pe", lambda e, pt=pt: e.matmul(pt, lhsT=ident_b[:], rhs=maskb, start=False, stop=True),
                             reads=["ident_b", "maskb"], writes=[ptk])
                ei = earot.next()
                ea = EA[ei]
                S.op("act", lambda e, ea=ea, pa=pa: e.activation(out=ea, in_=pa, func=AF.Exp),
                     reads=[pak, f"EA{ei}"], writes=[f"EA{ei}"])
                te = tmpE[mi]
                for c2 in range(2):
                    c = 2 * t + c2
                    csl = slice(c2 * 128, (c2 + 1) * 128)
                    S.op("act", lambda e, te=te, pm=pm, csl=csl, c=c, hh=hh: e.activation(
                        out=te[:, csl], in_=pm[:, csl], func=AF.Exp, bias=v3(nacs_t)[:, c, hh:hh + 1], scale=1.0),
                        reads=[pmk, "nacs", f"tmpE{mi}"], writes=[f"tmpE{mi}"])
                S.op("dve", lambda e, te=te, mi=mi: e.tensor_tensor(out=MT[mi], in0=te, in1=psc, op=ALU.mult),
                     reads=[f"tmpE{mi}", psc_k, f"MT{mi}"], writes=[f"MT{mi}"])
                S.op("pool", lambda e, ea=ea, mi=mi: e.tensor_tensor(out=Cs[mi], in0=xb5, in1=ea, op=ALU.mult),
                     reads=[f"EA{ei}", xk5, f"Cs{mi}"], writes=[f"Cs{mi}"])

            def b_head_y(g, t, fc, par):
                pyb, pyo, pyk = pyr.next()
                for c2 in range(2):
                    csl = slice(c2 * 128, (c2 + 1) * 128)
                    for hp in range(2):
                        h = 2 * fc + hp
                        mi = h % 4
                        outap = pyb[hp * 64:(hp + 1) * 64, pyo + c2 * 128:pyo + (c2 + 1) * 128]
                        tp = (0, 64) if hp == 1 else None
                        S.op("pe", lambda e, outap=outap, c2=c2, h=h, mi=mi, csl=csl, tp=tp: e.matmul(
                            outap, lhsT=Xdt[c2][:, h * 64:(h + 1) * 64], rhs=MT[mi][:, csl],
                            start=True, stop=False, tile_position=tp),
                            reads=[f"Xdt{c2}", f"MT{mi}"], writes=[pyk])
                        S.op("pe", lambda e, outap=outap, c2=c2, h=h, mi=mi, csl=csl, tp=tp: e.matmul(
                            outap, lhsT=prevb[c2][:, h * 64:(h + 1) * 64], rhs=Cs[mi][:, csl],
                            start=False, stop=True, tile_position=tp),
                            reads=[f"prevb{c2}", f"Cs{mi}"], writes=[pyk])
                pyap = pyb[:, pyo:pyo + 256]
                dcol = _SM["dfeat"] + 4 * g + fc
                yvb = yv[fc % 2]
                S.op("dve", lambda e, pyap=pyap, dcol=dcol, yvb=yvb: e.scalar_tensor_tensor(
                    out=yvb, in0=xbc[par][fc], scalar=sm[:, dcol:dcol + 1], in1=pyap, op0=ALU.mult, op1=ALU.add),
                    reads=[pyk, f"xbc{par}_{fc}", "sm", f"yv{fc % 2}"], writes=[f"yv{fc % 2}"])
                S.op("dve", lambda e, yvb=yvb: e.tensor_tensor(out=yg[fc], in0=yvb, in1=zs[par][fc], op=ALU.mult),
                     reads=[f"yv{fc % 2}", f"zs{par}_{fc}", f"yg{fc}"], writes=[f"yg{fc}"])
                if fc == 0:
                    S.op("act", lambda e: e.activation(out=ssq, in_=yg[0], func=AF.Square),
                         reads=["yg0", "ssq"], writes=["ssq"])
                else:
                    sqb = sq[fc % 2]
                    S.op("act", lambda e, sqb=sqb: e.activation(out=sqb, in_=yg[fc], func=AF.Square),
                         reads=[f"yg{fc}", f"sq{fc % 2}"], writes=[f"sq{fc % 2}"])
                    if fc < 3:
                        S.op("pool", lambda e, sqb=sqb: e.tensor_tensor(out=ssq, in0=ssq, in1=sqb, op=ALU.add),
                             reads=[f"sq{fc % 2}", "ssq"], writes=["ssq"])

            s4state = {}

            def b_s4(g, t):
                pn, pnk = big.next()
                S.op("pe", lambda e, pn=pn: e.matmul(pn, lhsT=ones_f[:], rhs=ssq, start=True, stop=False),
                     reads=["ones_f", "ssq"], writes=[pnk])
                S.op("pe", lambda e, pn=pn: e.matmul(pn, lhsT=ones_f[:], rhs=sq[1], start=False, stop=True),
                     reads=["ones_f", "sq1"], writes=[pnk])
                s4state["p"] = (pn, pnk, g)

            def b_s4e():
                pn, pnk, g = s4state.pop("p")
                S.op("act", lambda e, pn=pn: e.activation(out=grt, in_=pn, func=AF.Sqrt, bias=eps4_t[:, 0:1], scale=1.0 / 512.0),
                     reads=[pnk, "eps", "grt"], writes=["grt"])
                S.op("dve", lambda e: e.reciprocal(out=grstd, in_=grt), reads=["grt", "grstd"], writes=["grstd"])
                for fc in range(4):
                    ncol = _SM["nw"] + 4 * g + fc
                    S.op("dve", lambda e, fc=fc, ncol=ncol: e.scalar_tensor_tensor(
                        out=ynorm[fc], in0=yg[fc], scalar=sm[:, ncol:ncol + 1], in1=grstd, op0=ALU.mult, op1=ALU.mult),
                        reads=[f"yg{fc}", "sm", "grstd", f"ynorm{fc}"], writes=[f"ynorm{fc}"])

            def b_s4b(g, t, dc):
                tsl = slice(t * TT, (t + 1) * TT)
                if True:
                    pp, pkey = big.next()
                    for kc4 in range(4):
                        S.op("pe", lambda e, pp=pp, kc4=kc4, dc=dc: e.matmul(
                            pp, lhsT=Wo[:, kc4, dc * 128:(dc + 1) * 128], rhs=ynorm[kc4],
                            start=(kc4 == 0), stop=(kc4 == 3)), reads=WO_KEYS + [f"ynorm{kc4}"], writes=[pkey])
                    evac_x(pp, pkey, gate(dc), "modT0", dc, [t], tsl)

            iters = [(g, t) for g in range(4) for t in range(8)]
            load_wg(0)
            load_wo(0)
            cv0, zs0 = a_tasks(0, 0, 0)
            for task in cv0 + zs0:
                task()
            for idx, (g, t) in enumerate(iters):
                par = idx % 2
                convs, zsl, outs = [], [], []
                if idx + 1 < len(iters):
                    ng, nt = iters[idx + 1]
                    if nt == 0:
                        load_wg(ng)
                    convs, zsl = a_tasks(ng, nt, 1 - par)
                if idx > 0:
                    pg, pt = iters[idx - 1]
                    outs = [(lambda dc=dc, pg=pg, pt=pt: b_s4b(pg, pt, dc)) for dc in range(8)]
                convs, zsl, outs = iter(convs), iter(zsl), iter(outs)

                def fill(kinds):
                    for kd in kinds:
                        f = next({"c": convs, "z": zsl, "o": outs}[kd], None)
                        if f is not None:
                            f()
                if t == 0:
                    S.op("pool", lambda e: e.memset(prev, 0.0), reads=["prev"], writes=["prev"])
                b_s2(g, t, par, 0)
                if "p" in s4state:
                    b_s4e()
                fill("c")
                b_s2(g, t, par, 1)
                fill("cz")
                b_s2(g, t, par, 2)
                fill("cz")
                b_s3a(g, t, 0)
                fill("czo")
                b_s3a(g, t, 1)
                fill("czo")
                plan = ["o", "c", "o", "o", "o", "o", "o", ""]
                for h in range(8):
                    b_head_pre(g, t, h, par)
                    if h >= 2 and h % 2 == 0:
                        b_head_y(g, t, h // 2 - 1, par)
                    fill(plan[h])
                b_head_y(g, t, 3, par)
                fill("cccccczzzz")
                b_s4(g, t)
                fill("oooooooo")
                if idx > 0 and iters[idx - 1][1] == 7:
                    load_wo(g)
            b_s4e()
            for dc in range(8):
                b_s4b(iters[-1][0], iters[-1][1], dc)

        sc_wts = {}

        def load_sc(j):
            k = wctr["n"] % 4
            wctr["n"] += 1
            wt = wslot(k)[:, 0:3072].rearrange("p (kc th f) -> p kc th f", th=3, f=128)
            wkey = f"wslot{k}"
            for th in range(3):
                S.dma("pool", wt[:, :, th, :],
                      sc_in_w[:, th * 1024 + j * 128: th * 1024 + (j + 1) * 128].rearrange("(kc p) f -> p kc f", p=128),
                      f"w{k}", writes=[wkey])
            sc_wts[j] = (wt, wkey)

        def shortconv():
            cF.reset()
            cB.reset()
            yT = cB.get(8 * L).rearrange("p (j t) -> p j t", t=L)
            xv = [cF.get(512) for _ in range(2)]
            ub = [cF.get(516) for _ in range(2)]
            vb = [cF.get(512) for _ in range(2)]
            prot = Rot([(pbank[b][:], f"pb{b}") for b in range(7)])
            gate = lambda dc: modT[1][:, 16 + dc:17 + dc]
            n = 0
            wts = sc_wts
            for j in range(3):
                if j not in wts:
                    load_sc(j)
            wo = []
            for j in range(8):
                if j + 3 < 8:
                    load_sc(j + 3)
                elif len(wo) < 2:
                    hf = len(wo)
                    wo.append(load_wtile(sc_out_w[:, hf * 512:(hf + 1) * 512].rearrange("(kc p) n -> p kc n", p=128)))
                wt, wkey = wts[j]
                for tt in range(4):
                    tsl = slice(tt * 512, (tt + 1) * 512)
                    pps = []
                    for th in range(3):
                        pp, pkey = prot.next()
                        pps.append((pp, pkey))
                        for kc in range(KC):
                            S.op("pe", lambda e, pp=pp, kc=kc, th=th, wt=wt, tsl=tsl: e.matmul(
                                pp, lhsT=wt[:, kc, th, :], rhs=hT[:, kc, tsl], start=(kc == 0), stop=(kc == 7)),
                                reads=[wkey, hk(kc, tt)], writes=[pkey])
                    (pB, pBk), (pC, pCk), (pX, pXk) = pps
                    bi = n % 2
                    n += 1
                    u, uprev, v, xvb = ub[bi], ub[1 - bi], vb[bi], xv[bi]
                    S.op("act", lambda e, xvb=xvb, pX=pX: e.activation(out=xvb, in_=pX, func=AF.Copy),
                         reads=[pXk, f"xv{bi}"], writes=[f"xv{bi}"])
                    if tt == 0:
                        S.op("dve", lambda e, u=u: e.memset(u[:, 0:2], 0.0), reads=[f"u{bi}"], writes=[f"u{bi}"])
                    else:
                        S.op("act", lambda e, u=u, uprev=uprev: e.activation(out=u[:, 0:2], in_=uprev[:, 512:514], func=AF.Copy),
                             reads=[f"u{1 - bi}", f"u{bi}"], writes=[f"u{bi}"])
                    S.op("dve", lambda e, u=u, pC=pC, xvb=xvb: e.tensor_tensor(out=u[:, 2:514], in0=pC, in1=xvb, op=ALU.mult),
                         reads=[pCk, f"xv{bi}", f"u{bi}"], writes=[f"u{bi}"])
                    wof = _SM["scw"] + j * 3
                    S.op("act", lambda e, u=u, v=v, wof=wof: e.activation(
                        out=v, in_=u[:, 0:512], func=AF.Identity, scale=sm[:, wof:wof + 1]),
                        reads=[f"u{bi}", "sm", f"v{bi}"], writes=[f"v{bi}"])
                    for kk in (1, 2):
                        S.op("dve", lambda e, u=u, v=v, wof=wof, kk=kk: e.scalar_tensor_tensor(
                            out=v, in0=u[:, kk:kk + 512], scalar=sm[:, wof + kk:wof + kk + 1], in1=v,
                            op0=ALU.mult, op1=ALU.add), reads=[f"u{bi}", "sm", f"v{bi}"], writes=[f"v{bi}"])
                    S.op("dve", lambda e, v=v, pB=pB, j=j, tsl=tsl: e.tensor_tensor(out=yT[:, j, tsl], in0=pB, in1=v, op=ALU.mult),
                         reads=[pBk, f"v{bi}"], writes=[f"yT{j}_{tt}"])
            for (dc, tt) in [(dc, tt) for tt in range(4) for dc in range(8)]:
                wt, wkey = wo[dc // 4]
                col = (dc % 4) * 128
                if True:
                    tsl = slice(tt * 512, (tt + 1) * 512)
                    pp, pkey = prot.next()
                    for j in range(8):
                        S.op("pe", lambda e, pp=pp, wt=wt, j=j, col=col, tsl=tsl: e.matmul(
                            pp, lhsT=wt[:, j, col:col + 128], rhs=yT[:, j, tsl], start=(j == 0), stop=(j == 7)),
                            reads=[wkey, f"yT{j}_{tt}"], writes=[pkey])
                    evac_x(pp, pkey, gate(dc), "modT1", dc, [2 * tt, 2 * tt + 1], tsl)

        def dump_x():
            for kc in range(KC):
                S.dma("sp", out_d[:, kc, :], xT[:, kc, :], "out", reads=[xk(kc, t) for t in range(8)])

        def dump_h():
            cF.reset()
            tmp = cF.get(2048)
            for kc in range(KC):
                S.op("dve", lambda e, kc=kc: e.tensor_copy(out=tmp, in_=hT[:, kc, :]),
                     reads=[hk(kc, tt) for tt in range(4)] + ["dumptmp"], writes=["dumptmp"])
                S.dma("sp", out_d[:, kc, :], tmp, "out", reads=["dumptmp"])

        def final_norm():
            cF.reset()
            cF.get(1536)
            ob = [cF.get(512) for _ in range(3)]
            ob += [arF[:, 4608 + i * 512:4608 + (i + 1) * 512] for i in range(2)]
            NOB = len(ob)
            n = 0
            for tt in range(4):
                norm_stats(tt, pbank[tt % 2][:], f"pb{tt % 2}", float(D))
            for tt in range(4):
                tsl = slice(tt * 512, (tt + 1) * 512)
                for kc in range(KC):
                    oi = n % NOB
                    n += 1
                    o = ob[oi]
                    gcol = _SM["fing"] + kc
                    S.op("dve", lambda e, o=o, kc=kc, gcol=gcol, tsl=tsl, tt=tt: e.scalar_tensor_tensor(
                        out=o, in0=xT[:, kc, tsl], scalar=sm[:, gcol:gcol + 1], in1=nrstd[tt][:], op0=ALU.mult, op1=ALU.mult),
                        reads=xkeys512(kc, tt) + [f"nrstd{tt}", "sm", f"ob{oi}"], writes=[f"ob{oi}"])
                    S.dma("sp", out_d[:, kc, tsl], o, f"out{oi}", reads=[f"ob{oi}"])

        pmod = pbank[5]
        for tt in range(4):
            norm_stats(tt, pbank[tt % 2][:], f"pb{tt % 2}", float(D))
        for jj in range(12):
            ada_piece(0, jj * 4, 512, ada_st0, pmod)
        ada_finish(0, pmod)
        rmsnorm_mod(aM[0], "aM0", modT[0], "modT0", 0, stats=False)
        done = False
        if dbg_stop == "norm0":
            dump_h()
            done = True
        if not done:
            S.barrier()
            ssd()
            if dbg_stop == "mix0":
                dump_x()
                done = True
        if not done:
            S.barrier()
            rmsnorm_mod(aF[0], "aF0", modT[0], "modT0", 24)
            def ada1_gen():
                for jj in range(24):
                    yield (lambda jj=jj: ada_piece(1, jj * 2, 256, ada_st1, pmod))
            gen = ada1_gen()
            mlp(0, extra=gen)
            for rest in gen:
                rest()
            ada_finish(1, pmod)
            if dbg_stop == "mlp0":
                dump_x()
                done = True
        if not done:
            for j in range(3):
                load_sc(j)
            rmsnorm_mod(aM[1], "aM1", modT[1], "modT1", 0)
            S.barrier()
            shortconv()
            if dbg_stop == "mix1":
                dump_x()
                done = True
        if not done:
            rmsnorm_mod(aF[1], "aF1", modT[1], "modT1", 24)
            S.barrier()
            mlp(1)
            if dbg_stop == "mlp1":
                dump_x()
                done = True
        if not done:
            final_norm()
        S.emit(final_waits=[("sp", nm) for nm in S.dma_sems if nm.startswith("out")])
    return nc


def _chunks(v):
    v = np.asarray(v, np.float32)
    return np.ascontiguousarray(v.reshape(-1, 128).T)


def make_smalls(b, c, ada_b, mix_norm_w, mlp_norm_w, ssd_conv_w, ssd_conv_b, ssd_dt_bias, ssd_A_log,
                ssd_D, ssd_norm_w, sc_conv_w, final_norm_w):
    sm = np.zeros((128, NS), np.float32)
    sm[:, _SM["c"]:_SM["c"] + 8] = _chunks(c[b])
    for i in range(2):
        sm[:, _SM["adab"] + 48 * i:_SM["adab"] + 48 * (i + 1)] = _chunks(ada_b[i])
        sm[:, _SM["mixg"] + 8 * i:_SM["mixg"] + 8 * (i + 1)] = _chunks(mix_norm_w[i])
        sm[:, _SM["mlpg"] + 8 * i:_SM["mlpg"] + 8 * (i + 1)] = _chunks(mlp_norm_w[i])
    sm[:, _SM["fing"]:_SM["fing"] + 8] = _chunks(final_norm_w)
    cw = np.asarray(ssd_conv_w[0], np.float32)
    for k in range(4):
        sm[:, _SM["convw"] + k:_SM["convw"] + 96:4] = _chunks(cw[k])
    sm[:, _SM["convb"]:_SM["convb"] + 24] = _chunks(ssd_conv_b[0])
    sm[:, _SM["dfeat"]:_SM["dfeat"] + 16] = _chunks(np.repeat(np.asarray(ssd_D[0], np.float32), 64))
    sm[:, _SM["nw"]:_SM["nw"] + 16] = _chunks(ssd_norm_w[0])
    sm[:, _SM["dtb"]:_SM["dtb"] + 32] = np.broadcast_to(np.asarray(ssd_dt_bias[0], np.float32)[None, :], (128, 32))
    sm[:, _SM["alog"]:_SM["alog"] + 32] = np.broadcast_to(np.asarray(ssd_A_log[0], np.float32)[None, :], (128, 32))
    sw = np.asarray(sc_conv_w[0], np.float32)
    for k in range(3):
        sm[:, _SM["scw"] + k:_SM["scw"] + 24:3] = _chunks(sw[k])
    return sm


_NC_CACHE = {}


def _get_nc(dbg_stop=None):
    if dbg_stop not in _NC_CACHE:
        _NC_CACHE[dbg_stop] = build_program(dbg_stop)
    return _NC_CACHE[dbg_stop]


def kernel(x, c, ada_w, ada_b, mix_norm_w, mlp_norm_w, mlp_up, mlp_down,
           ssd_in_w, ssd_conv_w, ssd_conv_b, ssd_dt_bias, ssd_A_log, ssd_D,
           ssd_norm_w, ssd_out_w, sc_in_w, sc_conv_w, sc_out_w, final_norm_w, _dbg_stop=None):
    n = 8
    x = np.asarray(x, np.float32)
    f = lambda a: np.ascontiguousarray(np.asarray(a, np.float32))
    shared = {
        "ada_w": f(ada_w), "mlp_up": f(mlp_up), "mlp_down": f(mlp_down),
        "ssd_in_w": f(ssd_in_w[0]), "ssd_out_w": f(ssd_out_w[0]),
        "sc_in_w": f(sc_in_w[0]), "sc_out_w": f(sc_out_w[0]),
    }
    in_maps = []
    for b in range(n):
        xT = np.ascontiguousarray(x[b].reshape(L, KC, 128).transpose(2, 1, 0))
        smalls = make_smalls(b, c, ada_b, mix_norm_w, mlp_norm_w, ssd_conv_w, ssd_conv_b, ssd_dt_bias,
                             ssd_A_log, ssd_D, ssd_norm_w, sc_conv_w, final_norm_w)
        m = {"xT": xT, "smalls": smalls}
        m.update(shared)
        in_maps.append(m)
    nc = _get_nc(_dbg_stop)
    res = run_bass_kernel_spmd(nc, in_maps, core_ids=list(range(n)))
    outs = []
    for b in range(n):
        oT = np.asarray(res.results[b]["outT"], np.float32)
        outs.append(oT.transpose(2, 1, 0).reshape(L, D))
    return np.stack(outs, axis=0).astype(np.float32)
```
